# Optimizing a Trainium2 kernel written in Bass

```python
import math
import jax, jax.numpy as jnp
from jax import lax
import numpy as np

D_MODEL = 1024
BATCH = 8
SEQ = 2048
DEPTH = 1

D_MIX = D_MODEL
D_S5 = D_MIX // 2
D_RWKV = D_MIX - D_S5
S5_GROUP = 16
S5_GROUPS = D_S5 // S5_GROUP
S5_STATE = 64
RWKV_HEAD = 64
RWKV_HEADS = D_RWKV // RWKV_HEAD
D_DECAY_LORA = 64
D_AAA_LORA = 64
D_GATE_LORA = 160
D_RWKV_IN = 3 * D_RWKV + D_DECAY_LORA + D_AAA_LORA + D_GATE_LORA
D_IN = D_S5 + D_RWKV_IN
D_FF = 4 * D_MODEL
NORM_EPS = 1e-6
GN_EPS = 64e-5

kernel_name = "hymba_s5_rwkv7_hybrid_block"

F32 = jnp.float32


def rms_norm(x, g):
    xf = x.astype(F32)
    return xf * lax.rsqrt(jnp.mean(xf * xf, axis=-1, keepdims=True) + NORM_EPS) * g.astype(F32)


def s5_mixer(u, lam_re, lam_im, log_dt, b_re, b_im, c_re, c_im, d_skip, w_glu, b_glu):
    bsz, seqlen, _ = u.shape
    ug = u.reshape(bsz, seqlen, S5_GROUPS, S5_GROUP)
    lam = lax.complex(lam_re.astype(F32), lam_im.astype(F32))
    dt = jnp.exp(log_dt.astype(F32))[:, None]
    lam_bar = jnp.exp(lam * dt)
    b_bar = ((lam_bar - 1.0) / lam)[:, :, None] * lax.complex(b_re.astype(F32), b_im.astype(F32))
    bu = jnp.einsum('blgh,gph->blgp', ug.astype(jnp.complex64), b_bar)
    a_elems = jnp.broadcast_to(lam_bar, (1, seqlen) + lam_bar.shape)

    def combine(left, right):
        a1, b1 = left
        a2, b2 = right
        return a2 * a1, a2 * b1 + b2

    _, states = lax.associative_scan(combine, (a_elems, bu), axis=1)
    y = (jnp.einsum('ghp,blgp->blgh', c_re.astype(F32), states.real)
         - jnp.einsum('ghp,blgp->blgh', c_im.astype(F32), states.imag))
    y = y.reshape(bsz, seqlen, D_S5) + d_skip.astype(F32) * u
    g = jax.nn.gelu(y, approximate=False)
    return g * jax.nn.sigmoid(g @ w_glu.astype(F32) + b_glu.astype(F32))


def token_shift(z, mu):
    z_prev = jnp.pad(z, ((0, 0), (1, 0), (0, 0)))[:, :-1]
    return z + (z_prev - z) * mu.astype(F32)


def rwkv7_mixer(z, mu_shift, w0, w2, a0, a2, g2, k_k, k_a, r_k, ln_x_w, ln_x_b):
    bsz, seqlen, _ = z.shape
    z = token_shift(z, mu_shift)
    r, k, v, zw, za, zg = jnp.split(
        z, [D_RWKV, 2 * D_RWKV, 3 * D_RWKV, 3 * D_RWKV + D_DECAY_LORA,
            3 * D_RWKV + D_DECAY_LORA + D_AAA_LORA], axis=-1)
    w = -jax.nn.softplus(-(w0.astype(F32) + jnp.tanh(zw) @ w2.astype(F32))) - 0.5
    decay = jnp.exp(-jnp.exp(w))
    a = jax.nn.sigmoid(a0.astype(F32) + za @ a2.astype(F32))
    g = jax.nn.sigmoid(zg) @ g2.astype(F32)

    def heads(t):
        return t.reshape(bsz, seqlen, RWKV_HEADS, RWKV_HEAD)

    kk = heads(k * k_k.astype(F32))
    kk = kk / jnp.maximum(jnp.sqrt(jnp.sum(kk * kk, axis=-1, keepdims=True)), 1e-12)
    k = k * (1.0 + (a - 1.0) * k_a.astype(F32))
    r_h, k_h, v_h, a_h, d_h = heads(r), heads(k), heads(v), heads(a), heads(decay)

    def step(state, inp):
        r_t, d_t, k_t, v_t, kk_t, a_t = inp
        sa = jnp.einsum('bhvk,bhk->bhv', state, -kk_t)
        state = (state * d_t[:, :, None, :]
                 + sa[..., None] * (kk_t * a_t)[:, :, None, :]
                 + v_t[..., None] * k_t[:, :, None, :])
        return state, jnp.einsum('bhvk,bhk->bhv', state, r_t)

    xs = tuple(jnp.moveaxis(t, 1, 0) for t in (r_h, d_h, k_h, v_h, kk, a_h))
    s0 = jnp.zeros((bsz, RWKV_HEADS, RWKV_HEAD, RWKV_HEAD), F32)
    _, ys = lax.scan(step, s0, xs)
    y = jnp.moveaxis(ys, 0, 1)
    mean = jnp.mean(y, axis=-1, keepdims=True)
    var = jnp.mean(jnp.square(y - mean), axis=-1, keepdims=True)
    y = ((y - mean) * lax.rsqrt(var + GN_EPS)).reshape(bsz, seqlen, D_RWKV)
    y = y * ln_x_w.astype(F32) + ln_x_b.astype(F32)
    bonus = jnp.sum(r_h * k_h * r_k.astype(F32), axis=-1, keepdims=True) * v_h
    return (y + bonus.reshape(bsz, seqlen, D_RWKV)) * g


def setup_inputs(seed: int = 0) -> dict:
    key = jax.random.key(seed)
    ks = jax.random.split(key, 32)
    L = DEPTH

    def nrm(k, shape, scale):
        return jax.random.normal(k, shape, F32) * scale

    def gain(k, shape):
        return 1.0 + 0.02 * jax.random.normal(k, shape, F32)

    n = jnp.arange(D_RWKV, dtype=F32)
    decay_speed = -7.0 + 5.0 * (n / (D_RWKV - 1)) ** 0.85
    return {
        "x": jax.random.normal(ks[0], (BATCH, SEQ, D_MODEL), F32),
        "norm_mix_g": gain(ks[1], (L, D_MODEL)),
        "w_in": nrm(ks[2], (L, D_MODEL, D_IN), D_MODEL ** -0.5),
        "lam_re": -0.5 * jnp.exp(0.02 * jax.random.normal(ks[3], (L, S5_GROUPS, S5_STATE), F32)),
        "lam_im": math.pi * jnp.arange(S5_STATE, dtype=F32) + 0.02 * jax.random.normal(ks[4], (L, S5_GROUPS, S5_STATE), F32),
        "log_dt": jax.random.uniform(ks[5], (L, S5_GROUPS), F32, math.log(1e-3), math.log(1e-1)),
        "b_re": nrm(ks[6], (L, S5_GROUPS, S5_STATE, S5_GROUP), (2 * S5_GROUP) ** -0.5),
        "b_im": nrm(ks[7], (L, S5_GROUPS, S5_STATE, S5_GROUP), (2 * S5_GROUP) ** -0.5),
        "c_re": nrm(ks[8], (L, S5_GROUPS, S5_GROUP, S5_STATE), S5_STATE ** -0.5),
        "c_im": nrm(ks[9], (L, S5_GROUPS, S5_GROUP, S5_STATE), S5_STATE ** -0.5),
        "d_skip": nrm(ks[10], (L, D_S5), 1.0),
        "w_glu": nrm(ks[11], (L, D_S5, D_S5), D_S5 ** -0.5),
        "b_glu": nrm(ks[12], (L, D_S5), 0.01),
        "s5_out_g": gain(ks[13], (L, D_S5)),
        "mu_shift": jax.random.uniform(ks[14], (L, D_RWKV_IN), F32),
        "w0": decay_speed + 0.5 + 0.01 * jax.random.normal(ks[15], (L, D_RWKV), F32),
        "w2": nrm(ks[16], (L, D_DECAY_LORA, D_RWKV), 0.1 * D_DECAY_LORA ** -0.5),
        "a0": nrm(ks[17], (L, D_RWKV), 0.01),
        "a2": nrm(ks[18], (L, D_AAA_LORA, D_RWKV), 0.1 * D_AAA_LORA ** -0.5),
        "g2": nrm(ks[19], (L, D_GATE_LORA, D_RWKV), D_GATE_LORA ** -0.5),
        "k_k": 0.85 + 0.02 * jax.random.normal(ks[20], (L, D_RWKV), F32),
        "k_a": gain(ks[21], (L, D_RWKV)),
        "r_k": nrm(ks[22], (L, RWKV_HEADS, RWKV_HEAD), 0.1),
        "ln_x_w": gain(ks[23], (L, D_RWKV)),
        "ln_x_b": nrm(ks[24], (L, D_RWKV), 0.01),
        "w_out": nrm(ks[25], (L, D_MIX, D_MODEL), D_MIX ** -0.5),
        "norm_ffn_g": gain(ks[26], (L, D_MODEL)),
        "w_up": nrm(ks[27], (L, D_MODEL, D_FF), D_MODEL ** -0.5),
        "w_down": nrm(ks[28], (L, D_FF, D_MODEL), D_FF ** -0.5),
        "norm_final_g": gain(ks[29], (D_MODEL,)),
    }


def reference(x, norm_mix_g, w_in, lam_re, lam_im, log_dt, b_re, b_im, c_re, c_im,
              d_skip, w_glu, b_glu, s5_out_g, mu_shift, w0, w2, a0, a2, g2, k_k, k_a,
              r_k, ln_x_w, ln_x_b, w_out, norm_ffn_g, w_up, w_down, norm_final_g):
    h = x.astype(F32)
    for i in range(DEPTH):
        xn = rms_norm(h, norm_mix_g[i])
        z = xn @ w_in[i].astype(F32)
        y_s5 = s5_mixer(z[..., :D_S5], lam_re[i], lam_im[i], log_dt[i], b_re[i], b_im[i],
                        c_re[i], c_im[i], d_skip[i], w_glu[i], b_glu[i])
        y_s5 = rms_norm(y_s5, s5_out_g[i])
        y_rwkv = rwkv7_mixer(z[..., D_S5:], mu_shift[i], w0[i], w2[i], a0[i], a2[i], g2[i],
                             k_k[i], k_a[i], r_k[i], ln_x_w[i], ln_x_b[i])
        h = h + jnp.concatenate([y_s5, y_rwkv], axis=-1) @ w_out[i].astype(F32)
        hn = rms_norm(h, norm_ffn_g[i])
        h = h + jnp.square(jax.nn.relu(hn @ w_up[i].astype(F32))) @ w_down[i].astype(F32)
    return rms_norm(h, norm_final_g).astype(x.dtype)
```

```python
import contextlib
import math
import numpy as np
import concourse.bass as bass
import concourse.mybir as mybir
from concourse.bass_utils import run_bass_kernel_spmd

F32 = mybir.dt.float32
BF16 = mybir.dt.bfloat16
I32 = mybir.dt.int32
AF = mybir.ActivationFunctionType
ALU = mybir.AluOpType
AX = mybir.AxisListType

T = 2048
D = 1024
DIN = 2336
DFF = 4096
NORM_EPS = 1e-6
GN_EPS = 64e-5
C0 = math.exp(-0.5)
TWO_PI = 2.0 * math.pi
MAGIC = 12582912.0

ENGS = ("pe", "act", "dve", "pool", "sp")
N_DMA_SEMS = 24
DMA_POOLS = {"sp": (0, 14), "pool": (14, 8), "act": (22, 2)}
STRICT_SAME_ENGINE = True
ESZ = {F32: 4, BF16: 2, I32: 4}


class Op:
    __slots__ = ("eng", "fn", "deps", "is_dma", "dma_sem", "dma_val", "signaled", "sig_idx", "all_deps", "cost", "lat", "prio", "idx", "fin", "nsucc", "succ", "phase")

    def __init__(self, eng, fn, is_dma=False):
        self.eng = eng
        self.fn = fn
        self.deps = []
        self.is_dma = is_dma
        self.dma_sem = None
        self.dma_val = None
        self.signaled = False
        self.sig_idx = None
        self.all_deps = []
        self.cost = 0.2
        self.lat = 0.0
        self.prio = 0.0
        self.idx = 0
        self.fin = 0.0
        self.succ = []


class KB:
    def __init__(self, nc):
        self.nc = nc
        self.prog = {e: [] for e in ENGS}
        self.tinfo = {}
        self.recs = {}
        self.dma_rr = 0
        self.dma_rrs = {}
        self.dma_counts = [0] * N_DMA_SEMS
        self.final_dmas = []
        self.sb_off = 16640
        self.off_of = {}
        self.n_ops = 0
        self.all_ops = []
        self.bank_last = {}
        self.phase = 'A'

    def sb(self, name, shape, dtype=F32, at=None):
        fe = 1
        for s in shape[1:]:
            fe *= s
        nbytes = fe * ESZ[dtype]
        if at is None:
            at = self.sb_off
            self.sb_off += (nbytes + 63) // 64 * 64
        t = self.nc.alloc_sbuf_tensor_at(name, list(shape), dtype, offset=at)
        self.tinfo[t.name] = ("SB", at, fe, ESZ[dtype])
        self.off_of[name] = at
        return t

    def ps(self, name, shape, dtype=F32):
        fe = 1
        for s in shape[1:]:
            fe *= s
        t = self.nc.alloc_psum_tensor(name, list(shape), dtype)
        self.tinfo[t.name] = ("PS_" + name, 0, fe, ESZ[dtype])
        return t

    def rect(self, ap):
        info = self.tinfo.get(ap.tensor.name)
        if info is None:
            return None
        space, base, fe, es = info
        off = ap.offset
        pat = ap.ap
        p0 = off // fe
        f0 = off % fe
        pc = pat[0][1]
        ext = 1
        for s, c in pat[1:]:
            ext += (c - 1) * abs(s)
        return (space, p0, p0 + pc, base + f0 * es, base + (f0 + ext) * es)

    def op(self, eng, fn, outs=(), ins=(), is_dma=False, cost=None, extra_deps=()):
        o = Op(eng, fn, is_dma)
        deps = [d_ for d_ in extra_deps if d_ is not None]
        for ap in ins:
            if ap is None or isinstance(ap, (int, float)):
                continue
            r = self.rect(ap)
            if r is None:
                continue
            for rec in self.recs.setdefault(r[0], []):
                if rec[0] < r[2] and r[1] < rec[1] and rec[2] < r[4] and r[3] < rec[3]:
                    if rec[4] is not None:
                        deps.append(rec[4])
                    rec[5].append(o)
        for ap in outs:
            r = self.rect(ap)
            if r is None:
                continue
            lst = self.recs.setdefault(r[0], [])
            keep = []
            for rec in lst:
                if rec[0] < r[2] and r[1] < rec[1] and rec[2] < r[4] and r[3] < rec[3]:
                    if rec[4] is not None:
                        deps.append(rec[4])
                    deps.extend(rec[5])
                    if r[1] <= rec[0] and rec[1] <= r[2] and r[3] <= rec[2] and rec[3] <= r[4]:
                        continue
                keep.append(rec)
            keep.append([r[1], r[2], r[3], r[4], o, []])
            self.recs[r[0]] = keep
        seen = set()
        for d in deps:
            if d is o or id(d) in seen:
                continue
            seen.add(id(d))
            o.all_deps.append(d)
            if d.eng == eng and not d.is_dma:
                if eng == "pe" or not STRICT_SAME_ENGINE:
                    continue
            o.deps.append(d)
        n_el = 1
        ref = None
        for ap in list(outs) + list(ins):
            if ap is None or isinstance(ap, (int, float)):
                continue
            ref = ap
            break
        if ref is not None:
            for d_ in ref.shape[1:]:
                n_el *= d_
        if cost is not None:
            o.cost = cost
        elif is_dma:
            o.cost = 0.06
            o.lat = 2.0 + n_el * ref.shape[0] * 4 / 150e3
        elif eng == "pe":
            o.cost = max(0.055, n_el / 1200.0) if cost is None else cost
        elif eng == "act":
            o.cost = 0.2 + n_el / 1200.0
        elif eng == "dve":
            o.cost = 0.1 + n_el / 960.0
        elif eng == "pool":
            o.cost = 0.12 + n_el / 300.0
        o.idx = self.n_ops
        o.phase = self.phase
        self.all_ops.append(o)
        self.prog[eng].append(o)
        self.n_ops += 1
        return o

    def schedule(self):
        import heapq
        ops = self.all_ops
        for o in ops:
            o.succ = []
        for o in ops:
            for d in o.all_deps:
                d.succ.append(o)
        for o in reversed(ops):
            m = 0.0
            for s_ in o.succ:
                if s_.prio > m:
                    m = s_.prio
            o.prio = o.cost + o.lat + m
        indeg = {id(o): len(o.all_deps) for o in ops}
        ready = {e: [] for e in ENGS}
        for o in ops:
            if indeg[id(o)] == 0:
                heapq.heappush(ready[o.eng], (-o.prio, o.idx, o))
        free_at = {e: 0.0 for e in ENGS}
        running = []
        relc = [0]
        XLAT = 0.08
        order = {e: [] for e in ENGS}
        now = 0.0
        done = 0
        N = len(ops)
        while done < N:
            progressed = False
            for e in ENGS:
                if free_at[e] <= now and ready[e]:
                    _, _, o = heapq.heappop(ready[e])
                    order[e].append(o)
                    free_at[e] = now + o.cost
                    o.fin = now + o.cost + o.lat
                    heapq.heappush(running, (o.fin, o.idx, o))
                    progressed = True
            if progressed:
                continue
            cands = [free_at[e] for e in ENGS if free_at[e] > now and ready[e]]
            if running:
                cands.append(running[0][0])
            assert cands, "scheduler stuck"
            now = min(cands)
            while running and running[0][0] <= now + 1e-12:
                _, _, o = heapq.heappop(running)
                if o.phase == "__rel__":
                    s_ = o.succ[0]
                    heapq.heappush(ready[s_.eng], (-s_.prio, s_.idx, s_))
                    continue
                done += 1
                for s_ in o.succ:
                    indeg[id(s_)] -= 1
                    if indeg[id(s_)] == 0:
                        if XLAT > 0 and s_.eng != o.eng:
                            r_ = Op("x", None)
                            r_.phase = "__rel__"
                            r_.succ = [s_]
                            relc[0] += 1
                            heapq.heappush(running, (now + XLAT, -relc[0], r_))
                        else:
                            heapq.heappush(ready[s_.eng], (-s_.prio, s_.idx, s_))
        self.prog = order
        self.est_makespan = now
        tot = {e: sum(o.cost for o in order[e]) for e in ENGS}
        ph = {}
        for o in ops:
            d_ = ph.setdefault(o.phase, {'t0': 1e9, 't1': 0.0, 'pe': 0.0, 'act': 0.0, 'dve': 0.0, 'pool': 0.0, 'sp': 0.0, 'n': 0})
            d_['t0'] = min(d_['t0'], o.fin - o.cost - o.lat); d_['t1'] = max(d_['t1'], o.fin); d_[o.eng] += o.cost; d_['n'] += 1
        for k_, d_ in ph.items():
            pass
        return now

    def dma(self, out, in_, eng="sp", final=False, slow=False):
        kw = {"allow_slow_non_contiguous": True} if slow else {}
        o = self.op(eng, lambda e: e.dma_start(out=out, in_=in_, **kw), outs=[out], ins=[in_], is_dma=True)
        if final:
            self.final_dmas.append(o)
        return o

    def act(self, out, in_, func, bias=0.0, scale=1.0, eng="act"):
        return self.op(eng, lambda e: e.activation(out=out, in_=in_, func=func, bias=bias, scale=scale),
                       outs=[out], ins=[in_, bias, scale])

    def tt(self, eng, out, a, b, op):
        return self.op(eng, lambda e: e.tensor_tensor(out=out, in0=a, in1=b, op=op), outs=[out], ins=[a, b])

    def ts(self, eng, out, a, s1, op0, s2=None, op1=None):
        if op1 is None:
            return self.op(eng, lambda e: e.tensor_scalar(out=out, in0=a, scalar1=s1, scalar2=None, op0=op0),
                           outs=[out], ins=[a, s1])
        return self.op(eng, lambda e: e.tensor_scalar(out=out, in0=a, scalar1=s1, scalar2=s2, op0=op0, op1=op1),
                       outs=[out], ins=[a, s1, s2])

    def stt(self, out, in0, scalar, in1, op0, op1, accum_out=None):
        if accum_out is None:
            return self.op("dve", lambda e: e.scalar_tensor_tensor(out=out, in0=in0, scalar=scalar, in1=in1, op0=op0, op1=op1),
                           outs=[out], ins=[in0, scalar, in1])
        return self.op("dve", lambda e: e.scalar_tensor_tensor(out=out, in0=in0, scalar=scalar, in1=in1, op0=op0, op1=op1, accum_out=accum_out),
                       outs=[out, accum_out], ins=[in0, scalar, in1])

    def copy(self, eng, out, in_):
        if eng == "act":
            return self.op(eng, lambda e: e.copy(out=out, in_=in_), outs=[out], ins=[in_])
        return self.op(eng, lambda e: e.tensor_copy(out=out, in_=in_), outs=[out], ins=[in_])

    def memset(self, eng, out, val):
        return self.op(eng, lambda e: e.memset(out, val), outs=[out])

    def mm(self, out, lhsT, rhs, start=True, stop=True, tile_position=None):
        if tile_position is None:
            bp = self.rect(lhsT)[1]
            if bp == 96:
                tile_position = (96, self.rect(out)[1])
        if tile_position is None:
            fn = lambda e: e.matmul(out, lhsT=lhsT, rhs=rhs, start=start, stop=stop)
        else:
            fn = lambda e: e.matmul(out, lhsT=lhsT, rhs=rhs, start=start, stop=stop, tile_position=tile_position)
        n_ = 1
        for d_ in rhs.shape[1:]:
            n_ *= d_
        c_ = max(0.055, n_ / (2300.0 if getattr(self, 'pe_warm', False) else 1200.0)) * (4.0 if rhs.dtype == F32 else 1.0)
        bkey = out.tensor.name
        xd = [self.bank_last.get(bkey)] if start else []
        o_ = self.op("pe", fn, outs=[out], ins=[lhsT, rhs] + ([] if start else [out]), cost=c_, extra_deps=xd)
        self.bank_last[bkey] = o_
        return o_

    def mmg(self, items):
        fns = []
        outs, ins, banks_start, banks_all = [], [], [], []
        cost = 0.0
        for it in items:
            out = it["out"]
            bkey = out.tensor.name
            if bkey not in banks_all:
                banks_all.append(bkey)
            if it.get("kind", "mm") == "tr":
                in_, ident = it["in_"], it["ident"]
                fns.append(lambda e, out=out, in_=in_, ident=ident: e.transpose(out=out, in_=in_, identity=ident))
                ins += [in_, ident]
                if bkey not in banks_start:
                    banks_start.append(bkey)
                n_ = in_.shape[-1]
                cost += max(0.03, n_ / 2400.0)
            else:
                lhsT, rhs = it["lhsT"], it["rhs"]
                start, stop = it.get("start", True), it.get("stop", True)
                tp = it.get("tile_position")
                if tp is None and self.rect(lhsT)[1] == 96:
                    tp = (96, self.rect(out)[1])
                if tp is None:
                    fns.append(lambda e, out=out, lhsT=lhsT, rhs=rhs, start=start, stop=stop: e.matmul(out, lhsT=lhsT, rhs=rhs, start=start, stop=stop))
                else:
                    fns.append(lambda e, out=out, lhsT=lhsT, rhs=rhs, start=start, stop=stop, tp=tp: e.matmul(out, lhsT=lhsT, rhs=rhs, start=start, stop=stop, tile_position=tp))
                ins += [lhsT, rhs]
                if start and bkey not in banks_start:
                    banks_start.append(bkey)
                n_ = 1
                for d_ in rhs.shape[1:]:
                    n_ *= d_
                cost += max(0.03, n_ / (2300.0 if getattr(self, 'pe_warm', False) else 1200.0) * (0.6 if n_ <= 128 else 1.0)) * (4.0 if rhs.dtype == F32 else 1.0)
            outs.append(out)

        def fn(e):
            r_ = None
            for f_ in fns:
                r_ = f_(e)
            return r_
        xd = [self.bank_last.get(b_) for b_ in banks_start]
        o_ = self.op("pe", fn, outs=outs, ins=ins, cost=cost, extra_deps=xd)
        for b_ in banks_all:
            self.bank_last[b_] = o_
        return o_

    def tr(self, out, in_, ident):
        bkey = out.tensor.name
        o_ = self.op("pe", lambda e: e.transpose(out=out, in_=in_, identity=ident), outs=[out], ins=[in_, ident],
                     extra_deps=[self.bank_last.get(bkey)])
        self.bank_last[bkey] = o_
        return o_

    def recip(self, out, in_):
        return self.op("dve", lambda e: e.reciprocal(out=out, in_=in_), outs=[out], ins=[in_])

    def scan(self, out, d0, d1, initial):
        return self.op("dve", lambda e: e.tensor_tensor_scan(out=out, data0=d0, data1=d1, initial=initial, op0=ALU.mult, op1=ALU.add),
                       outs=[out], ins=[d0, d1, initial])

    def reduce(self, out, in_, axis=None):
        return self.op("dve", lambda e: e.tensor_reduce(out=out, in_=in_, axis=AX.X, op=ALU.add), outs=[out], ins=[in_])

    def emit(self):
        nc = self.nc
        rr = {}
        cnt = [0] * N_DMA_SEMS
        for e in ENGS:
            for o in self.prog[e]:
                if o.is_dma:
                    lo, n = DMA_POOLS[e]
                    k = lo + rr.get(e, 0)
                    rr[e] = (rr.get(e, 0) + 1) % n
                    cnt[k] += 1
                    o.dma_sem = k
                    o.dma_val = 16 * cnt[k]
        for e in ENGS:
            for o in self.prog[e]:
                for d in o.deps:
                    if not d.is_dma:
                        d.signaled = True
        for e in ENGS:
            c = 0
            for o in self.prog[e]:
                if o.signaled and not o.is_dma:
                    c += 1
                    o.sig_idx = c
        with contextlib.ExitStack() as st:
            sems = {e: st.enter_context(nc.semaphore("s_" + e)) for e in ENGS}
            dsems = [st.enter_context(nc.semaphore("d%d" % i)) for i in range(N_DMA_SEMS)]
            block = st.enter_context(nc.Block())
            engobj = {"pe": block.tensor, "act": block.scalar, "dve": block.vector,
                      "pool": block.gpsimd, "sp": block.sync}

            def make(e):
                def body(eng):
                    waited = {}
                    for o in self.prog[e]:
                        need = {}
                        for d in o.deps:
                            if d.is_dma:
                                key = ("d", d.dma_sem)
                                val = d.dma_val
                            else:
                                key = ("c", d.eng)
                                val = d.sig_idx
                            if waited.get(key, 0) >= val:
                                continue
                            if need.get(key, 0) < val:
                                need[key] = val
                        for key, val in need.items():
                            sem = dsems[key[1]] if key[0] == "d" else sems[key[1]]
                            eng.wait_ge(sem, val)
                            waited[key] = val
                        if o.is_dma and o.dma_val > 16:
                            key = ("d", o.dma_sem)
                            if waited.get(key, 0) < o.dma_val - 16:
                                eng.wait_ge(dsems[o.dma_sem], o.dma_val - 16)
                                waited[key] = o.dma_val - 16
                        ins = o.fn(eng)
                        if o.is_dma:
                            ins.then_inc(dsems[o.dma_sem], 16)
                        elif o.signaled:
                            ins.then_inc(sems[e], 1)
                    if e == "sp":
                        need = {}
                        for d in self.final_dmas:
                            need[d.dma_sem] = max(need.get(d.dma_sem, 0), d.dma_val)
                        for k, v in need.items():
                            eng.wait_ge(dsems[k], v)
                return body

            for e in ENGS:
                engobj[e](make(e))


IN_SHAPES = {
    "x": [T, D], "norm_mix_g": [D], "w_in": [D, DIN], "lam_re": [32, 64], "lam_im": [32, 64],
    "log_dt": [1, 32], "b_re": [32, 64, 16], "b_im": [32, 64, 16], "c_re": [32, 16, 64], "c_im": [32, 16, 64],
    "d_skip": [512], "w_glu": [512, 512], "b_glu": [512], "s5_out_g": [512], "mu_shift": [1824],
    "w0": [512], "w2": [64, 512], "a0": [512], "a2": [64, 512], "g2": [160, 512], "k_k": [512], "k_a": [512],
    "r_k": [512], "ln_x_w": [512], "ln_x_b": [512], "w_out": [D, D], "norm_ffn_g": [D],
    "w_up": [D, DFF], "w_down": [DFF, D], "norm_final_g": [1, D],
}


def build(debug=(), skip=()):
    nc = bass.Bass("TRN2", target_bir_lowering=False)
    kb = KB(nc)
    I = {k: nc.dram_tensor(k, v, F32, kind="ExternalInput").ap() for k, v in IN_SHAPES.items()}
    out_d = nc.dram_tensor("out", [T, D], F32, kind="ExternalOutput").ap()
    dbg_out = {}

    def dbg(name, ap, shape):
        if name not in debug:
            return
        d = nc.dram_tensor("dbg_" + name, list(shape), ap.dtype, kind="ExternalOutput").ap()
        dbg_out[name] = d
        kb.dma(d, ap, final=True)

    ident_f = kb.sb("ident_f", [128, 128]); ident_b = kb.sb("ident_b", [128, 128], BF16)
    ones_b = kb.sb("ones_b", [128, 128], BF16); blk_b = kb.sb("blk_b", [128, 128], BF16)
    m_su = kb.sb("m_su", [128, 8, 64], BF16); m_iu = kb.sb("m_iu", [128, 8, 64], BF16)
    m_sl = kb.sb("m_sl", [128, 8, 64], BF16); m_id = kb.sb("m_id", [128, 8, 64], BF16)
    pcol = kb.sb("pcol", [128, 96])
    gF = kb.sb("gF", [128, D])
    stat = kb.sb("stat", [128, 64])
    constc = kb.sb("constc", [128, 8])
    kb.memset("pool", constc[:, 0:1], NORM_EPS)
    kb.memset("pool", constc[:, 1:2], 1.0)
    kb.memset("pool", constc[:, 2:3], GN_EPS)
    cE = constc[:, 0:1]; c1c = constc[:, 1:2]; cG = constc[:, 2:3]
    xnT = kb.sb("xnT", [128, 8, T], BF16)
    mixT = kb.sb("mixT", [128, 8, T], BF16)
    wct = [kb.sb("wct%d" % i, [128, 8, 128], BF16) for i in range(2)]
    xst = [kb.sb("xst%d" % i, [128, D]) for i in range(2)]
    AR = kb.sb_off
    AR_END = 223 * 1024
    assert AR < 108 * 1024, AR

    class Arena:
        def __init__(self, start):
            self.cur = start

        def t(self, name, shape, dt=F32):
            fe = 1
            for s_ in shape[1:]:
                fe *= s_
            t_ = kb.sb(name, shape, dt, at=self.cur)
            self.cur += (fe * ESZ[dt] + 63) // 64 * 64
            assert self.cur <= AR_END, (name, self.cur)
            return t_

    PB = [kb.ps("pb%d" % i, [128, 512]) for i in range(5)]
    PH = kb.ps("pH", [128, 512])
    PH2 = kb.ps("pH2", [128, 512])
    PHs = [PH, PH2]
    PT = kb.ps("pT", [128, 1024], BF16)
    pbi = [0]

    def bank():
        b = PB[pbi[0] % 5]
        pbi[0] += 1
        return b

    wsti = [0]

    def stage():
        b = wst[wsti[0] % 2]
        wsti[0] += 1
        return b

    arA = Arena(AR)
    junk = arA.t("junk", [128, D])
    xnb = [arA.t("xnb%d" % i, [128, D], BF16) for i in range(2)]
    xs_t = [arA.t("xs%d" % i, [128, D]) for i in range(3)]
    onesf = arA.t("onesf", [128, 512])
    tmpf = arA.t("tmpf", [128, 512])
    kb.memset("pool", onesf[:], 1.0)
    kb.op("pool", lambda e: e.affine_select(out=ident_f[:], in_=onesf[:, 0:128], pattern=[[-1, 128]], compare_op=ALU.is_equal, fill=0.0, base=0, channel_multiplier=1), outs=[ident_f[:]], ins=[onesf[:]])
    kb.copy("pool", ident_b[:], ident_f[:])
    kb.copy("pool", ones_b[:], onesf[:, 0:128])
    kb.memset("pool", blk_b[:], 0.0)
    kb.memset("pool", blk_b[0:64, 0:64], 1.0)
    kb.memset("pool", blk_b[64:128, 64:128], 1.0)
    o3 = onesf[:].rearrange("p (a b) -> p a b", a=8)
    t3 = tmpf[:].rearrange("p (a b) -> p a b", a=8)

    def mk_mask(dst, cm, base, cmp):
        for h in range(2):
            sl = slice(64 * h, 64 * h + 64)
            kb.op("pool", lambda e, sl=sl, h=h: e.affine_select(out=t3[sl], in_=o3[sl], pattern=[[0, 8], [-cm, 64]], compare_op=cmp, fill=0.0, base=base, channel_multiplier=cm),
                  outs=[t3[sl]], ins=[o3[sl]])
        kb.copy("pool", dst[:], t3)

    mk_mask(m_su, -1, -1, ALU.is_ge)
    mk_mask(m_iu, -1, 0, ALU.is_ge)
    mk_mask(m_sl, 1, -1, ALU.is_ge)
    mk_mask(m_id, 1, 0, ALU.is_equal)

    pc_i = [0]
    PC = {}

    def col(name, ap1d, n):
        c = n // 128
        c0 = pc_i[0]
        pc_i[0] += c
        kb.dma(pcol[:, c0:c0 + c], ap1d.rearrange("(c p) -> p c", p=128), slow=True)
        PC[name] = lambda j, c0=c0: pcol[:, c0 + j:c0 + j + 1]

    col("g_mix", I["norm_mix_g"], 1024)
    col("g_ffn", I["norm_ffn_g"], 1024)
    for nm in ("d_skip", "b_glu", "s5_out_g", "w0", "a0", "k_k", "k_a", "r_k", "ln_x_w", "ln_x_b"):
        col(nm, I[nm], 512)
    col("mu_rkv", I["mu_shift"][0:1536], 1536)
    col("mu_l", I["mu_shift"][1536:1792], 256)
    omu_c0 = pc_i[0]
    pc_i[0] += 12
    mu_c0 = omu_c0 - 2 - 12
    kb.ts("dve", pcol[:, omu_c0:omu_c0 + 12], pcol[:, mu_c0:mu_c0 + 12], -1.0, ALU.mult, 1.0, ALU.add)
    PC["omu_rkv"] = lambda j, c0=omu_c0: pcol[:, c0 + j:c0 + j + 1]
    mu_g2 = pcol[0:32, pc_i[0]:pc_i[0] + 1]
    kb.dma(mu_g2, I["mu_shift"][1792:1824].rearrange("(c p) -> p c", p=32), slow=True)
    pc_i[0] += 1
    assert pc_i[0] <= 96
    gmix_T = pcol[:, 0:8]
    gffn_T = pcol[:, 8:16]
    kb.dma(gF[:], I["norm_final_g"].partition_broadcast(128))

    class Collect:
        def __enter__(self):
            self.items = []
            self.om, self.ot = kb.mm, kb.tr
            kb.mm = lambda out, lhsT, rhs, start=True, stop=True, tile_position=None: self.items.append(
                dict(out=out, lhsT=lhsT, rhs=rhs, start=start, stop=stop, tile_position=tile_position))
            kb.tr = lambda out, in_, ident: self.items.append(dict(kind="tr", out=out, in_=in_, ident=ident))
            return self

        def __exit__(self, *a):
            kb.mm, kb.tr = self.om, self.ot
            if self.items:
                kb.mmg(self.items)
            return False


    def norm_T(src_fn, gT, dstT, ss_col0, junk_, xnb_):
        for tt in range(16):
            xs = src_fn(tt)
            ss = stat[:, ss_col0 + tt:ss_col0 + tt + 1]
            kb.stt(junk_[:], xs, 1.0, xs, ALU.mult, ALU.mult, accum_out=ss)
            kb.act(ss, ss, AF.Sqrt, bias=cE, scale=1.0 / D)
            kb.recip(ss, ss)
            xb = xnb_[tt % 2]
            kb.act(xb[:], xs, AF.Copy, scale=ss)
            with Collect():
                for c in range(8):
                    kb.tr(PT[:, c * 128:(c + 1) * 128], xb[:, c * 128:(c + 1) * 128], ident_b[:])
            kb.tt("dve", dstT[:, :, tt * 128:(tt + 1) * 128], PT[:].rearrange("p (c t) -> p c t", c=8),
                  gT.unsqueeze(2).to_broadcast([128, 8, 128]), ALU.mult)

    def x_src(tt):
        b = xs_t[tt % 3]
        kb.dma(b[:], I["x"][tt * 128:(tt + 1) * 128, :])
        return b[:]

    dbg('pcol', pcol[:], [128, 96]); dbg('gF', gF[:], [128, D]); dbg('msu', m_su[:].rearrange('p a b -> p (a b)'), [128, 512])
    if 'A' not in skip:
        norm_T(x_src, gmix_T, xnT, 0, junk, xnb)
    dbg('xnT', xnT[:, 0, :], [128, T])

    def load_w(dst_bf, src2d, kc, ncols, dst_f32=False):
        per = max(1, 4096 // ncols)
        k = 0
        while k < kc:
            n = min(per, kc - k)
            kb.dma(dst_bf[:, k:k + n, :], src2d[k * 128:(k + n) * 128, :].rearrange("(c p) n -> p c n", p=128), eng="pool")
            k += n

    wcti = [0]

    def proj(c0, M, evac, bank_fn=None):
        w = wct[wcti[0] % 2]
        wcti[0] += 1
        load_w(w[:, :, 0:M], I["w_in"][:, c0:c0 + M], 8, M)
        for n in range(4):
            ps = (bank_fn or bank)()
            with Collect():
                for c in range(8):
                    kb.mm(ps[0:M, :], w[:, c, 0:M], xnT[:, c, n * 512:(n + 1) * 512], start=(c == 0), stop=(c == 7))
            evac(n, ps[0:M, :])

    def s5_phase():
        ar = Arena(AR)
        CH = 8
        NCH = T // CH
        uT = ar.t("uT", [128, T])
        uTb = ar.t("uTb", [128, T], BF16)
        yacc = ar.t("yacc", [128, T])
        gb = ar.t("gb", [128, 4, T], BF16)
        sp_small = ar.t("sp_small", [128, 32, 16])
        Mre = ar.t("Mre", [128, 16, 32]); Mim = ar.t("Mim", [128, 16, 32]); Mimn = ar.t("Mimn", [128, 16, 32])
        BT0re = ar.t("BT0re", [128, 4, 128]); BT0im = ar.t("BT0im", [128, 4, 128])
        L1re = ar.t("L1re", [128, 4, 128]); L1im = ar.t("L1im", [128, 4, 128])
        CL0re = ar.t("CL0re", [128, 16, 32]); CL0im = ar.t("CL0im", [128, 16, 32])
        bm32 = ar.t("bm32", [128, 128])
        iota_c = ar.t("iota_c", [128, NCH])
        iota_ci = ar.t("iota_ci", [128, NCH], I32)
        wglu = ar.t("wglu", [128, 4, 512], BF16)
        tsets = []
        for i_ in range(2):
            tsets.append({"WBb": [ar.t("WBb_re%d" % i_, [128, CH, 128], BF16), ar.t("WBb_im%d" % i_, [128, CH, 128], BF16)],
                          "CLb": [ar.t("CLb_re%d" % i_, [128, CH + 1, 128], BF16), ar.t("CLb_in%d" % i_, [128, CH + 1, 128], BF16)],
                          "Kbd": ar.t("Kbd%d" % i_, [128, CH, 128], BF16)})
        wr = [ar.t("wr%d" % i, [128, 128]) for i in range(2)]
        wi = [ar.t("wi%d" % i, [128, 128]) for i in range(2)]
        cr = [ar.t("cr%d" % i, [128, 128]) for i in range(2)]
        ci = [ar.t("ci%d" % i, [128, 128]) for i in range(2)]
        tw = [ar.t("tw%d" % i, [128, 128]) for i in range(4)]
        tc_ = [ar.t("tc%d" % i, [128, 128]) for i in range(4)]
        Xs = [[ar.t("Xs%d_%d" % (q_, r_), [128, NCH + 1], BF16) for r_ in range(2)] for q_ in range(4)]
        ov_ = ar.cur
        csets = []
        for i_ in range(4):
            csets.append({nm: ar.t("%s_c%d" % (nm, i_), [128, NCH]) for nm in ("tcos", "tsin", "cre", "cim", "tm1", "tm2")})
        arg = Arena(ov_)
        ysb = arg.t("ysb", [128, 4, 512])
        sigt = arg.t("sigt", [128, 512])
        sqb = arg.t("sqb", [128, 512], BF16)
        rstd = arg.t("rstd", [128, 512])
        aro = Arena(ov_)
        bre = aro.t("bre", [128, 16, 16]); bim = aro.t("bim", [128, 16, 16])
        bbr = aro.t("bbr", [128, 16, 16]); bbi = aro.t("bbi", [128, 16, 16]); btmp = aro.t("btmp", [128, 16, 16])
        MLr = aro.t("MLr", [128, 16, 32]); MLi = aro.t("MLi", [128, 16, 32])
        cdr = aro.t("cdr", [128, 4, 2, 64]); cdi = aro.t("cdi", [128, 4, 2, 64])

        def sm(i):
            return sp_small[:, i, :]

        (lre, lim, ldt, dt_, th, rl, rho, f0, rn, fr, s1, sh, c1, lbr, lbi, den, inv, fre, fim, x1, x2,
         rho16, f16) = [sm(i) for i in range(23)]
        for two in range(2):
            sl = slice(64 * two, 64 * two + 64)
            kb.dma(lre[sl], I["lam_re"].rearrange("(k two) p -> two p k", two=2)[two], slow=True)
            kb.dma(lim[sl], I["lam_im"].rearrange("(k two) p -> two p k", two=2)[two], slow=True)
            kb.dma(ldt[sl], I["log_dt"].rearrange("o (k two) -> o two k", two=2)[:, two, :].partition_broadcast(64), slow=True)
            kb.dma(bre[sl], I["b_re"].rearrange("(k two) p h -> two p k h", two=2)[two])
            kb.dma(bim[sl], I["b_im"].rearrange("(k two) p h -> two p k h", two=2)[two])
        kb.act(dt_, ldt, AF.Exp)
        kb.tt("dve", th, lim, dt_, ALU.mult)
        kb.tt("dve", rl, lre, dt_, ALU.mult)
        kb.act(rho, rl, AF.Exp)
        kb.act(rho16, rl, AF.Exp, scale=float(CH))
        kb.ts("dve", f0, th, 1.0 / TWO_PI, ALU.mult)
        kb.ts("dve", f16, f0, float(CH), ALU.mult)
        kb.ts("dve", rn, f0, MAGIC, ALU.add, -MAGIC, ALU.add)
        kb.tt("dve", fr, f0, rn, ALU.subtract)
        kb.act(s1, fr, AF.Sin, scale=TWO_PI * 0.9999999)
        kb.act(sh, fr, AF.Sin, scale=math.pi * 0.9999999)
        kb.act(c1, sh, AF.Square, scale=math.sqrt(2.0))
        kb.ts("dve", c1, c1, -1.0, ALU.mult, 1.0, ALU.add)
        kb.tt("dve", lbr, rho, c1, ALU.mult)
        kb.tt("dve", lbi, rho, s1, ALU.mult)
        kb.ts("dve", x1, lbr, -1.0, ALU.add)
        kb.tt("dve", den, lre, lre, ALU.mult)
        kb.tt("dve", x2, lim, lim, ALU.mult)
        kb.tt("dve", den, den, x2, ALU.add)
        kb.recip(inv, den)
        kb.tt("dve", fre, x1, lre, ALU.mult)
        kb.tt("dve", x2, lbi, lim, ALU.mult)
        kb.tt("dve", fre, fre, x2, ALU.add)
        kb.tt("dve", fre, fre, inv, ALU.mult)
        kb.tt("dve", fim, lbi, lre, ALU.mult)
        kb.tt("dve", x2, x1, lim, ALU.mult)
        kb.tt("dve", fim, fim, x2, ALU.subtract)
        kb.tt("dve", fim, fim, inv, ALU.mult)
        freb = fre.unsqueeze(2).to_broadcast([128, 16, 16])
        fimb = fim.unsqueeze(2).to_broadcast([128, 16, 16])
        kb.tt("dve", bbr[:], bre[:], freb, ALU.mult)
        kb.tt("dve", btmp[:], bim[:], fimb, ALU.mult)
        kb.tt("dve", bbr[:], bbr[:], btmp[:], ALU.subtract)
        kb.tt("dve", bbi[:], bim[:], freb, ALU.mult)
        kb.tt("dve", btmp[:], bre[:], fimb, ALU.mult)
        kb.tt("dve", bbi[:], bbi[:], btmp[:], ALU.add)
        lbrb = lbr.unsqueeze(2).to_broadcast([128, 16, 16])
        lbib = lbi.unsqueeze(2).to_broadcast([128, 16, 16])
        for (M_, src_) in ((Mre, bbr[:]), (Mim, bbi[:]), (MLr, lbrb), (MLi, lbib)):
            kb.memset("pool", M_[:], 0.0)
            kb.copy("pool", M_[0:64, :, 0:16], src_[0:64])
            kb.copy("pool", M_[64:128, :, 16:32], src_[64:128])
        kb.ts("pool", Mimn[:], Mim[:], -1.0, ALU.mult)
        for (M_, BT) in ((Mre, BT0re), (Mim, BT0im), (MLr, L1re), (MLi, L1im)):
            for j in range(4):
                ps = bank()
                kb.tr(ps[:, 0:128], M_[:, 4 * j:4 * j + 4, :].rearrange("p a b -> p (a b)"), ident_f[:])
                kb.copy("act", BT[:, j, :], ps[:, 0:128])
        for (cd, nm) in ((cdr, "c_re"), (cdi, "c_im")):
            src = I[nm].rearrange("(j g) h p -> (g h) j p", g=8)
            kb.dma(cd[:, :, 0, :], src)
            kb.dma(cd[:, :, 1, :], src)
        for (cd, CL0) in ((cdr, CL0re), (cdi, CL0im)):
            kb.memset("pool", CL0[:], 0.0)
            for j in range(4):
                ps = bank()
                kb.tr(ps[:, 0:128], cd[:, j].rearrange("p a b -> p (a b)"), ident_f[:])
                pv = ps[:, 0:128].rearrange("p (q t h) -> p q t h", q=4, t=2)
                kb.copy("act", CL0[0:64, 4 * j:4 * j + 4, 0:16], pv[0:64, :, 0, :])
                kb.copy("act", CL0[64:128, 4 * j:4 * j + 4, 16:32], pv[64:128, :, 1, :])
        load_w(wglu, I["w_glu"], 4, 512)
        kb.op("pool", lambda e: e.iota(iota_ci[:], pattern=[[1, NCH]], base=0, channel_multiplier=0), outs=[iota_ci[:]])
        kb.copy("pool", iota_c[:], iota_ci[:])
        kb.memset("pool", tw[0][:], 1.0)
        kb.op("pool", lambda e: e.affine_select(out=tw[1][:].rearrange("p (a b) -> p a b", a=4), in_=tw[0][:].rearrange("p (a b) -> p a b", a=4), pattern=[[-32, 4], [0, 32]], compare_op=ALU.is_ge, fill=0.0, base=0, channel_multiplier=1),
              outs=[tw[1][:]], ins=[tw[0][:]])
        kb.op("pool", lambda e: e.affine_select(out=bm32[:].rearrange("p (a b) -> p a b", a=4), in_=tw[1][:].rearrange("p (a b) -> p a b", a=4), pattern=[[32, 4], [0, 32]], compare_op=ALU.is_ge, fill=0.0, base=31, channel_multiplier=-1),
              outs=[bm32[:]], ins=[tw[1][:]])
        for q_ in range(4):
            for r_ in range(2):
                kb.memset("pool", Xs[q_][r_][:, 0:1], 0.0)

        def cmul_step(a_r, a_i, b_r, b_i, o_r, o_i, t4):
            kb.tt("dve", t4[0][:], a_r, b_r, ALU.mult)
            kb.tt("pool", t4[1][:], a_i, b_i, ALU.mult)
            kb.tt("dve", t4[2][:], a_r, b_i, ALU.mult)
            kb.tt("pool", t4[3][:], a_i, b_r, ALU.mult)
            kb.tt("dve", o_r, t4[0][:], t4[1][:], ALU.subtract)
            kb.tt("pool", o_i, t4[2][:], t4[3][:], ALU.add)

        b7 = [PB[4], PH, PH2]
        b7i = [0]

        def bank7():
            b7i[0] += 1
            return b7[b7i[0] % 3]

        def pre_gen(j, S):
            WBb, CLb, Kbd = S["WBb"], S["CLb"], S["Kbd"]
            Mre_j = Mre[:, 4 * j:4 * j + 4, :].rearrange("p a b -> p (a b)")
            Mimn_j = Mimn[:, 4 * j:4 * j + 4, :].rearrange("p a b -> p (a b)")
            lamr_b = lbr[:, 4 * j:4 * j + 4].unsqueeze(2).to_broadcast([128, 4, 32])
            lami_b = lbi[:, 4 * j:4 * j + 4].unsqueeze(2).to_broadcast([128, 4, 32])
            kb.copy("pool", wr[0][:], BT0re[:, j, :])
            kb.copy("pool", wi[0][:], BT0im[:, j, :])
            kb.copy("pool", cr[0][:], CL0re[:, 4 * j:4 * j + 4, :].rearrange("p a b -> p (a b)"))
            kb.copy("pool", ci[0][:], CL0im[:, 4 * j:4 * j + 4, :].rearrange("p a b -> p (a b)"))
            c3 = lambda t_: t_[:].rearrange("p (a b) -> p a b", a=4)
            for n in range(CH + 1):
                a, b = n % 2, (n + 1) % 2
                if n < CH:
                    kb.copy("act", WBb[0][:, n, :], wr[a][:])
                    kb.copy("act", WBb[1][:, n, :], wi[a][:])
                kb.copy("act", CLb[0][:, n, :], cr[a][:])
                kb.act(CLb[1][:, n, :], ci[a][:], AF.Copy, scale=-1.0)
                if n < CH:
                    ps = bank7()
                    with Collect():
                        kb.mm(ps[:, 0:128], Mre_j, cr[a][:], start=True, stop=False)
                        kb.mm(ps[:, 0:128], Mimn_j, ci[a][:], start=False, stop=True)
                    kb.tt("dve", Kbd[:, n, :], ps[:, 0:128], bm32[:], ALU.mult)
                    if n < CH - 1:
                        cmul_step(wr[a][:], wi[a][:], L1re[:, j, :], L1im[:, j, :], wr[b][:], wi[b][:], tw)
                    kb.tt("dve", c3(tc_[0]), c3(cr[a]), lamr_b, ALU.mult)
                    kb.tt("pool", c3(tc_[1]), c3(ci[a]), lami_b, ALU.mult)
                    kb.tt("dve", c3(tc_[2]), c3(cr[a]), lami_b, ALU.mult)
                    kb.tt("pool", c3(tc_[3]), c3(ci[a]), lamr_b, ALU.mult)
                    kb.tt("dve", cr[b][:], tc_[0][:], tc_[1][:], ALU.subtract)
                    kb.tt("pool", ci[b][:], tc_[2][:], tc_[3][:], ALU.add)
                yield

        def pair_gen(j, q, S, B, u3):
            WBb = S["WBb"]
            k = 4 * j + q
            rs = slice(32 * q, 32 * q + 32)
            pr = PB[2 * (q % 2)][:, 0:NCH]
            pi = PB[2 * (q % 2) + 1][:, 0:NCH]
            with Collect():
                for i_ in range(CH):
                    kb.mm(pr, WBb[0][rs, CH - 1 - i_, :], u3[rs, :, i_], start=(i_ == 0), stop=(i_ == CH - 1))
            with Collect():
                for i_ in range(CH):
                    kb.mm(pi, WBb[1][rs, CH - 1 - i_, :], u3[rs, :, i_], start=(i_ == 0), stop=(i_ == CH - 1))
            f16k = f16[:, k:k + 1]
            tcos, tsin, cre, cim, tm1, tm2 = (B[x_] for x_ in ("tcos", "tsin", "cre", "cim", "tm1", "tm2"))
            kb.ts("pool", tm1[:], iota_c[:], f16k, ALU.mult, MAGIC, ALU.add)
            kb.ts("pool", tm1[:], tm1[:], -1.0, ALU.mult, MAGIC, ALU.add)
            kb.stt(tm1[:], iota_c[:], f16k, tm1[:], ALU.mult, ALU.add)
            yield
            kb.act(tsin[:], tm1[:], AF.Sin, scale=TWO_PI * 0.9999999)
            kb.act(tcos[:], tm1[:], AF.Sin, scale=math.pi * 0.9999999)
            kb.act(tcos[:], tcos[:], AF.Square, scale=math.sqrt(2.0))
            kb.act(tcos[:], tcos[:], AF.Identity, scale=-1.0, bias=c1c)
            yield
            kb.tt("dve", cre[:], pr, tcos[:], ALU.mult)
            kb.tt("dve", tm1[:], pi, tsin[:], ALU.mult)
            kb.tt("dve", cre[:], cre[:], tm1[:], ALU.add)
            kb.tt("dve", cim[:], pi, tcos[:], ALU.mult)
            kb.tt("dve", tm2[:], pr, tsin[:], ALU.mult)
            kb.tt("dve", cim[:], cim[:], tm2[:], ALU.subtract)
            yield
            r16 = rho16[:, k:k + 1].to_broadcast([128, NCH])
            kb.scan(cre[:], r16, cre[:], 0.0)
            kb.scan(cim[:], r16, cim[:], 0.0)
            yield
            kb.tt("dve", tm1[:], cre[:], tcos[:], ALU.mult)
            kb.tt("pool", tm2[:], cim[:], tsin[:], ALU.mult)
            kb.tt("dve", Xs[q][0][:, 1:NCH + 1], tm1[:], tm2[:], ALU.subtract)
            yield
            kb.tt("dve", tm1[:], cre[:], tsin[:], ALU.mult)
            kb.tt("pool", tm2[:], cim[:], tcos[:], ALU.mult)
            kb.tt("dve", Xs[q][1][:, 1:NCH + 1], tm1[:], tm2[:], ALU.add)
            yield

        def main_gen(j, S):
            CLb, Kbd = S["CLb"], S["Kbd"]

            upv = uTb[:, :].rearrange("p (i c) -> p c i", i=CH)

            def ev(n, ps):
                kb.copy("act", uT[:, n * 512:(n + 1) * 512], ps)
                cpc = 512 // CH
                kb.copy("act", upv[:, n * cpc:(n + 1) * cpc, :], ps.rearrange("p (c i) -> p c i", i=CH))
            proj(128 * j, 128, ev, bank_fn=bank7)
            if j == 0:
                dbg("uT", uT[:], [128, T])
            yield
            u3 = uTb[:, :].rearrange("p (i c) -> p c i", i=CH)
            for q0 in (0, 2):
                gens = [pair_gen(j, q0 + q_, S, csets[q0 + q_], u3) for q_ in range(2)]
                alive = [True] * 2
                while any(alive):
                    for gi_ in range(2):
                        if alive[gi_]:
                            try:
                                next(gens[gi_])
                            except StopIteration:
                                alive[gi_] = False
                    yield
            uf3 = uT[:, :].rearrange("p (c i) -> p c i", i=CH)
            y3 = yacc[:, :].rearrange("p (c i) -> p c i", i=CH)
            for jl in range(CH):
                acc = bank7()
                with Collect():
                    for tau in range(jl + 1):
                        kb.mm(acc[:, 0:NCH], Kbd[:, tau, :], u3[:, :, jl - tau], start=(tau == 0), stop=False)
                    for r_ in range(2):
                        for q in range(4):
                            kb.mm(acc[32 * q:32 * q + 32, 0:NCH], CLb[r_][:, jl + 1, 32 * q:32 * q + 32], Xs[q][r_][:, 0:NCH],
                                  start=False, stop=(r_ == 1), tile_position=(0, 32 * q))
                kb.stt(y3[:, :, jl], uf3[:, :, jl], PC["d_skip"](j), acc[:, 0:NCH], ALU.mult, ALU.add)
                yield
            kb.act(gb[:, j, :], yacc[:], AF.Gelu)
            yield

        pg = pre_gen(0, tsets[0])
        for _ in pg:
            pass
        for j in range(4):
            mg = main_gen(j, tsets[j % 2])
            pg = pre_gen(j + 1, tsets[(j + 1) % 2]) if j + 1 < 4 else None
            am, ap_ = True, pg is not None
            while am or ap_:
                if am:
                    try:
                        next(mg)
                    except StopIteration:
                        am = False
                if ap_:
                    try:
                        next(pg)
                    except StopIteration:
                        ap_ = False
        for n in range(4):
            ns = slice(n * 512, (n + 1) * 512)
            for jo in range(4):
                ps = bank()
                for ji in range(4):
                    kb.mm(ps[:, :], wglu[:, ji, jo * 128:(jo + 1) * 128], gb[:, ji, ns], start=(ji == 0), stop=(ji == 3))
                kb.act(sigt[:], ps[:, :], AF.Sigmoid, bias=PC["b_glu"](jo))
                kb.tt("dve", ysb[:, jo, :], gb[:, jo, ns], sigt[:], ALU.mult)
                kb.act(sqb[:], ysb[:, jo, :], AF.Square)
                kb.mm(PH[:, :], ones_b[:], sqb[:], start=(jo == 0), stop=(jo == 3))
            kb.act(rstd[:], PH[:, :], AF.Sqrt, bias=cE, scale=1.0 / 512.0)
            kb.recip(rstd[:], rstd[:])
            for jo in range(4):
                kb.stt(mixT[:, jo, ns], ysb[:, jo, :], PC["s5_out_g"](jo), rstd[:], ALU.mult, ALU.mult)
        dbg("ys5", mixT[:, 0, :], [128, T])

    def rwkv_phase():
        BL = 512
        HS = [slice(0, 64), slice(64, 128)]

        def cc8():
            with Collect():
                for cc_ in range(8):
                    yield cc_
        ar = Arena(AR)
        twza = ar.t("twza", [128, T], BF16)
        sg1b = ar.t("sg1b", [128, T], BF16)
        sg2b = ar.t("sg2b", [128, T], BF16)
        wa2b = ar.t("wa2b", [128, 512], BF16)
        g2b1 = ar.t("g2b1", [128, 512], BF16)
        g2b2 = ar.t("g2b2", [128, 512], BF16)
        resetm = ar.t("resetm", [128, BL], BF16)
        wtl = [[ar.t("w%s%d" % (nm, i), [128, 8, 128], BF16) for i in range(1)] for nm in "krv"]
        carry = ar.t("carry", [128, 4])
        S32 = ar.t("S32", [128, 64])
        Sb = [ar.t("Sb%d" % i, [128, 64], BF16) for i in range(2)]
        gst = ar.t("gst", [128, 64])
        zf = ar.t("zf", [128, 64])
        base_pipe = ar.cur
        arl = Arena(base_pipe)
        Rf = arl.t("Rf", [128, T + 1]); X4f = arl.t("X4f", [128, T]); X5f = arl.t("X5f", [128, T])
        s1o = []
        for i in range(3):
            d_ = {}
            for nm in ("At", "Bt", "Kt", "Rt", "BP", "KP", "vb"):
                d_[nm] = ar.t("s1_%s%d" % (nm, i), [128, BL], BF16)
            d_["PCc"] = ar.t("s1_PC%d" % i, [128, 8])
            d_["bonus"] = ar.t("s1_bo%d" % i, [128, BL])
            s1o.append(d_)
        s2o = []
        for i in range(2):
            d_ = {}
            for nm in ("BPt", "KPt", "Vt", "Att", "AK", "RB", "RK", "Z", "TAk", "AKV", "U0", "GT", "H0", "R2T"):
                d_[nm] = ar.t("s2_%s%d" % (nm, i), [128, 8, 64], BF16)
            s2o.append(d_)
        Rk = [ar.t("Rk%d" % i, [128, BL + 1]) for i in range(3)]
        X1 = ar.t("X1", [128, BL]); X2 = ar.t("X2", [128, BL]); X3 = ar.t("X3", [128, BL])
        X4 = ar.t("X4", [128, BL]); X5 = ar.t("X5", [128, BL]); X6 = ar.t("X6", [128, BL]); X7 = ar.t("X7", [128, BL])
        sqb = ar.t("rsqb", [128, BL], BF16)
        sqb2 = ar.t("rsqb2", [128, BL], BF16)
        Y2 = ar.t("Y2", [128, BL]); Y3 = ar.t("Y3", [128, BL]); CS = ar.t("CS", [128, BL])
        Ee = [ar.t("Ee%d" % i, [128, 512], BF16) for i in range(2)]
        Ff = [ar.t("Ff%d" % i, [128, 512], BF16) for i in range(2)]
        Zz = [ar.t("Zz%d" % i, [128, 512], BF16) for i in range(2)]
        Ubs = [ar.t("Ub%d" % i, [128, 64], BF16) for i in range(2)]
        sqf = ar.t("sqf", [128, 512]); ytm = ar.t("ytm", [128, 512]); ynb = ar.t("ynb", [128, 512], BF16)
        ygf = ar.t("ygf", [128, 512])

        bS1 = [PB[0], PB[1]]; bS2 = [PB[2], PB[3]]; bS3 = PB[4]
        ci = {"1": 0, "2": 0}

        def bk1():
            ci["1"] += 1
            return bS1[ci["1"] % 2]

        def bk2():
            ci["2"] += 1
            return bS2[ci["2"] % 2]

        kb.dma(X4f[0:64, 0:512], I["w2"])
        kb.dma(X4f[64:128, 0:512], I["a2"])
        kb.copy("pool", wa2b[:], X4f[:, 0:512])
        kb.dma(X5f[:, 0:512], I["g2"][0:128, :])
        kb.dma(X5f[0:32, 512:1024], I["g2"][128:160, :])
        kb.copy("pool", g2b1[:], X5f[:, 0:512])
        kb.copy("pool", g2b2[0:32, :], X5f[0:32, 512:1024])
        kb.memset("pool", resetm[:], 1.0)
        kb.memset("pool", resetm[:].rearrange("p (c j) -> p c j", j=64)[:, :, 0:1], 0.0)
        kb.memset("pool", zf[:], 0.0)

        def proj_shift_full(c0, M, mu_ap, dst):
            def ev(n, ps):
                kb.copy("act", Rf[0:M, 1 + n * 512:1 + (n + 1) * 512], ps)
            proj(c0, M, ev)
            kb.memset("pool", Rf[0:M, 0:1], 0.0)
            kb.tt("pool", X4f[0:M, :], Rf[0:M, 0:T], Rf[0:M, 1:T + 1], ALU.subtract)
            kb.stt(dst[0:M, :], X4f[0:M, :], mu_ap, Rf[0:M, 1:T + 1], ALU.mult, ALU.add)

        proj_shift_full(2048, 128, PC["mu_l"](0), X5f)
        kb.act(twza[0:64, :], X5f[0:64, :], AF.Tanh)
        kb.copy("act", twza[64:128, :], X5f[64:128, :])
        proj_shift_full(2176, 128, PC["mu_l"](1), X5f)
        kb.act(sg1b[:], X5f[:], AF.Sigmoid)
        proj_shift_full(2304, 32, mu_g2, X5f)
        kb.act(sg2b[0:32, :], X5f[0:32, :], AF.Sigmoid)

        def load_hp_weights(hp):
            for xi, c0 in enumerate((1024, 512, 1536)):
                load_w(wtl[xi][0], I["w_in"][:, c0 + 128 * hp:c0 + 128 * hp + 128], 8, 128)

        def S1(u):
            hp, blk = divmod(u, 4)
            o = s1o[u % 3]
            hc = slice(hp * 128, (hp + 1) * 128)
            ts_ = slice(blk * BL, (blk + 1) * BL)
            if blk == 0:
                load_hp_weights(hp)
            dsts = (X1, X3, X5)
            mi = (4 + hp, hp, 8 + hp)
            for xi in range(3):
                w = wtl[xi][0]
                R = Rk[xi]
                ps = bk1()
                with Collect():
                    for c in range(8):
                        kb.mm(ps[:, :], w[:, c, :], xnT[:, c, ts_], start=(c == 0), stop=(c == 7))
                kb.copy("act", R[:, 1:BL + 1], ps[:, :])
                kb.act(dsts[xi][:], ps[:, :], AF.Copy, scale=PC["omu_rkv"](mi[xi]))
                if blk == 0:
                    kb.memset("pool", R[:, 0:1], 0.0)
                else:
                    kb.copy("pool", R[:, 0:1], carry[:, xi:xi + 1])
                kb.copy("pool", carry[:, xi:xi + 1], R[:, BL:BL + 1])
                kb.stt(dsts[xi][:], R[:, 0:BL], PC["mu_rkv"](mi[xi]), dsts[xi][:], ALU.mult, ALU.add)
                yield
            ps = bk1()
            kb.mm(ps[:, :], wa2b[64:128, hc], twza[64:128, ts_])
            kb.act(X2[:], ps[:, :], AF.Sigmoid, bias=PC["a0"](hp))
            ps = bk1()
            kb.mm(ps[:, :], wa2b[0:64, hc], twza[0:64, ts_])
            kb.act(X6[:], ps[:, :], AF.Sigmoid, bias=PC["w0"](hp))
            yield
            kb.scan(CS[:], resetm[:], X6[:], 0.0)
            cs3 = CS[:].rearrange("p (c j) -> p c j", j=64)
            kb.act(X4[:], X1[:], AF.Copy, scale=PC["k_k"](hp))
            kb.act(sqb[:], X4[:], AF.Square)
            ps = bk1()
            kb.mm(ps[:, :], blk_b[:], sqb[:])
            kb.act(X7[:], ps[:, :], AF.Ln)
            kb.act(X7[:], X7[:], AF.Exp, scale=-0.5)
            kb.stt(X4[:], X7[:], 1e12, X4[:], ALU.min, ALU.mult)
            yield
            kb.ts("dve", Y2[:], X2[:], -1.0, ALU.add, PC["k_a"](hp), ALU.mult)
            kb.stt(X1[:], Y2[:], 1.0, X1[:], ALU.add, ALU.mult)
            kb.tt("dve", X2[:], X4[:], X2[:], ALU.mult)
            yield
            kb.tt("dve", X6[:], CS[:], X6[:], ALU.subtract)
            kb.act(X6[:], X6[:], AF.Exp, scale=-C0)
            kb.stt(o["At"][:], X4[:], -1.0, X6[:], ALU.mult, ALU.mult)
            yield
            kb.act(Y3[:], CS[:], AF.Exp, scale=C0)
            kb.tt("dve", o["Bt"][:], X2[:], Y3[:], ALU.mult)
            kb.tt("dve", o["Kt"][:], X1[:], Y3[:], ALU.mult)
            yield
            kb.tt("dve", Y2[:].rearrange("p (c j) -> p c j", j=64), cs3[:, :, 63:64].to_broadcast([128, 8, 64]), cs3, ALU.subtract)
            kb.act(Y2[:], Y2[:], AF.Exp, scale=-C0)
            kb.tt("dve", o["BP"][:], X2[:], Y2[:], ALU.mult)
            kb.tt("dve", o["KP"][:], X1[:], Y2[:], ALU.mult)
            kb.act(o["PCc"][:, :], cs3[:, :, 63], AF.Exp, scale=-C0)
            yield
            kb.act(X7[:], CS[:], AF.Exp, scale=-C0)
            kb.tt("dve", o["Rt"][:], X3[:], X7[:], ALU.mult)
            kb.tt("dve", Y3[:], X3[:], X1[:], ALU.mult)
            kb.act(sqb2[:], Y3[:], AF.Copy, scale=PC["r_k"](hp))
            kb.copy("act", o["vb"][:], X5[:])
            ps = bk1()
            kb.mm(ps[:, :], blk_b[:], sqb2[:])
            kb.tt("dve", o["bonus"][:], ps[:, :], X5[:], ALU.mult)
            yield

        def S2(u):
            o = s1o[u % 3]
            q = s2o[u % 2]
            for (src, dstt) in ((o["BP"], q["BPt"]), (o["KP"], q["KPt"]), (o["vb"], q["Vt"]), (o["At"], q["Att"])):
                for cc in cc8():
                    for h in range(2):
                        sl = HS[h]
                        kb.tr(PT[sl, cc * 64:(cc + 1) * 64], src[sl, cc * 64:(cc + 1) * 64], ident_b[sl, sl])
                kb.copy("act", dstt[:].rearrange("p c f -> p (c f)"), PT[:, 0:512])
                yield
            combos = ((o["Bt"], o["At"], m_su, None), (o["At"], o["Bt"], m_sl, None), (o["Kt"], o["At"], m_su, q["AK"]),
                      (o["Bt"], o["Rt"], m_iu, q["RB"]), (o["Kt"], o["Rt"], m_iu, q["RK"]))
            for ci_, (lh, rh, msk, dst_) in enumerate(combos):
                ps = bk2()
                for cc in cc8():
                    cs_ = slice(cc * 64, (cc + 1) * 64)
                    for h in range(2):
                        sl = HS[h]
                        kb.mm(ps[sl, cs_], lh[sl, cs_], rh[sl, cs_])
                if ci_ == 0:
                    o_ = Ee[0][:]
                elif ci_ == 1:
                    o_ = Ff[0][:]
                else:
                    o_ = dst_[:].rearrange("p c f -> p (c f)")
                kb.tt("dve", o_, ps[:, :], msk[:].rearrange("p c f -> p (c f)"), ALU.mult)
                yield
            kb.tt("dve", Zz[0][:], Ee[0][:], m_id[:].rearrange("p c f -> p (c f)"), ALU.add)
            for lvl in range(1, 6):
                ep, fp, zp = Ee[(lvl - 1) % 2], Ff[(lvl - 1) % 2], Zz[(lvl - 1) % 2]
                en, fn_, zn = Ee[lvl % 2], Ff[lvl % 2], Zz[lvl % 2]
                ps = bk2()
                for cc in cc8():
                    c6 = slice(cc * 64, (cc + 1) * 64)
                    for h in range(2):
                        sl = HS[h]
                        kb.mm(ps[sl, c6], ep[sl, c6], fp[sl, c6])
                kb.copy("act", fn_[:], ps[:, :])
                yield
                if lvl < 5:
                    ps = bk2()
                    for cc in cc8():
                        c6 = slice(cc * 64, (cc + 1) * 64)
                        for h in range(2):
                            sl = HS[h]
                            kb.mm(ps[sl, c6], fp[sl, c6], ep[sl, c6])
                    kb.copy("act", en[:], ps[:, :])
                    yield
                ps = bk2()
                for cc in cc8():
                    c6 = slice(cc * 64, (cc + 1) * 64)
                    for h in range(2):
                        sl = HS[h]
                        kb.mm(ps[sl, c6], fn_[sl, c6], zp[sl, c6])
                zo = zn[:] if lvl < 5 else q["Z"][:].rearrange("p c f -> p (c f)")
                kb.tt("dve", zo, zp[:], ps[:, :], ALU.add)
                yield
            Z2 = q["Z"][:].rearrange("p c f -> p (c f)")
            ps = bk2()
            for cc in cc8():
                c6 = slice(cc * 64, (cc + 1) * 64)
                for h in range(2):
                    sl = HS[h]
                    kb.mm(ps[sl, c6], Z2[sl, c6], q["Att"][sl, cc, :])
            kb.copy("act", q["TAk"][:].rearrange("p c f -> p (c f)"), ps[:, :])
            yield
            ps = bk2()
            for cc in cc8():
                c6 = slice(cc * 64, (cc + 1) * 64)
                for h in range(2):
                    sl = HS[h]
                    kb.mm(ps[sl, c6], q["AK"][sl, cc, :], q["Vt"][sl, cc, :])
            kb.copy("act", q["AKV"][:].rearrange("p c f -> p (c f)"), ps[:, :])
            yield
            ps = bk2()
            for cc in cc8():
                c6 = slice(cc * 64, (cc + 1) * 64)
                for h in range(2):
                    sl = HS[h]
                    kb.mm(ps[sl, c6], Z2[sl, c6], q["AKV"][sl, cc, :])
            kb.copy("act", q["U0"][:].rearrange("p c f -> p (c f)"), ps[:, :])
            yield
            ps = bk2()
            for cc in cc8():
                c6 = slice(cc * 64, (cc + 1) * 64)
                for h in range(2):
                    sl = HS[h]
                    kb.mm(ps[sl, c6], q["TAk"][sl, cc, :], q["BPt"][sl, cc, :])
            kb.copy("act", q["GT"][:].rearrange("p c f -> p (c f)"), ps[:, :])
            yield
            ps = bk2()
            for cc in cc8():
                c6 = slice(cc * 64, (cc + 1) * 64)
                for h in range(2):
                    sl = HS[h]
                    kb.mm(ps[sl, c6], q["BPt"][sl, cc, :], q["U0"][sl, cc, :], start=True, stop=False)
                    kb.mm(ps[sl, c6], q["KPt"][sl, cc, :], q["Vt"][sl, cc, :], start=False, stop=True)
            kb.copy("act", q["H0"][:].rearrange("p c f -> p (c f)"), ps[:, :])
            yield
            ps = bk2()
            for cc in cc8():
                c6 = slice(cc * 64, (cc + 1) * 64)
                for h in range(2):
                    sl = HS[h]
                    kb.mm(ps[sl, c6], q["TAk"][sl, cc, :], q["RB"][sl, cc, :])
            kb.tt("dve", q["R2T"][:].rearrange("p c f -> p (c f)"), ps[:, :], o["Rt"][:], ALU.add)
            yield

        cglob = [0]

        def S3(u):
            hp, blk = divmod(u, 4)
            o = s1o[u % 3]
            q = s2o[u % 2]
            hc = slice(hp * 128, (hp + 1) * 128)
            ts_ = slice(blk * BL, (blk + 1) * BL)
            PY = PHs[u % 2]
            if blk == 0:
                kb.memset("pool", S32[:], 0.0)
                kb.memset("pool", Sb[cglob[0] % 2][:], 0.0)
            for cc in range(8):
                c6 = slice(cc * 64, (cc + 1) * 64)
                g = cglob[0]
                cglob[0] += 1
                sb_old = Sb[g % 2]
                sb_new = Sb[(g + 1) % 2]
                pcol_ = 0
                pS = bS3[:, pcol_:pcol_ + 64]
                with Collect():
                    for h in range(2):
                        sl = HS[h]
                        kb.mm(bS3[sl, pcol_:pcol_ + 64], ident_b[sl, sl], q["H0"][sl, cc, :], start=True, stop=False)
                    for h in range(2):
                        sl = HS[h]
                        kb.mm(bS3[sl, pcol_:pcol_ + 64], q["GT"][sl, cc, :], sb_old[sl, :], start=False, stop=True)
                with Collect():
                    for st_, (A_, B_) in enumerate(((q["RB"], q["U0"]), (q["RK"], q["Vt"]), (q["R2T"], None))):
                        for h in range(2):
                            sl = HS[h]
                            rhs_ = sb_old[sl, :] if B_ is None else B_[sl, cc, :]
                            kb.mm(PY[sl, c6], A_[sl, cc, :], rhs_, start=(st_ == 0), stop=(st_ == 2))
                kb.stt(sb_new[:], S32[:], o["PCc"][:, cc:cc + 1], pS, ALU.mult, ALU.add)
                kb.stt(S32[:], S32[:], o["PCc"][:, cc:cc + 1], pS, ALU.mult, ALU.add)
                yield
            kb.copy("act", ytm[:], PY[:, :])
            for cc in range(8):
                c6 = slice(cc * 64, (cc + 1) * 64)
                kb.stt(sqf[:, c6], ytm[:, c6], 1.0, ytm[:, c6], ALU.mult, ALU.mult, accum_out=gst[:, 8 + cc:9 + cc])
                kb.stt(sqf[:, c6], ytm[:, c6], 1.0, zf[:, :], ALU.mult, ALU.add, accum_out=gst[:, cc:cc + 1])
            yield
            kb.ts("dve", gst[:, 16:24], gst[:, 0:8], 1.0 / 64.0, ALU.mult)
            kb.tt("dve", gst[:, 24:32], gst[:, 16:24], gst[:, 16:24], ALU.mult)
            kb.stt(gst[:, 32:40], gst[:, 8:16], 1.0 / 64.0, gst[:, 24:32], ALU.mult, ALU.subtract)
            kb.act(gst[:, 40:48], gst[:, 32:40], AF.Sqrt, bias=cG)
            kb.recip(gst[:, 40:48], gst[:, 40:48])
            y3 = ytm[:].rearrange("p (c f) -> p c f", f=64)
            kb.tt("dve", y3, y3, gst[:, 16:24].unsqueeze(2).to_broadcast([128, 8, 64]), ALU.subtract)
            kb.tt("dve", ynb[:].rearrange("p (c f) -> p c f", f=64), y3, gst[:, 40:48].unsqueeze(2).to_broadcast([128, 8, 64]), ALU.mult)
            yield
            for cc in cc8():
                for h in range(2):
                    sl = HS[h]
                    kb.tr(PT[sl, 512 + cc * 64:512 + (cc + 1) * 64], ynb[sl, cc * 64:(cc + 1) * 64], ident_b[sl, sl])
            kb.act(ygf[:], PT[:, 512:1024], AF.Identity, scale=PC["ln_x_w"](hp), bias=PC["ln_x_b"](hp))
            kb.tt("pool", ygf[:], ygf[:], o["bonus"][:], ALU.add)
            pg = bk2()
            kb.mm(pg[:, :], g2b1[:, hc], sg1b[:, ts_], start=True, stop=False)
            kb.mm(pg[:, :], g2b2[0:32, hc], sg2b[0:32, ts_], start=False, stop=True)
            kb.tt("dve", mixT[:, 4 + hp, ts_], ygf[:], pg[:, :], ALU.mult)
            yield

        def drain(gen, n=None):
            k = 0
            if gen is None:
                return False
            while n is None or k < n:
                try:
                    next(gen)
                except StopIteration:
                    return False
                k += 1
            return True

        NU = 16
        g1 = {u: S1(u) for u in range(NU)}
        g2 = {u: S2(u) for u in range(NU)}
        drain(g1[0]); drain(g2[0]); drain(g1[1])
        for u in range(NU):
            g3 = S3(u)
            a2 = g2.get(u + 1)
            a1 = g1.get(u + 2)
            alive3 = True
            alive2 = a2 is not None
            alive1 = a1 is not None
            while alive3 or alive2 or alive1:
                if alive3:
                    alive3 = drain(g3, 1)
                if alive2:
                    alive2 = drain(a2, 3)
                if alive1:
                    alive1 = drain(a1, 1)
        dbg("yr", mixT[:, 4, :], [128, T])

    def tail_phase():
        ar = Arena(AR)
        H = ar.t("H", [128, 16, D])
        xnb2 = [ar.t("xnb2_%d" % i, [128, D], BF16) for i in range(2)]
        actT = ar.t("actT", [128, 4, T], BF16)
        rls = [ar.t("rl%d" % i, [128, 512]) for i in range(2)]
        wo_b = ar.t("wo_b", [128, 8, D], BF16)
        wup_l = [kb.sb("wup_b0", [128, 8, 512], BF16, at=kb.off_of["wo_b"]), ar.t("wup_b1", [128, 8, 512], BF16)]
        wdn_l = [kb.sb("wdn_b0", [128, 4, D], BF16, at=kb.off_of["wo_b"] + 8192), ar.t("wdn_b1", [128, 4, D], BF16)]
        kb.pe_warm = True
        load_w(wo_b, I["w_out"], 8, D)
        for tt in range(16):
            xb_ = xst[tt % 2]
            kb.dma(xb_[:], I["x"][tt * 128:(tt + 1) * 128, :])
            for dh in range(2):
                ps = bank()
                with Collect():
                    for c in range(8):
                        kb.mm(ps[:, :], mixT[:, c, tt * 128:(tt + 1) * 128], wo_b[:, c, dh * 512:(dh + 1) * 512], start=(c == 0), stop=(c == 7))
                hs = H[:, tt, dh * 512:(dh + 1) * 512]
                kb.tt("dve", hs, ps[:, :], xb_[:, dh * 512:(dh + 1) * 512], ALU.add)
        dbg("h1", H[:, 0, :], [128, D])
        if 'hn' not in skip:
            norm_T(lambda tt: H[:, tt, :], gffn_T, xnT, 16, actT[:, 0, 0:D], xnb2)
        for g in range(0 if 'ffn' in skip else 8):
            wup_b = wup_l[g % 2]
            wdn_b = wdn_l[g % 2]
            load_w(wup_b, I["w_up"][:, g * 512:(g + 1) * 512], 8, 512)
            load_w(wdn_b, I["w_down"][g * 512:(g + 1) * 512, :], 4, D)
            for fc in range(4):
                for n in range(4):
                    ns = slice(n * 512, (n + 1) * 512)
                    ps = bank()
                    with Collect():
                        for c in range(8):
                            kb.mm(ps[:, :], wup_b[:, c, fc * 128:(fc + 1) * 128], xnT[:, c, ns], start=(c == 0), stop=(c == 7))
                    rl_ = rls[(fc * 4 + n) % 2]
                    kb.act(rl_[:], ps[:, :], AF.Relu)
                    kb.act(actT[:, fc, ns], rl_[:], AF.Square)
            for tt in range(16):
                for dh in range(2):
                    ps = bank()
                    with Collect():
                        for fc in range(4):
                            kb.mm(ps[:, :], actT[:, fc, tt * 128:(tt + 1) * 128], wdn_b[:, fc, dh * 512:(dh + 1) * 512], start=(fc == 0), stop=(fc == 3))
                    hs = H[:, tt, dh * 512:(dh + 1) * 512]
                    kb.tt("dve", hs, hs, ps[:, :], ALU.add)
        for tt in range(16):
            ss = stat[:, 32 + tt:33 + tt]
            kb.stt(actT[:, tt % 4, 0:D], H[:, tt, :], 1.0, H[:, tt, :], ALU.mult, ALU.mult, accum_out=ss)
            kb.act(ss, ss, AF.Sqrt, bias=cE, scale=1.0 / D)
            kb.recip(ss, ss)
            jb_ = xst[tt % 2]
            kb.act(jb_[:], H[:, tt, :], AF.Copy, scale=ss)
            kb.tt("dve", H[:, tt, :], jb_[:], gF[:], ALU.mult)
            kb.dma(out_d[tt * 128:(tt + 1) * 128, :], H[:, tt, :], final=True)

    if "s5" not in skip:
        kb.phase = 'S5'
        s5_phase()
    else:
        kb.memset("pool", mixT[:, 0:4, :], 0.0)
    if "rwkv" not in skip:
        kb.phase = 'RWKV'
        rwkv_phase()
    else:
        kb.memset("pool", mixT[:, 4:8, :], 0.0)
    if 'tail' not in skip:
        kb.phase = 'TAIL'
        tail_phase()
    if 'nosched' not in skip:
        est = kb.schedule()
    kb.emit()
    return nc, dbg_out


_CACHE = {}


def _prep_inputs(inputs):
    sq = {}
    for k, shp in IN_SHAPES.items():
        if k == "x":
            continue
        a = np.asarray(inputs[k], dtype=np.float32)
        sq[k] = np.ascontiguousarray(a.reshape(shp))
    return sq


def kernel(**inputs):
    x = np.asarray(inputs["x"], dtype=np.float32)
    if "nc" not in _CACHE:
        _CACHE["nc"] = build()[0]
    nc = _CACHE["nc"]
    shared = _prep_inputs(inputs)
    in_maps = []
    for b in range(8):
        m = dict(shared)
        m["x"] = np.ascontiguousarray(x[b])
        in_maps.append(m)
    res = run_bass_kernel_spmd(nc, in_maps, core_ids=list(range(8)))
    out = np.stack([np.asarray(r["out"], dtype=np.float32) for r in res.results], axis=0)
    return out.astype(inputs["x"].dtype if hasattr(inputs["x"], "dtype") else np.float32)
```

```python
import contextlib
import math
import numpy as np
import concourse.bass as bass
import concourse.mybir as mybir
from concourse.bass_utils import run_bass_kernel_spmd

F32 = mybir.dt.float32
BF16 = mybir.dt.bfloat16
I32 = mybir.dt.int32
AF = mybir.ActivationFunctionType
ALU = mybir.AluOpType
AX = mybir.AxisListType

T = 2048
D = 1024
DIN = 2336
DFF = 4096
NORM_EPS = 1e-6
GN_EPS = 64e-5
C0 = math.exp(-0.5)
TWO_PI = 2.0 * math.pi
MAGIC = 12582912.0

ENGS = ("pe", "act", "dve", "pool", "sp")
N_DMA_SEMS = 24
DMA_POOLS = {"sp": (0, 14), "pool": (14, 8), "act": (22, 2)}
STRICT_SAME_ENGINE = True
ESZ = {F32: 4, BF16: 2, I32: 4}


class Op:
    __slots__ = ("eng", "fn", "deps", "is_dma", "dma_sem", "dma_val", "signaled", "sig_idx", "all_deps", "cost", "lat", "prio", "idx", "fin", "nsucc", "succ", "phase")

    def __init__(self, eng, fn, is_dma=False):
        self.eng = eng
        self.fn = fn
        self.deps = []
        self.is_dma = is_dma
        self.dma_sem = None
        self.dma_val = None
        self.signaled = False
        self.sig_idx = None
        self.all_deps = []
        self.cost = 0.2
        self.lat = 0.0
        self.prio = 0.0
        self.idx = 0
        self.fin = 0.0
        self.succ = []


class KB:
    def __init__(self, nc):
        self.nc = nc
        self.prog = {e: [] for e in ENGS}
        self.tinfo = {}
        self.recs = {}
        self.dma_rr = 0
        self.dma_rrs = {}
        self.dma_counts = [0] * N_DMA_SEMS
        self.final_dmas = []
        self.sb_off = 16640
        self.off_of = {}
        self.n_ops = 0
        self.all_ops = []
        self.bank_last = {}
        self.phase = 'A'

    def sb(self, name, shape, dtype=F32, at=None):
        fe = 1
        for s in shape[1:]:
            fe *= s
        nbytes = fe * ESZ[dtype]
        if at is None:
            at = self.sb_off
            self.sb_off += (nbytes + 63) // 64 * 64
        t = self.nc.alloc_sbuf_tensor_at(name, list(shape), dtype, offset=at)
        self.tinfo[t.name] = ("SB", at, fe, ESZ[dtype])
        self.off_of[name] = at
        return t

    def ps(self, name, shape, dtype=F32):
        fe = 1
        for s in shape[1:]:
            fe *= s
        t = self.nc.alloc_psum_tensor(name, list(shape), dtype)
        self.tinfo[t.name] = ("PS_" + name, 0, fe, ESZ[dtype])
        return t

    def rect(self, ap):
        info = self.tinfo.get(ap.tensor.name)
        if info is None:
            return None
        space, base, fe, es = info
        off = ap.offset
        pat = ap.ap
        p0 = off // fe
        f0 = off % fe
        pc = pat[0][1]
        ext = 1
        for s, c in pat[1:]:
            ext += (c - 1) * abs(s)
        return (space, p0, p0 + pc, base + f0 * es, base + (f0 + ext) * es)

    def op(self, eng, fn, outs=(), ins=(), is_dma=False, cost=None, extra_deps=()):
        o = Op(eng, fn, is_dma)
        deps = [d_ for d_ in extra_deps if d_ is not None]
        for ap in ins:
            if ap is None or isinstance(ap, (int, float)):
                continue
            r = self.rect(ap)
            if r is None:
                continue
            for rec in self.recs.setdefault(r[0], []):
                if rec[0] < r[2] and r[1] < rec[1] and rec[2] < r[4] and r[3] < rec[3]:
                    if rec[4] is not None:
                        deps.append(rec[4])
                    rec[5].append(o)
        for ap in outs:
            r = self.rect(ap)
            if r is None:
                continue
            lst = self.recs.setdefault(r[0], [])
            keep = []
            for rec in lst:
                if rec[0] < r[2] and r[1] < rec[1] and rec[2] < r[4] and r[3] < rec[3]:
                    if rec[4] is not None:
                        deps.append(rec[4])
                    deps.extend(rec[5])
                    if r[1] <= rec[0] and rec[1] <= r[2] and r[3] <= rec[2] and rec[3] <= r[4]:
                        continue
                keep.append(rec)
            keep.append([r[1], r[2], r[3], r[4], o, []])
            self.recs[r[0]] = keep
        seen = set()
        for d in deps:
            if d is o or id(d) in seen:
                continue
            seen.add(id(d))
            o.all_deps.append(d)
            if d.eng == eng and not d.is_dma:
                if eng == "pe" or not STRICT_SAME_ENGINE:
                    continue
            o.deps.append(d)
        n_el = 1
        ref = None
        for ap in list(outs) + list(ins):
            if ap is None or isinstance(ap, (int, float)):
                continue
            ref = ap
            break
        if ref is not None:
            for d_ in ref.shape[1:]:
                n_el *= d_
        if cost is not None:
            o.cost = cost
        elif is_dma:
            o.cost = 0.06
            o.lat = 2.0 + n_el * ref.shape[0] * 4 / 150e3
        elif eng == "pe":
            o.cost = max(0.055, n_el / 1200.0) if cost is None else cost
        elif eng == "act":
            o.cost = 0.2 + n_el / 1200.0
        elif eng == "dve":
            o.cost = 0.1 + n_el / 960.0
        elif eng == "pool":
            o.cost = 0.12 + n_el / 300.0
        o.idx = self.n_ops
        o.phase = self.phase
        self.all_ops.append(o)
        self.prog[eng].append(o)
        self.n_ops += 1
        return o

    def schedule(self):
        import heapq
        ops = self.all_ops
        for o in ops:
            o.succ = []
        for o in ops:
            for d in o.all_deps:
                d.succ.append(o)
        for o in reversed(ops):
            m = 0.0
            for s_ in o.succ:
                if s_.prio > m:
                    m = s_.prio
            o.prio = o.cost + o.lat + m
        indeg = {id(o): len(o.all_deps) for o in ops}
        ready = {e: [] for e in ENGS}
        for o in ops:
            if indeg[id(o)] == 0:
                heapq.heappush(ready[o.eng], (-o.prio, o.idx, o))
        free_at = {e: 0.0 for e in ENGS}
        running = []
        relc = [0]
        XLAT = 0.08
        order = {e: [] for e in ENGS}
        now = 0.0
        done = 0
        N = len(ops)
        while done < N:
            progressed = False
            for e in ENGS:
                if free_at[e] <= now and ready[e]:
                    _, _, o = heapq.heappop(ready[e])
                    order[e].append(o)
                    free_at[e] = now + o.cost
                    o.fin = now + o.cost + o.lat
                    heapq.heappush(running, (o.fin, o.idx, o))
                    progressed = True
            if progressed:
                continue
            cands = [free_at[e] for e in ENGS if free_at[e] > now and ready[e]]
            if running:
                cands.append(running[0][0])
            assert cands, "scheduler stuck"
            now = min(cands)
            while running and running[0][0] <= now + 1e-12:
                _, _, o = heapq.heappop(running)
                if o.phase == "__rel__":
                    s_ = o.succ[0]
                    heapq.heappush(ready[s_.eng], (-s_.prio, s_.idx, s_))
                    continue
                done += 1
                for s_ in o.succ:
                    indeg[id(s_)] -= 1
                    if indeg[id(s_)] == 0:
                        if XLAT > 0 and s_.eng != o.eng:
                            r_ = Op("x", None)
                            r_.phase = "__rel__"
                            r_.succ = [s_]
                            relc[0] += 1
                            heapq.heappush(running, (now + XLAT, -relc[0], r_))
                        else:
                            heapq.heappush(ready[s_.eng], (-s_.prio, s_.idx, s_))
        self.prog = order
        self.est_makespan = now
        tot = {e: sum(o.cost for o in order[e]) for e in ENGS}
        ph = {}
        for o in ops:
            d_ = ph.setdefault(o.phase, {'t0': 1e9, 't1': 0.0, 'pe': 0.0, 'act': 0.0, 'dve': 0.0, 'pool': 0.0, 'sp': 0.0, 'n': 0})
            d_['t0'] = min(d_['t0'], o.fin - o.cost - o.lat); d_['t1'] = max(d_['t1'], o.fin); d_[o.eng] += o.cost; d_['n'] += 1
        for k_, d_ in ph.items():
            pass
        return now

    def dma(self, out, in_, eng="sp", final=False, slow=False):
        kw = {"allow_slow_non_contiguous": True} if slow else {}
        o = self.op(eng, lambda e: e.dma_start(out=out, in_=in_, **kw), outs=[out], ins=[in_], is_dma=True)
        if final:
            self.final_dmas.append(o)
        return o

    def act(self, out, in_, func, bias=0.0, scale=1.0, eng="act"):
        return self.op(eng, lambda e: e.activation(out=out, in_=in_, func=func, bias=bias, scale=scale),
                       outs=[out], ins=[in_, bias, scale])

    def tt(self, eng, out, a, b, op):
        return self.op(eng, lambda e: e.tensor_tensor(out=out, in0=a, in1=b, op=op), outs=[out], ins=[a, b])

    def ts(self, eng, out, a, s1, op0, s2=None, op1=None):
        if op1 is None:
            return self.op(eng, lambda e: e.tensor_scalar(out=out, in0=a, scalar1=s1, scalar2=None, op0=op0),
                           outs=[out], ins=[a, s1])
        return self.op(eng, lambda e: e.tensor_scalar(out=out, in0=a, scalar1=s1, scalar2=s2, op0=op0, op1=op1),
                       outs=[out], ins=[a, s1, s2])

    def stt(self, out, in0, scalar, in1, op0, op1, accum_out=None):
        if accum_out is None:
            return self.op("dve", lambda e: e.scalar_tensor_tensor(out=out, in0=in0, scalar=scalar, in1=in1, op0=op0, op1=op1),
                           outs=[out], ins=[in0, scalar, in1])
        return self.op("dve", lambda e: e.scalar_tensor_tensor(out=out, in0=in0, scalar=scalar, in1=in1, op0=op0, op1=op1, accum_out=accum_out),
                       outs=[out, accum_out], ins=[in0, scalar, in1])

    def copy(self, eng, out, in_):
        if eng == "act":
            return self.op(eng, lambda e: e.copy(out=out, in_=in_), outs=[out], ins=[in_])
        return self.op(eng, lambda e: e.tensor_copy(out=out, in_=in_), outs=[out], ins=[in_])

    def memset(self, eng, out, val):
        return self.op(eng, lambda e: e.memset(out, val), outs=[out])

    def mm(self, out, lhsT, rhs, start=True, stop=True, tile_position=None):
        if tile_position is None:
            bp = self.rect(lhsT)[1]
            if bp == 96:
                tile_position = (96, self.rect(out)[1])
        if tile_position is None:
            fn = lambda e: e.matmul(out, lhsT=lhsT, rhs=rhs, start=start, stop=stop)
        else:
            fn = lambda e: e.matmul(out, lhsT=lhsT, rhs=rhs, start=start, stop=stop, tile_position=tile_position)
        n_ = 1
        for d_ in rhs.shape[1:]:
            n_ *= d_
        c_ = max(0.055, n_ / (2300.0 if getattr(self, 'pe_warm', False) else 1200.0)) * (4.0 if rhs.dtype == F32 else 1.0)
        bkey = out.tensor.name
        xd = [self.bank_last.get(bkey)] if start else []
        o_ = self.op("pe", fn, outs=[out], ins=[lhsT, rhs] + ([] if start else [out]), cost=c_, extra_deps=xd)
        self.bank_last[bkey] = o_
        return o_

    def mmg(self, items):
        fns = []
        outs, ins, banks_start, banks_all = [], [], [], []
        cost = 0.0
        for it in items:
            out = it["out"]
            bkey = out.tensor.name
            if bkey not in banks_all:
                banks_all.append(bkey)
            if it.get("kind", "mm") == "tr":
                in_, ident = it["in_"], it["ident"]
                fns.append(lambda e, out=out, in_=in_, ident=ident: e.transpose(out=out, in_=in_, identity=ident))
                ins += [in_, ident]
                if bkey not in banks_start:
                    banks_start.append(bkey)
                n_ = in_.shape[-1]
                cost += max(0.03, n_ / 2400.0)
            else:
                lhsT, rhs = it["lhsT"], it["rhs"]
                start, stop = it.get("start", True), it.get("stop", True)
                tp = it.get("tile_position")
                if tp is None and self.rect(lhsT)[1] == 96:
                    tp = (96, self.rect(out)[1])
                if tp is None:
                    fns.append(lambda e, out=out, lhsT=lhsT, rhs=rhs, start=start, stop=stop: e.matmul(out, lhsT=lhsT, rhs=rhs, start=start, stop=stop))
                else:
                    fns.append(lambda e, out=out, lhsT=lhsT, rhs=rhs, start=start, stop=stop, tp=tp: e.matmul(out, lhsT=lhsT, rhs=rhs, start=start, stop=stop, tile_position=tp))
                ins += [lhsT, rhs]
                if start and bkey not in banks_start:
                    banks_start.append(bkey)
                n_ = 1
                for d_ in rhs.shape[1:]:
                    n_ *= d_
                cost += max(0.03, n_ / (2300.0 if getattr(self, 'pe_warm', False) else 1200.0) * (0.6 if n_ <= 128 else 1.0)) * (4.0 if rhs.dtype == F32 else 1.0)
            outs.append(out)

        def fn(e):
            r_ = None
            for f_ in fns:
                r_ = f_(e)
            return r_
        xd = [self.bank_last.get(b_) for b_ in banks_start]
        o_ = self.op("pe", fn, outs=outs, ins=ins, cost=cost, extra_deps=xd)
        for b_ in banks_all:
            self.bank_last[b_] = o_
        return o_

    def tr(self, out, in_, ident):
        bkey = out.tensor.name
        o_ = self.op("pe", lambda e: e.transpose(out=out, in_=in_, identity=ident), outs=[out], ins=[in_, ident],
                     extra_deps=[self.bank_last.get(bkey)])
        self.bank_last[bkey] = o_
        return o_

    def recip(self, out, in_):
        return self.op("dve", lambda e: e.reciprocal(out=out, in_=in_), outs=[out], ins=[in_])

    def scan(self, out, d0, d1, initial):
        return self.op("dve", lambda e: e.tensor_tensor_scan(out=out, data0=d0, data1=d1, initial=initial, op0=ALU.mult, op1=ALU.add),
                       outs=[out], ins=[d0, d1, initial])

    def reduce(self, out, in_, axis=None):
        return self.op("dve", lambda e: e.tensor_reduce(out=out, in_=in_, axis=AX.X, op=ALU.add), outs=[out], ins=[in_])

    def emit(self):
        nc = self.nc
        rr = {}
        cnt = [0] * N_DMA_SEMS
        for e in ENGS:
            for o in self.prog[e]:
                if o.is_dma:
                    lo, n = DMA_POOLS[e]
                    k = lo + rr.get(e, 0)
                    rr[e] = (rr.get(e, 0) + 1) % n
                    cnt[k] += 1
                    o.dma_sem = k
                    o.dma_val = 16 * cnt[k]
        for e in ENGS:
            for o in self.prog[e]:
                for d in o.deps:
                    if not d.is_dma:
                        d.signaled = True
        for e in ENGS:
            c = 0
            for o in self.prog[e]:
                if o.signaled and not o.is_dma:
                    c += 1
                    o.sig_idx = c
        with contextlib.ExitStack() as st:
            sems = {e: st.enter_context(nc.semaphore("s_" + e)) for e in ENGS}
            dsems = [st.enter_context(nc.semaphore("d%d" % i)) for i in range(N_DMA_SEMS)]
            block = st.enter_context(nc.Block())
            engobj = {"pe": block.tensor, "act": block.scalar, "dve": block.vector,
                      "pool": block.gpsimd, "sp": block.sync}

            def make(e):
                def body(eng):
                    waited = {}
                    for o in self.prog[e]:
                        need = {}
                        for d in o.deps:
                            if d.is_dma:
                                key = ("d", d.dma_sem)
                                val = d.dma_val
                            else:
                                key = ("c", d.eng)
                                val = d.sig_idx
                            if waited.get(key, 0) >= val:
                                continue
                            if need.get(key, 0) < val:
                                need[key] = val
                        for key, val in need.items():
                            sem = dsems[key[1]] if key[0] == "d" else sems[key[1]]
                            eng.wait_ge(sem, val)
                            waited[key] = val
                        if o.is_dma and o.dma_val > 16:
                            key = ("d", o.dma_sem)
                            if waited.get(key, 0) < o.dma_val - 16:
                                eng.wait_ge(dsems[o.dma_sem], o.dma_val - 16)
                                waited[key] = o.dma_val - 16
                        ins = o.fn(eng)
                        if o.is_dma:
                            ins.then_inc(dsems[o.dma_sem], 16)
                        elif o.signaled:
                            ins.then_inc(sems[e], 1)
                    if e == "sp":
                        need = {}
                        for d in self.final_dmas:
                            need[d.dma_sem] = max(need.get(d.dma_sem, 0), d.dma_val)
                        for k, v in need.items():
                            eng.wait_ge(dsems[k], v)
                return body

            for e in ENGS:
                engobj[e](make(e))


IN_SHAPES = {
    "x": [T, D], "norm_mix_g": [D], "w_in": [D, DIN], "lam_re": [32, 64], "lam_im": [32, 64],
    "log_dt": [1, 32], "b_re": [32, 64, 16], "b_im": [32, 64, 16], "c_re": [32, 16, 64], "c_im": [32, 16, 64],
    "d_skip": [512], "w_glu": [512, 512], "b_glu": [512], "s5_out_g": [512], "mu_shift": [1824],
    "w0": [512], "w2": [64, 512], "a0": [512], "a2": [64, 512], "g2": [160, 512], "k_k": [512], "k_a": [512],
    "r_k": [512], "ln_x_w": [512], "ln_x_b": [512], "w_out": [D, D], "norm_ffn_g": [D],
    "w_up": [D, DFF], "w_down": [DFF, D], "norm_final_g": [1, D],
}


def build(debug=(), skip=()):
    nc = bass.Bass("TRN2", target_bir_lowering=False)
    kb = KB(nc)
    I = {k: nc.dram_tensor(k, v, F32, kind="ExternalInput").ap() for k, v in IN_SHAPES.items()}
    out_d = nc.dram_tensor("out", [T, D], F32, kind="ExternalOutput").ap()
    dbg_out = {}

    def dbg(name, ap, shape):
        if name not in debug:
            return
        d = nc.dram_tensor("dbg_" + name, list(shape), ap.dtype, kind="ExternalOutput").ap()
        dbg_out[name] = d
        kb.dma(d, ap, final=True)

    ident_f = kb.sb("ident_f", [128, 128]); ident_b = kb.sb("ident_b", [128, 128], BF16)
    ones_b = kb.sb("ones_b", [128, 128], BF16); blk_b = kb.sb("blk_b", [128, 128], BF16)
    m_su = kb.sb("m_su", [128, 8, 64], BF16); m_iu = kb.sb("m_iu", [128, 8, 64], BF16)
    m_sl = kb.sb("m_sl", [128, 8, 64], BF16); m_id = kb.sb("m_id", [128, 8, 64], BF16)
    pcol = kb.sb("pcol", [128, 96])
    gF = kb.sb("gF", [128, D])
    stat = kb.sb("stat", [128, 64])
    constc = kb.sb("constc", [128, 8])
    kb.memset("pool", constc[:, 0:1], NORM_EPS)
    kb.memset("pool", constc[:, 1:2], 1.0)
    kb.memset("pool", constc[:, 2:3], GN_EPS)
    cE = constc[:, 0:1]; c1c = constc[:, 1:2]; cG = constc[:, 2:3]
    xnT = kb.sb("xnT", [128, 8, T], BF16)
    mixT = kb.sb("mixT", [128, 8, T], BF16)
    wct = [kb.sb("wct%d" % i, [128, 8, 128], BF16) for i in range(2)]
    xst = [kb.sb("xst%d" % i, [128, D]) for i in range(2)]
    AR = kb.sb_off
    AR_END = 223 * 1024
    assert AR < 108 * 1024, AR

    class Arena:
        def __init__(self, start):
            self.cur = start

        def t(self, name, shape, dt=F32):
            fe = 1
            for s_ in shape[1:]:
                fe *= s_
            t_ = kb.sb(name, shape, dt, at=self.cur)
            self.cur += (fe * ESZ[dt] + 63) // 64 * 64
            assert self.cur <= AR_END, (name, self.cur)
            return t_

    PB = [kb.ps("pb%d" % i, [128, 512]) for i in range(5)]
    PH = kb.ps("pH", [128, 512])
    PH2 = kb.ps("pH2", [128, 512])
    PHs = [PH, PH2]
    PT = kb.ps("pT", [128, 1024], BF16)
    pbi = [0]

    def bank():
        b = PB[pbi[0] % 5]
        pbi[0] += 1
        return b

    wsti = [0]

    def stage():
        b = wst[wsti[0] % 2]
        wsti[0] += 1
        return b

    arA = Arena(AR)
    junk = arA.t("junk", [128, D])
    xnb = [arA.t("xnb%d" % i, [128, D], BF16) for i in range(2)]
    xs_t = [arA.t("xs%d" % i, [128, D]) for i in range(3)]
    onesf = arA.t("onesf", [128, 512])
    tmpf = arA.t("tmpf", [128, 512])
    kb.memset("pool", onesf[:], 1.0)
    kb.op("pool", lambda e: e.affine_select(out=ident_f[:], in_=onesf[:, 0:128], pattern=[[-1, 128]], compare_op=ALU.is_equal, fill=0.0, base=0, channel_multiplier=1), outs=[ident_f[:]], ins=[onesf[:]])
    kb.copy("pool", ident_b[:], ident_f[:])
    kb.copy("pool", ones_b[:], onesf[:, 0:128])
    kb.memset("pool", blk_b[:], 0.0)
    kb.memset("pool", blk_b[0:64, 0:64], 1.0)
    kb.memset("pool", blk_b[64:128, 64:128], 1.0)
    o3 = onesf[:].rearrange("p (a b) -> p a b", a=8)
    t3 = tmpf[:].rearrange("p (a b) -> p a b", a=8)

    def mk_mask(dst, cm, base, cmp):
        for h in range(2):
            sl = slice(64 * h, 64 * h + 64)
            kb.op("pool", lambda e, sl=sl, h=h: e.affine_select(out=t3[sl], in_=o3[sl], pattern=[[0, 8], [-cm, 64]], compare_op=cmp, fill=0.0, base=base, channel_multiplier=cm),
                  outs=[t3[sl]], ins=[o3[sl]])
        kb.copy("pool", dst[:], t3)

    mk_mask(m_su, -1, -1, ALU.is_ge)
    mk_mask(m_iu, -1, 0, ALU.is_ge)
    mk_mask(m_sl, 1, -1, ALU.is_ge)
    mk_mask(m_id, 1, 0, ALU.is_equal)

    pc_i = [0]
    PC = {}

    def col(name, ap1d, n):
        c = n // 128
        c0 = pc_i[0]
        pc_i[0] += c
        kb.dma(pcol[:, c0:c0 + c], ap1d.rearrange("(c p) -> p c", p=128), slow=True)
        PC[name] = lambda j, c0=c0: pcol[:, c0 + j:c0 + j + 1]

    col("g_mix", I["norm_mix_g"], 1024)
    col("g_ffn", I["norm_ffn_g"], 1024)
    for nm in ("d_skip", "b_glu", "s5_out_g", "w0", "a0", "k_k", "k_a", "r_k", "ln_x_w", "ln_x_b"):
        col(nm, I[nm], 512)
    col("mu_rkv", I["mu_shift"][0:1536], 1536)
    col("mu_l", I["mu_shift"][1536:1792], 256)
    omu_c0 = pc_i[0]
    pc_i[0] += 12
    mu_c0 = omu_c0 - 2 - 12
    kb.ts("dve", pcol[:, omu_c0:omu_c0 + 12], pcol[:, mu_c0:mu_c0 + 12], -1.0, ALU.mult, 1.0, ALU.add)
    PC["omu_rkv"] = lambda j, c0=omu_c0: pcol[:, c0 + j:c0 + j + 1]
    mu_g2 = pcol[0:32, pc_i[0]:pc_i[0] + 1]
    kb.dma(mu_g2, I["mu_shift"][1792:1824].rearrange("(c p) -> p c", p=32), slow=True)
    pc_i[0] += 1
    assert pc_i[0] <= 96
    gmix_T = pcol[:, 0:8]
    gffn_T = pcol[:, 8:16]
    kb.dma(gF[:], I["norm_final_g"].partition_broadcast(128))

    class Collect:
        def __enter__(self):
            self.items = []
            self.om, self.ot = kb.mm, kb.tr
            kb.mm = lambda out, lhsT, rhs, start=True, stop=True, tile_position=None: self.items.append(
                dict(out=out, lhsT=lhsT, rhs=rhs, start=start, stop=stop, tile_position=tile_position))
            kb.tr = lambda out, in_, ident: self.items.append(dict(kind="tr", out=out, in_=in_, ident=ident))
            return self

        def __exit__(self, *a):
            kb.mm, kb.tr = self.om, self.ot
            if self.items:
                kb.mmg(self.items)
            return False


    def norm_T(src_fn, gT, dstT, ss_col0, junk_, xnb_):
        for tt in range(16):
            xs = src_fn(tt)
            ss = stat[:, ss_col0 + tt:ss_col0 + tt + 1]
            kb.stt(junk_[:], xs, 1.0, xs, ALU.mult, ALU.mult, accum_out=ss)
            kb.act(ss, ss, AF.Sqrt, bias=cE, scale=1.0 / D)
            kb.recip(ss, ss)
            xb = xnb_[tt % 2]
            kb.act(xb[:], xs, AF.Copy, scale=ss)
            with Collect():
                for c in range(8):
                    kb.tr(PT[:, c * 128:(c + 1) * 128], xb[:, c * 128:(c + 1) * 128], ident_b[:])
            kb.tt("dve", dstT[:, :, tt * 128:(tt + 1) * 128], PT[:].rearrange("p (c t) -> p c t", c=8),
                  gT.unsqueeze(2).to_broadcast([128, 8, 128]), ALU.mult)

    def x_src(tt):
        b = xs_t[tt % 3]
        kb.dma(b[:], I["x"][tt * 128:(tt + 1) * 128, :])
        return b[:]

    dbg('pcol', pcol[:], [128, 96]); dbg('gF', gF[:], [128, D]); dbg('msu', m_su[:].rearrange('p a b -> p (a b)'), [128, 512])
    if 'A' not in skip:
        norm_T(x_src, gmix_T, xnT, 0, junk, xnb)
    dbg('xnT', xnT[:, 0, :], [128, T])

    def load_w(dst_bf, src2d, kc, ncols, dst_f32=False):
        per = max(1, 4096 // ncols)
        k = 0
        while k < kc:
            n = min(per, kc - k)
            kb.dma(dst_bf[:, k:k + n, :], src2d[k * 128:(k + n) * 128, :].rearrange("(c p) n -> p c n", p=128), eng="pool")
            k += n

    wcti = [0]

    def proj(c0, M, evac, bank_fn=None):
        w = wct[wcti[0] % 2]
        wcti[0] += 1
        load_w(w[:, :, 0:M], I["w_in"][:, c0:c0 + M], 8, M)
        for n in range(4):
            ps = (bank_fn or bank)()
            with Collect():
                for c in range(8):
                    kb.mm(ps[0:M, :], w[:, c, 0:M], xnT[:, c, n * 512:(n + 1) * 512], start=(c == 0), stop=(c == 7))
            evac(n, ps[0:M, :])

    def s5_phase():
        ar = Arena(AR)
        CH = 8
        NCH = T // CH
        uT = ar.t("uT", [128, T])
        uTb = ar.t("uTb", [128, T], BF16)
        yacc = ar.t("yacc", [128, T])
        gb = ar.t("gb", [128, 4, T], BF16)
        sp_small = ar.t("sp_small", [128, 32, 16])
        Mre = ar.t("Mre", [128, 16, 32]); Mim = ar.t("Mim", [128, 16, 32]); Mimn = ar.t("Mimn", [128, 16, 32])
        BT0re = ar.t("BT0re", [128, 4, 128]); BT0im = ar.t("BT0im", [128, 4, 128])
        L1re = ar.t("L1re", [128, 4, 128]); L1im = ar.t("L1im", [128, 4, 128])
        CL0re = ar.t("CL0re", [128, 16, 32]); CL0im = ar.t("CL0im", [128, 16, 32])
        bm32 = ar.t("bm32", [128, 128])
        iota_c = ar.t("iota_c", [128, NCH])
        iota_ci = ar.t("iota_ci", [128, NCH], I32)
        wglu = ar.t("wglu", [128, 4, 512], BF16)
        tsets = []
        for i_ in range(2):
            tsets.append({"WBb": [ar.t("WBb_re%d" % i_, [128, CH, 128], BF16), ar.t("WBb_im%d" % i_, [128, CH, 128], BF16)],
                          "CLb": [ar.t("CLb_re%d" % i_, [128, CH + 1, 128], BF16), ar.t("CLb_in%d" % i_, [128, CH + 1, 128], BF16)],
                          "Kbd": ar.t("Kbd%d" % i_, [128, CH, 128], BF16)})
        wr = [ar.t("wr%d" % i, [128, 128]) for i in range(2)]
        wi = [ar.t("wi%d" % i, [128, 128]) for i in range(2)]
        cr = [ar.t("cr%d" % i, [128, 128]) for i in range(2)]
        ci = [ar.t("ci%d" % i, [128, 128]) for i in range(2)]
        tw = [ar.t("tw%d" % i, [128, 128]) for i in range(4)]
        tc_ = [ar.t("tc%d" % i, [128, 128]) for i in range(4)]
        Xs = [[ar.t("Xs%d_%d" % (q_, r_), [128, NCH + 1], BF16) for r_ in range(2)] for q_ in range(4)]
        ov_ = ar.cur
        csets = []
        for i_ in range(4):
            csets.append({nm: ar.t("%s_c%d" % (nm, i_), [128, NCH]) for nm in ("tcos", "tsin", "cre", "cim", "tm1", "tm2")})
        arg = Arena(ov_)
        ysb = arg.t("ysb", [128, 4, 512])
        sigt = arg.t("sigt", [128, 512])
        sqb = arg.t("sqb", [128, 512], BF16)
        rstd = arg.t("rstd", [128, 512])
        aro = Arena(ov_)
        bre = aro.t("bre", [128, 16, 16]); bim = aro.t("bim", [128, 16, 16])
        bbr = aro.t("bbr", [128, 16, 16]); bbi = aro.t("bbi", [128, 16, 16]); btmp = aro.t("btmp", [128, 16, 16])
        MLr = aro.t("MLr", [128, 16, 32]); MLi = aro.t("MLi", [128, 16, 32])
        cdr = aro.t("cdr", [128, 4, 2, 64]); cdi = aro.t("cdi", [128, 4, 2, 64])

        def sm(i):
            return sp_small[:, i, :]

        (lre, lim, ldt, dt_, th, rl, rho, f0, rn, fr, s1, sh, c1, lbr, lbi, den, inv, fre, fim, x1, x2,
         rho16, f16) = [sm(i) for i in range(23)]
        for two in range(2):
            sl = slice(64 * two, 64 * two + 64)
            kb.dma(lre[sl], I["lam_re"].rearrange("(k two) p -> two p k", two=2)[two], slow=True)
            kb.dma(lim[sl], I["lam_im"].rearrange("(k two) p -> two p k", two=2)[two], slow=True)
            kb.dma(ldt[sl], I["log_dt"].rearrange("o (k two) -> o two k", two=2)[:, two, :].partition_broadcast(64), slow=True)
            kb.dma(bre[sl], I["b_re"].rearrange("(k two) p h -> two p k h", two=2)[two])
            kb.dma(bim[sl], I["b_im"].rearrange("(k two) p h -> two p k h", two=2)[two])
        kb.act(dt_, ldt, AF.Exp)
        kb.tt("dve", th, lim, dt_, ALU.mult)
        kb.tt("dve", rl, lre, dt_, ALU.mult)
        kb.act(rho, rl, AF.Exp)
        kb.act(rho16, rl, AF.Exp, scale=float(CH))
        kb.ts("dve", f0, th, 1.0 / TWO_PI, ALU.mult)
        kb.ts("dve", f16, f0, float(CH), ALU.mult)
        kb.ts("dve", rn, f0, MAGIC, ALU.add, -MAGIC, ALU.add)
        kb.tt("dve", fr, f0, rn, ALU.subtract)
        kb.act(s1, fr, AF.Sin, scale=TWO_PI * 0.9999999)
        kb.act(sh, fr, AF.Sin, scale=math.pi * 0.9999999)
        kb.act(c1, sh, AF.Square, scale=math.sqrt(2.0))
        kb.ts("dve", c1, c1, -1.0, ALU.mult, 1.0, ALU.add)
        kb.tt("dve", lbr, rho, c1, ALU.mult)
        kb.tt("dve", lbi, rho, s1, ALU.mult)
        kb.ts("dve", x1, lbr, -1.0, ALU.add)
        kb.tt("dve", den, lre, lre, ALU.mult)
        kb.tt("dve", x2, lim, lim, ALU.mult)
        kb.tt("dve", den, den, x2, ALU.add)
        kb.recip(inv, den)
        kb.tt("dve", fre, x1, lre, ALU.mult)
        kb.tt("dve", x2, lbi, lim, ALU.mult)
        kb.tt("dve", fre, fre, x2, ALU.add)
        kb.tt("dve", fre, fre, inv, ALU.mult)
        kb.tt("dve", fim, lbi, lre, ALU.mult)
        kb.tt("dve", x2, x1, lim, ALU.mult)
        kb.tt("dve", fim, fim, x2, ALU.subtract)
        kb.tt("dve", fim, fim, inv, ALU.mult)
        freb = fre.unsqueeze(2).to_broadcast([128, 16, 16])
        fimb = fim.unsqueeze(2).to_broadcast([128, 16, 16])
        kb.tt("dve", bbr[:], bre[:], freb, ALU.mult)
        kb.tt("dve", btmp[:], bim[:], fimb, ALU.mult)
        kb.tt("dve", bbr[:], bbr[:], btmp[:], ALU.subtract)
        kb.tt("dve", bbi[:], bim[:], freb, ALU.mult)
        kb.tt("dve", btmp[:], bre[:], fimb, ALU.mult)
        kb.tt("dve", bbi[:], bbi[:], btmp[:], ALU.add)
        lbrb = lbr.unsqueeze(2).to_broadcast([128, 16, 16])
        lbib = lbi.unsqueeze(2).to_broadcast([128, 16, 16])
        for (M_, src_) in ((Mre, bbr[:]), (Mim, bbi[:]), (MLr, lbrb), (MLi, lbib)):
            kb.memset("pool", M_[:], 0.0)
            kb.copy("pool", M_[0:64, :, 0:16], src_[0:64])
            kb.copy("pool", M_[64:128, :, 16:32], src_[64:128])
        kb.ts("pool", Mimn[:], Mim[:], -1.0, ALU.mult)
        for (M_, BT) in ((Mre, BT0re), (Mim, BT0im), (MLr, L1re), (MLi, L1im)):
            for j in range(4):
                ps = bank()
                kb.tr(ps[:, 0:128], M_[:, 4 * j:4 * j + 4, :].rearrange("p a b -> p (a b)"), ident_f[:])
                kb.copy("act", BT[:, j, :], ps[:, 0:128])
        for (cd, nm) in ((cdr, "c_re"), (cdi, "c_im")):
            src = I[nm].rearrange("(j g) h p -> (g h) j p", g=8)
            kb.dma(cd[:, :, 0, :], src)
            kb.dma(cd[:, :, 1, :], src)
        for (cd, CL0) in ((cdr, CL0re), (cdi, CL0im)):
            kb.memset("pool", CL0[:], 0.0)
            for j in range(4):
                ps = bank()
                kb.tr(ps[:, 0:128], cd[:, j].rearrange("p a b -> p (a b)"), ident_f[:])
                pv = ps[:, 0:128].rearrange("p (q t h) -> p q t h", q=4, t=2)
                kb.copy("act", CL0[0:64, 4 * j:4 * j + 4, 0:16], pv[0:64, :, 0, :])
                kb.copy("act", CL0[64:128, 4 * j:4 * j + 4, 16:32], pv[64:128, :, 1, :])
        load_w(wglu, I["w_glu"], 4, 512)
        kb.op("pool", lambda e: e.iota(iota_ci[:], pattern=[[1, NCH]], base=0, channel_multiplier=0), outs=[iota_ci[:]])
        kb.copy("pool", iota_c[:], iota_ci[:])
        kb.memset("pool", tw[0][:], 1.0)
        kb.op("pool", lambda e: e.affine_select(out=tw[1][:].rearrange("p (a b) -> p a b", a=4), in_=tw[0][:].rearrange("p (a b) -> p a b", a=4), pattern=[[-32, 4], [0, 32]], compare_op=ALU.is_ge, fill=0.0, base=0, channel_multiplier=1),
              outs=[tw[1][:]], ins=[tw[0][:]])
        kb.op("pool", lambda e: e.affine_select(out=bm32[:].rearrange("p (a b) -> p a b", a=4), in_=tw[1][:].rearrange("p (a b) -> p a b", a=4), pattern=[[32, 4], [0, 32]], compare_op=ALU.is_ge, fill=0.0, base=31, channel_multiplier=-1),
              outs=[bm32[:]], ins=[tw[1][:]])
        for q_ in range(4):
            for r_ in range(2):
                kb.memset("pool", Xs[q_][r_][:, 0:1], 0.0)

        def cmul_step(a_r, a_i, b_r, b_i, o_r, o_i, t4):
            kb.tt("dve", t4[0][:], a_r, b_r, ALU.mult)
            kb.tt("pool", t4[1][:], a_i, b_i, ALU.mult)
            kb.tt("dve", t4[2][:], a_r, b_i, ALU.mult)
            kb.tt("pool", t4[3][:], a_i, b_r, ALU.mult)
            kb.tt("dve", o_r, t4[0][:], t4[1][:], ALU.subtract)
            kb.tt("pool", o_i, t4[2][:], t4[3][:], ALU.add)

        b7 = [PB[4], PH, PH2]
        b7i = [0]

        def bank7():
            b7i[0] += 1
            return b7[b7i[0] % 3]

        def pre_gen(j, S):
            WBb, CLb, Kbd = S["WBb"], S["CLb"], S["Kbd"]
            Mre_j = Mre[:, 4 * j:4 * j + 4, :].rearrange("p a b -> p (a b)")
            Mimn_j = Mimn[:, 4 * j:4 * j + 4, :].rearrange("p a b -> p (a b)")
            lamr_b = lbr[:, 4 * j:4 * j + 4].unsqueeze(2).to_broadcast([128, 4, 32])
            lami_b = lbi[:, 4 * j:4 * j + 4].unsqueeze(2).to_broadcast([128, 4, 32])
            kb.copy("pool", wr[0][:], BT0re[:, j, :])
            kb.copy("pool", wi[0][:], BT0im[:, j, :])
            kb.copy("pool", cr[0][:], CL0re[:, 4 * j:4 * j + 4, :].rearrange("p a b -> p (a b)"))
            kb.copy("pool", ci[0][:], CL0im[:, 4 * j:4 * j + 4, :].rearrange("p a b -> p (a b)"))
            c3 = lambda t_: t_[:].rearrange("p (a b) -> p a b", a=4)
            for n in range(CH + 1):
                a, b = n % 2, (n + 1) % 2
                if n < CH:
                    kb.copy("act", WBb[0][:, n, :], wr[a][:])
                    kb.copy("act", WBb[1][:, n, :], wi[a][:])
                kb.copy("act", CLb[0][:, n, :], cr[a][:])
                kb.act(CLb[1][:, n, :], ci[a][:], AF.Copy, scale=-1.0)
                if n < CH:
                    ps = bank7()
                    with Collect():
                        kb.mm(ps[:, 0:128], Mre_j, cr[a][:], start=True, stop=False)
                        kb.mm(ps[:, 0:128], Mimn_j, ci[a][:], start=False, stop=True)
                    kb.tt("dve", Kbd[:, n, :], ps[:, 0:128], bm32[:], ALU.mult)
                    if n < CH - 1:
                        cmul_step(wr[a][:], wi[a][:], L1re[:, j, :], L1im[:, j, :], wr[b][:], wi[b][:], tw)
                    kb.tt("dve", c3(tc_[0]), c3(cr[a]), lamr_b, ALU.mult)
                    kb.tt("pool", c3(tc_[1]), c3(ci[a]), lami_b, ALU.mult)
                    kb.tt("dve", c3(tc_[2]), c3(cr[a]), lami_b, ALU.mult)
                    kb.tt("pool", c3(tc_[3]), c3(ci[a]), lamr_b, ALU.mult)
                    kb.tt("dve", cr[b][:], tc_[0][:], tc_[1][:], ALU.subtract)
                    kb.tt("pool", ci[b][:], tc_[2][:], tc_[3][:], ALU.add)
                yield

        def pair_gen(j, q, S, B, u3):
            WBb = S["WBb"]
            k = 4 * j + q
            rs = slice(32 * q, 32 * q + 32)
            pr = PB[2 * (q % 2)][:, 0:NCH]
            pi = PB[2 * (q % 2) + 1][:, 0:NCH]
            with Collect():
                for i_ in range(CH):
                    kb.mm(pr, WBb[0][rs, CH - 1 - i_, :], u3[rs, :, i_], start=(i_ == 0), stop=(i_ == CH - 1))
            with Collect():
                for i_ in range(CH):
                    kb.mm(pi, WBb[1][rs, CH - 1 - i_, :], u3[rs, :, i_], start=(i_ == 0), stop=(i_ == CH - 1))
            f16k = f16[:, k:k + 1]
            tcos, tsin, cre, cim, tm1, tm2 = (B[x_] for x_ in ("tcos", "tsin", "cre", "cim", "tm1", "tm2"))
            kb.ts("pool", tm1[:], iota_c[:], f16k, ALU.mult, MAGIC, ALU.add)
            kb.ts("pool", tm1[:], tm1[:], -1.0, ALU.mult, MAGIC, ALU.add)
            kb.stt(tm1[:], iota_c[:], f16k, tm1[:], ALU.mult, ALU.add)
            yield
            kb.act(tsin[:], tm1[:], AF.Sin, scale=TWO_PI * 0.9999999)
            kb.act(tcos[:], tm1[:], AF.Sin, scale=math.pi * 0.9999999)
            kb.act(tcos[:], tcos[:], AF.Square, scale=math.sqrt(2.0))
            kb.act(tcos[:], tcos[:], AF.Identity, scale=-1.0, bias=c1c)
            yield
            kb.tt("dve", cre[:], pr, tcos[:], ALU.mult)
            kb.tt("dve", tm1[:], pi, tsin[:], ALU.mult)
            kb.tt("dve", cre[:], cre[:], tm1[:], ALU.add)
            kb.tt("dve", cim[:], pi, tcos[:], ALU.mult)
            kb.tt("dve", tm2[:], pr, tsin[:], ALU.mult)
            kb.tt("dve", cim[:], cim[:], tm2[:], ALU.subtract)
            yield
            r16 = rho16[:, k:k + 1].to_broadcast([128, NCH])
            kb.scan(cre[:], r16, cre[:], 0.0)
            kb.scan(cim[:], r16, cim[:], 0.0)
            yield
            kb.tt("dve", tm1[:], cre[:], tcos[:], ALU.mult)
            kb.tt("pool", tm2[:], cim[:], tsin[:], ALU.mult)
            kb.tt("dve", Xs[q][0][:, 1:NCH + 1], tm1[:], tm2[:], ALU.subtract)
            yield
            kb.tt("dve", tm1[:], cre[:], tsin[:], ALU.mult)
            kb.tt("pool", tm2[:], cim[:], tcos[:], ALU.mult)
            kb.tt("dve", Xs[q][1][:, 1:NCH + 1], tm1[:], tm2[:], ALU.add)
            yield

        def main_gen(j, S):
            CLb, Kbd = S["CLb"], S["Kbd"]

            upv = uTb[:, :].rearrange("p (i c) -> p c i", i=CH)

            def ev(n, ps):
                kb.copy("act", uT[:, n * 512:(n + 1) * 512], ps)
                cpc = 512 // CH
                kb.copy("act", upv[:, n * cpc:(n + 1) * cpc, :], ps.rearrange("p (c i) -> p c i", i=CH))
            proj(128 * j, 128, ev, bank_fn=bank7)
            if j == 0:
                dbg("uT", uT[:], [128, T])
            yield
            u3 = uTb[:, :].rearrange("p (i c) -> p c i", i=CH)
            for q0 in (0, 2):
                gens = [pair_gen(j, q0 + q_, S, csets[q0 + q_], u3) for q_ in range(2)]
                alive = [True] * 2
                while any(alive):
                    for gi_ in range(2):
                        if alive[gi_]:
                            try:
                                next(gens[gi_])
                            except StopIteration:
                                alive[gi_] = False
                    yield
            uf3 = uT[:, :].rearrange("p (c i) -> p c i", i=CH)
            y3 = yacc[:, :].rearrange("p (c i) -> p c i", i=CH)
            for jl in range(CH):
                acc = bank7()
                with Collect():
                    for tau in range(jl + 1):
                        kb.mm(acc[:, 0:NCH], Kbd[:, tau, :], u3[:, :, jl - tau], start=(tau == 0), stop=False)
                    for r_ in range(2):
                        for q in range(4):
                            kb.mm(acc[32 * q:32 * q + 32, 0:NCH], CLb[r_][:, jl + 1, 32 * q:32 * q + 32], Xs[q][r_][:, 0:NCH],
                                  start=False, stop=(r_ == 1), tile_position=(0, 32 * q))
                kb.stt(y3[:, :, jl], uf3[:, :, jl], PC["d_skip"](j), acc[:, 0:NCH], ALU.mult, ALU.add)
                yield
            kb.act(gb[:, j, :], yacc[:], AF.Gelu)
            yield

        pg = pre_gen(0, tsets[0])
        for _ in pg:
            pass
        for j in range(4):
            mg = main_gen(j, tsets[j % 2])
            pg = pre_gen(j + 1, tsets[(j + 1) % 2]) if j + 1 < 4 else None
            am, ap_ = True, pg is not None
            while am or ap_:
                if am:
                    try:
                        next(mg)
                    except StopIteration:
                        am = False
                if ap_:
                    try:
                        next(pg)
                    except StopIteration:
                        ap_ = False
        for n in range(4):
            ns = slice(n * 512, (n + 1) * 512)
            for jo in range(4):
                ps = bank()
                for ji in range(4):
                    kb.mm(ps[:, :], wglu[:, ji, jo * 128:(jo + 1) * 128], gb[:, ji, ns], start=(ji == 0), stop=(ji == 3))
                kb.act(sigt[:], ps[:, :], AF.Sigmoid, bias=PC["b_glu"](jo))
                kb.tt("dve", ysb[:, jo, :], gb[:, jo, ns], sigt[:], ALU.mult)
                kb.act(sqb[:], ysb[:, jo, :], AF.Square)
                kb.mm(PH[:, :], ones_b[:], sqb[:], start=(jo == 0), stop=(jo == 3))
            kb.act(rstd[:], PH[:, :], AF.Sqrt, bias=cE, scale=1.0 / 512.0)
            kb.recip(rstd[:], rstd[:])
            for jo in range(4):
                kb.stt(mixT[:, jo, ns], ysb[:, jo, :], PC["s5_out_g"](jo), rstd[:], ALU.mult, ALU.mult)
        dbg("ys5", mixT[:, 0, :], [128, T])

    def rwkv_phase():
        BL = 512
        HS = [slice(0, 64), slice(64, 128)]

        def cc8():
            with Collect():
                for cc_ in range(8):
                    yield cc_
        ar = Arena(AR)
        twza = ar.t("twza", [128, T], BF16)
        sg1b = ar.t("sg1b", [128, T], BF16)
        sg2b = ar.t("sg2b", [128, T], BF16)
        wa2b = ar.t("wa2b", [128, 512], BF16)
        g2b1 = ar.t("g2b1", [128, 512], BF16)
        g2b2 = ar.t("g2b2", [128, 512], BF16)
        resetm = ar.t("resetm", [128, BL], BF16)
        wtl = [[ar.t("w%s%d" % (nm, i), [128, 8, 128], BF16) for i in range(1)] for nm in "krv"]
        carry = ar.t("carry", [128, 4])
        S32 = ar.t("S32", [128, 64])
        Sb = [ar.t("Sb%d" % i, [128, 64], BF16) for i in range(2)]
        gst = ar.t("gst", [128, 64])
        zf = ar.t("zf", [128, 64])
        base_pipe = ar.cur
        arl = Arena(base_pipe)
        Rf = arl.t("Rf", [128, T + 1]); X4f = arl.t("X4f", [128, T]); X5f = arl.t("X5f", [128, T])
        s1o = []
        for i in range(3):
            d_ = {}
            for nm in ("At", "Bt", "Kt", "Rt", "BP", "KP", "vb"):
                d_[nm] = ar.t("s1_%s%d" % (nm, i), [128, BL], BF16)
            d_["PCc"] = ar.t("s1_PC%d" % i, [128, 8])
            d_["bonus"] = ar.t("s1_bo%d" % i, [128, BL])
            s1o.append(d_)
        s2o = []
        for i in range(2):
            d_ = {}
            for nm in ("BPt", "KPt", "Vt", "Att", "AK", "RB", "RK", "Z", "TAk", "AKV", "U0", "GT", "H0", "R2T"):
                d_[nm] = ar.t("s2_%s%d" % (nm, i), [128, 8, 64], BF16)
            s2o.append(d_)
        Rk = [ar.t("Rk%d" % i, [128, BL + 1]) for i in range(3)]
        X1 = ar.t("X1", [128, BL]); X2 = ar.t("X2", [128, BL]); X3 = ar.t("X3", [128, BL])
        X4 = ar.t("X4", [128, BL]); X5 = ar.t("X5", [128, BL]); X6 = ar.t("X6", [128, BL]); X7 = ar.t("X7", [128, BL])
        sqb = ar.t("rsqb", [128, BL], BF16)
        sqb2 = ar.t("rsqb2", [128, BL], BF16)
        Y2 = ar.t("Y2", [128, BL]); Y3 = ar.t("Y3", [128, BL]); CS = ar.t("CS", [128, BL])
        Ee = [ar.t("Ee%d" % i, [128, 512], BF16) for i in range(2)]
        Ff = [ar.t("Ff%d" % i, [128, 512], BF16) for i in range(2)]
        Zz = [ar.t("Zz%d" % i, [128, 512], BF16) for i in range(2)]
        Ubs = [ar.t("Ub%d" % i, [128, 64], BF16) for i in range(2)]
        sqf = ar.t("sqf", [128, 512]); ytm = ar.t("ytm", [128, 512]); ynb = ar.t("ynb", [128, 512], BF16)
        ygf = ar.t("ygf", [128, 512])

        bS1 = [PB[0], PB[1]]; bS2 = [PB[2], PB[3]]; bS3 = PB[4]
        ci = {"1": 0, "2": 0}

        def bk1():
            ci["1"] += 1
            return bS1[ci["1"] % 2]

        def bk2():
            ci["2"] += 1
            return bS2[ci["2"] % 2]

        kb.dma(X4f[0:64, 0:512], I["w2"])
        kb.dma(X4f[64:128, 0:512], I["a2"])
        kb.copy("pool", wa2b[:], X4f[:, 0:512])
        kb.dma(X5f[:, 0:512], I["g2"][0:128, :])
        kb.dma(X5f[0:32, 512:1024], I["g2"][128:160, :])
        kb.copy("pool", g2b1[:], X5f[:, 0:512])
        kb.copy("pool", g2b2[0:32, :], X5f[0:32, 512:1024])
        kb.memset("pool", resetm[:], 1.0)
        kb.memset("pool", resetm[:].rearrange("p (c j) -> p c j", j=64)[:, :, 0:1], 0.0)
        kb.memset("pool", zf[:], 0.0)

        def proj_shift_full(c0, M, mu_ap, dst):
            def ev(n, ps):
                kb.copy("act", Rf[0:M, 1 + n * 512:1 + (n + 1) * 512], ps)
            proj(c0, M, ev)
            kb.memset("pool", Rf[0:M, 0:1], 0.0)
            kb.tt("pool", X4f[0:M, :], Rf[0:M, 0:T], Rf[0:M, 1:T + 1], ALU.subtract)
            kb.stt(dst[0:M, :], X4f[0:M, :], mu_ap, Rf[0:M, 1:T + 1], ALU.mult, ALU.add)

        proj_shift_full(2048, 128, PC["mu_l"](0), X5f)
        kb.act(twza[0:64, :], X5f[0:64, :], AF.Tanh)
        kb.copy("act", twza[64:128, :], X5f[64:128, :])
        proj_shift_full(2176, 128, PC["mu_l"](1), X5f)
        kb.act(sg1b[:], X5f[:], AF.Sigmoid)
        proj_shift_full(2304, 32, mu_g2, X5f)
        kb.act(sg2b[0:32, :], X5f[0:32, :], AF.Sigmoid)

        def load_hp_weights(hp):
            for xi, c0 in enumerate((1024, 512, 1536)):
                load_w(wtl[xi][0], I["w_in"][:, c0 + 128 * hp:c0 + 128 * hp + 128], 8, 128)

        def S1(u):
            hp, blk = divmod(u, 4)
            o = s1o[u % 3]
            hc = slice(hp * 128, (hp + 1) * 128)
            ts_ = slice(blk * BL, (blk + 1) * BL)
            if blk == 0:
                load_hp_weights(hp)
            dsts = (X1, X3, X5)
            mi = (4 + hp, hp, 8 + hp)
            for xi in range(3):
                w = wtl[xi][0]
                R = Rk[xi]
                ps = bk1()
                with Collect():
                    for c in range(8):
                        kb.mm(ps[:, :], w[:, c, :], xnT[:, c, ts_], start=(c == 0), stop=(c == 7))
                kb.copy("act", R[:, 1:BL + 1], ps[:, :])
                kb.act(dsts[xi][:], ps[:, :], AF.Copy, scale=PC["omu_rkv"](mi[xi]))
                if blk == 0:
                    kb.memset("pool", R[:, 0:1], 0.0)
                else:
                    kb.copy("pool", R[:, 0:1], carry[:, xi:xi + 1])
                kb.copy("pool", carry[:, xi:xi + 1], R[:, BL:BL + 1])
                kb.stt(dsts[xi][:], R[:, 0:BL], PC["mu_rkv"](mi[xi]), dsts[xi][:], ALU.mult, ALU.add)
                yield
            ps = bk1()
            kb.mm(ps[:, :], wa2b[64:128, hc], twza[64:128, ts_])
            kb.act(X2[:], ps[:, :], AF.Sigmoid, bias=PC["a0"](hp))
            ps = bk1()
            kb.mm(ps[:, :], wa2b[0:64, hc], twza[0:64, ts_])
            kb.act(X6[:], ps[:, :], AF.Sigmoid, bias=PC["w0"](hp))
            yield
            kb.scan(CS[:], resetm[:], X6[:], 0.0)
            cs3 = CS[:].rearrange("p (c j) -> p c j", j=64)
            kb.act(X4[:], X1[:], AF.Copy, scale=PC["k_k"](hp))
            kb.act(sqb[:], X4[:], AF.Square)
            ps = bk1()
            kb.mm(ps[:, :], blk_b[:], sqb[:])
            kb.act(X7[:], ps[:, :], AF.Ln)
            kb.act(X7[:], X7[:], AF.Exp, scale=-0.5)
            kb.stt(X4[:], X7[:], 1e12, X4[:], ALU.min, ALU.mult)
            yield
            kb.ts("dve", Y2[:], X2[:], -1.0, ALU.add, PC["k_a"](hp), ALU.mult)
            kb.stt(X1[:], Y2[:], 1.0, X1[:], ALU.add, ALU.mult)
            kb.tt("dve", X2[:], X4[:], X2[:], ALU.mult)
            yield
            kb.tt("dve", X6[:], CS[:], X6[:], ALU.subtract)
            kb.act(X6[:], X6[:], AF.Exp, scale=-C0)
            kb.stt(o["At"][:], X4[:], -1.0, X6[:], ALU.mult, ALU.mult)
            yield
            kb.act(Y3[:], CS[:], AF.Exp, scale=C0)
            kb.tt("dve", o["Bt"][:], X2[:], Y3[:], ALU.mult)
            kb.tt("dve", o["Kt"][:], X1[:], Y3[:], ALU.mult)
            yield
            kb.tt("dve", Y2[:].rearrange("p (c j) -> p c j", j=64), cs3[:, :, 63:64].to_broadcast([128, 8, 64]), cs3, ALU.subtract)
            kb.act(Y2[:], Y2[:], AF.Exp, scale=-C0)
            kb.tt("dve", o["BP"][:], X2[:], Y2[:], ALU.mult)
            kb.tt("dve", o["KP"][:], X1[:], Y2[:], ALU.mult)
            kb.act(o["PCc"][:, :], cs3[:, :, 63], AF.Exp, scale=-C0)
            yield
            kb.act(X7[:], CS[:], AF.Exp, scale=-C0)
            kb.tt("dve", o["Rt"][:], X3[:], X7[:], ALU.mult)
            kb.tt("dve", Y3[:], X3[:], X1[:], ALU.mult)
            kb.act(sqb2[:], Y3[:], AF.Copy, scale=PC["r_k"](hp))
            kb.copy("act", o["vb"][:], X5[:])
            ps = bk1()
            kb.mm(ps[:, :], blk_b[:], sqb2[:])
            kb.tt("dve", o["bonus"][:], ps[:, :], X5[:], ALU.mult)
            yield

        def S2(u):
            o = s1o[u % 3]
            q = s2o[u % 2]
            for (src, dstt) in ((o["BP"], q["BPt"]), (o["KP"], q["KPt"]), (o["vb"], q["Vt"]), (o["At"], q["Att"])):
                for cc in cc8():
                    for h in range(2):
                        sl = HS[h]
                        kb.tr(PT[sl, cc * 64:(cc + 1) * 64], src[sl, cc * 64:(cc + 1) * 64], ident_b[sl, sl])
                kb.copy("act", dstt[:].rearrange("p c f -> p (c f)"), PT[:, 0:512])
                yield
            combos = ((o["Bt"], o["At"], m_su, None), (o["At"], o["Bt"], m_sl, None), (o["Kt"], o["At"], m_su, q["AK"]),
                      (o["Bt"], o["Rt"], m_iu, q["RB"]), (o["Kt"], o["Rt"], m_iu, q["RK"]))
            for ci_, (lh, rh, msk, dst_) in enumerate(combos):
                ps = bk2()
                for cc in cc8():
                    cs_ = slice(cc * 64, (cc + 1) * 64)
                    for h in range(2):
                        sl = HS[h]
                        kb.mm(ps[sl, cs_], lh[sl, cs_], rh[sl, cs_])
                if ci_ == 0:
                    o_ = Ee[0][:]
                elif ci_ == 1:
                    o_ = Ff[0][:]
                else:
                    o_ = dst_[:].rearrange("p c f -> p (c f)")
                kb.tt("dve", o_, ps[:, :], msk[:].rearrange("p c f -> p (c f)"), ALU.mult)
                yield
            kb.tt("dve", Zz[0][:], Ee[0][:], m_id[:].rearrange("p c f -> p (c f)"), ALU.add)
            for lvl in range(1, 6):
                ep, fp, zp = Ee[(lvl - 1) % 2], Ff[(lvl - 1) % 2], Zz[(lvl - 1) % 2]
                en, fn_, zn = Ee[lvl % 2], Ff[lvl % 2], Zz[lvl % 2]
                ps = bk2()
                for cc in cc8():
                    c6 = slice(cc * 64, (cc + 1) * 64)
                    for h in range(2):
                        sl = HS[h]
                        kb.mm(ps[sl, c6], ep[sl, c6], fp[sl, c6])
                kb.copy("act", fn_[:], ps[:, :])
                yield
                if lvl < 5:
                    ps = bk2()
                    for cc in cc8():
                        c6 = slice(cc * 64, (cc + 1) * 64)
                        for h in range(2):
                            sl = HS[h]
                            kb.mm(ps[sl, c6], fp[sl, c6], ep[sl, c6])
                    kb.copy("act", en[:], ps[:, :])
                    yield
                ps = bk2()
                for cc in cc8():
                    c6 = slice(cc * 64, (cc + 1) * 64)
                    for h in range(2):
                        sl = HS[h]
                        kb.mm(ps[sl, c6], fn_[sl, c6], zp[sl, c6])
                zo = zn[:] if lvl < 5 else q["Z"][:].rearrange("p c f -> p (c f)")
                kb.tt("dve", zo, zp[:], ps[:, :], ALU.add)
                yield
            Z2 = q["Z"][:].rearrange("p c f -> p (c f)")
            ps = bk2()
            for cc in cc8():
                c6 = slice(cc * 64, (cc + 1) * 64)
                for h in range(2):
                    sl = HS[h]
                    kb.mm(ps[sl, c6], Z2[sl, c6], q["Att"][sl, cc, :])
            kb.copy("act", q["TAk"][:].rearrange("p c f -> p (c f)"), ps[:, :])
            yield
            ps = bk2()
            for cc in cc8():
                c6 = slice(cc * 64, (cc + 1) * 64)
                for h in range(2):
                    sl = HS[h]
                    kb.mm(ps[sl, c6], q["AK"][sl, cc, :], q["Vt"][sl, cc, :])
            kb.copy("act", q["AKV"][:].rearrange("p c f -> p (c f)"), ps[:, :])
            yield
            ps = bk2()
            for cc in cc8():
                c6 = slice(cc * 64, (cc + 1) * 64)
                for h in range(2):
                    sl = HS[h]
                    kb.mm(ps[sl, c6], Z2[sl, c6], q["AKV"][sl, cc, :])
            kb.copy("act", q["U0"][:].rearrange("p c f -> p (c f)"), ps[:, :])
            yield
            ps = bk2()
            for cc in cc8():
                c6 = slice(cc * 64, (cc + 1) * 64)
                for h in range(2):
                    sl = HS[h]
                    kb.mm(ps[sl, c6], q["TAk"][sl, cc, :], q["BPt"][sl, cc, :])
            kb.copy("act", q["GT"][:].rearrange("p c f -> p (c f)"), ps[:, :])
            yield
            ps = bk2()
            for cc in cc8():
                c6 = slice(cc * 64, (cc + 1) * 64)
                for h in range(2):
                    sl = HS[h]
                    kb.mm(ps[sl, c6], q["BPt"][sl, cc, :], q["U0"][sl, cc, :], start=True, stop=False)
                    kb.mm(ps[sl, c6], q["KPt"][sl, cc, :], q["Vt"][sl, cc, :], start=False, stop=True)
            kb.copy("act", q["H0"][:].rearrange("p c f -> p (c f)"), ps[:, :])
            yield
            ps = bk2()
            for cc in cc8():
                c6 = slice(cc * 64, (cc + 1) * 64)
                for h in range(2):
                    sl = HS[h]
                    kb.mm(ps[sl, c6], q["TAk"][sl, cc, :], q["RB"][sl, cc, :])
            kb.tt("dve", q["R2T"][:].rearrange("p c f -> p (c f)"), ps[:, :], o["Rt"][:], ALU.add)
            yield

        cglob = [0]

        def S3(u):
            hp, blk = divmod(u, 4)
            o = s1o[u % 3]
            q = s2o[u % 2]
            hc = slice(hp * 128, (hp + 1) * 128)
            ts_ = slice(blk * BL, (blk + 1) * BL)
            PY = PHs[u % 2]
            if blk == 0:
                kb.memset("pool", S32[:], 0.0)
                kb.memset("pool", Sb[cglob[0] % 2][:], 0.0)
            for cc in range(8):
                c6 = slice(cc * 64, (cc + 1) * 64)
                g = cglob[0]
                cglob[0] += 1
                sb_old = Sb[g % 2]
                sb_new = Sb[(g + 1) % 2]
                pcol_ = 0
                pS = bS3[:, pcol_:pcol_ + 64]
                with Collect():
                    for h in range(2):
                        sl = HS[h]
                        kb.mm(bS3[sl, pcol_:pcol_ + 64], ident_b[sl, sl], q["H0"][sl, cc, :], start=True, stop=False)
                    for h in range(2):
                        sl = HS[h]
                        kb.mm(bS3[sl, pcol_:pcol_ + 64], q["GT"][sl, cc, :], sb_old[sl, :], start=False, stop=True)
                with Collect():
                    for st_, (A_, B_) in enumerate(((q["RB"], q["U0"]), (q["RK"], q["Vt"]), (q["R2T"], None))):
                        for h in range(2):
                            sl = HS[h]
                            rhs_ = sb_old[sl, :] if B_ is None else B_[sl, cc, :]
                            kb.mm(PY[sl, c6], A_[sl, cc, :], rhs_, start=(st_ == 0), stop=(st_ == 2))
                kb.stt(sb_new[:], S32[:], o["PCc"][:, cc:cc + 1], pS, ALU.mult, ALU.add)
                kb.stt(S32[:], S32[:], o["PCc"][:, cc:cc + 1], pS, ALU.mult, ALU.add)
                yield
            kb.copy("act", ytm[:], PY[:, :])
            for cc in range(8):
                c6 = slice(cc * 64, (cc + 1) * 64)
                kb.stt(sqf[:, c6], ytm[:, c6], 1.0, ytm[:, c6], ALU.mult, ALU.mult, accum_out=gst[:, 8 + cc:9 + cc])
                kb.stt(sqf[:, c6], ytm[:, c6], 1.0, zf[:, :], ALU.mult, ALU.add, accum_out=gst[:, cc:cc + 1])
            yield
            kb.ts("dve", gst[:, 16:24], gst[:, 0:8], 1.0 / 64.0, ALU.mult)
            kb.tt("dve", gst[:, 24:32], gst[:, 16:24], gst[:, 16:24], ALU.mult)
            kb.stt(gst[:, 32:40], gst[:, 8:16], 1.0 / 64.0, gst[:, 24:32], ALU.mult, ALU.subtract)
            kb.act(gst[:, 40:48], gst[:, 32:40], AF.Ln, bias=cG)
            kb.act(gst[:, 40:48], gst[:, 40:48], AF.Exp, scale=-0.5)
            y3 = ytm[:].rearrange("p (c f) -> p c f", f=64)
            kb.tt("dve", y3, y3, gst[:, 16:24].unsqueeze(2).to_broadcast([128, 8, 64]), ALU.subtract)
            kb.tt("dve", ynb[:].rearrange("p (c f) -> p c f", f=64), y3, gst[:, 40:48].unsqueeze(2).to_broadcast([128, 8, 64]), ALU.mult)
            yield
            for cc in cc8():
                for h in range(2):
                    sl = HS[h]
                    kb.tr(PT[sl, 512 + cc * 64:512 + (cc + 1) * 64], ynb[sl, cc * 64:(cc + 1) * 64], ident_b[sl, sl])
            kb.act(ygf[:], PT[:, 512:1024], AF.Identity, scale=PC["ln_x_w"](hp), bias=PC["ln_x_b"](hp))
            kb.tt("pool", ygf[:], ygf[:], o["bonus"][:], ALU.add)
            pg = bk2()
            kb.mm(pg[:, :], g2b1[:, hc], sg1b[:, ts_], start=True, stop=False)
            kb.mm(pg[:, :], g2b2[0:32, hc], sg2b[0:32, ts_], start=False, stop=True)
            kb.tt("dve", mixT[:, 4 + hp, ts_], ygf[:], pg[:, :], ALU.mult)
            yield

        def drain(gen, n=None):
            k = 0
            if gen is None:
                return False
            while n is None or k < n:
                try:
                    next(gen)
                except StopIteration:
                    return False
                k += 1
            return True

        NU = 16
        g1 = {u: S1(u) for u in range(NU)}
        g2 = {u: S2(u) for u in range(NU)}
        drain(g1[0]); drain(g2[0]); drain(g1[1])
        for u in range(NU):
            g3 = S3(u)
            a2 = g2.get(u + 1)
            a1 = g1.get(u + 2)
            alive3 = True
            alive2 = a2 is not None
            alive1 = a1 is not None
            while alive3 or alive2 or alive1:
                if alive3:
                    alive3 = drain(g3, 1)
                if alive2:
                    alive2 = drain(a2, 3)
                if alive1:
                    alive1 = drain(a1, 1)
        dbg("yr", mixT[:, 4, :], [128, T])

    def tail_phase():
        ar = Arena(AR)
        H = ar.t("H", [128, 16, D])
        xnb2 = [ar.t("xnb2_%d" % i, [128, D], BF16) for i in range(2)]
        actT = ar.t("actT", [128, 4, T], BF16)
        rls = [ar.t("rl%d" % i, [128, 512]) for i in range(2)]
        wo_b = ar.t("wo_b", [128, 8, D], BF16)
        wup_l = [kb.sb("wup_b0", [128, 8, 512], BF16, at=kb.off_of["wo_b"]), ar.t("wup_b1", [128, 8, 512], BF16)]
        wdn_l = [kb.sb("wdn_b0", [128, 4, D], BF16, at=kb.off_of["wo_b"] + 8192), ar.t("wdn_b1", [128, 4, D], BF16)]
        kb.pe_warm = True
        load_w(wo_b, I["w_out"], 8, D)
        for tt in range(16):
            xb_ = xst[tt % 2]
            kb.dma(xb_[:], I["x"][tt * 128:(tt + 1) * 128, :])
            for dh in range(2):
                ps = bank()
                with Collect():
                    for c in range(8):
                        kb.mm(ps[:, :], mixT[:, c, tt * 128:(tt + 1) * 128], wo_b[:, c, dh * 512:(dh + 1) * 512], start=(c == 0), stop=(c == 7))
                hs = H[:, tt, dh * 512:(dh + 1) * 512]
                kb.tt("dve", hs, ps[:, :], xb_[:, dh * 512:(dh + 1) * 512], ALU.add)
        dbg("h1", H[:, 0, :], [128, D])
        if 'hn' not in skip:
            norm_T(lambda tt: H[:, tt, :], gffn_T, xnT, 16, actT[:, 0, 0:D], xnb2)
        for g in range(0 if 'ffn' in skip else 8):
            wup_b = wup_l[g % 2]
            wdn_b = wdn_l[g % 2]
            load_w(wup_b, I["w_up"][:, g * 512:(g + 1) * 512], 8, 512)
            load_w(wdn_b, I["w_down"][g * 512:(g + 1) * 512, :], 4, D)
            for fc in range(4):
                for n in range(4):
                    ns = slice(n * 512, (n + 1) * 512)
                    ps = bank()
                    with Collect():
                        for c in range(8):
                            kb.mm(ps[:, :], wup_b[:, c, fc * 128:(fc + 1) * 128], xnT[:, c, ns], start=(c == 0), stop=(c == 7))
                    rl_ = rls[(fc * 4 + n) % 2]
                    kb.act(rl_[:], ps[:, :], AF.Relu)
                    kb.act(actT[:, fc, ns], rl_[:], AF.Square)
            for tt in range(16):
                for dh in range(2):
                    ps = bank()
                    with Collect():
                        for fc in range(4):
                            kb.mm(ps[:, :], actT[:, fc, tt * 128:(tt + 1) * 128], wdn_b[:, fc, dh * 512:(dh + 1) * 512], start=(fc == 0), stop=(fc == 3))
                    hs = H[:, tt, dh * 512:(dh + 1) * 512]
                    kb.tt("dve", hs, hs, ps[:, :], ALU.add)
        for tt in range(16):
            ss = stat[:, 32 + tt:33 + tt]
            kb.stt(actT[:, tt % 4, 0:D], H[:, tt, :], 1.0, H[:, tt, :], ALU.mult, ALU.mult, accum_out=ss)
            kb.act(ss, ss, AF.Sqrt, bias=cE, scale=1.0 / D)
            kb.recip(ss, ss)
            jb_ = xst[tt % 2]
            kb.act(jb_[:], H[:, tt, :], AF.Copy, scale=ss)
            kb.tt("dve", H[:, tt, :], jb_[:], gF[:], ALU.mult)
            kb.dma(out_d[tt * 128:(tt + 1) * 128, :], H[:, tt, :], final=True)

    if "s5" not in skip:
        kb.phase = 'S5'
        s5_phase()
    else:
        kb.memset("pool", mixT[:, 0:4, :], 0.0)
    if "rwkv" not in skip:
        kb.phase = 'RWKV'
        rwkv_phase()
    else:
        kb.memset("pool", mixT[:, 4:8, :], 0.0)
    if 'tail' not in skip:
        kb.phase = 'TAIL'
        tail_phase()
    if 'nosched' not in skip:
        est = kb.schedule()
    kb.emit()
    return nc, dbg_out


_CACHE = {}


def _prep_inputs(inputs):
    sq = {}
    for k, shp in IN_SHAPES.items():
        if k == "x":
            continue
        a = np.asarray(inputs[k], dtype=np.float32)
        sq[k] = np.ascontiguousarray(a.reshape(shp))
    return sq


def kernel(**inputs):
    x = np.asarray(inputs["x"], dtype=np.float32)
    if "nc" not in _CACHE:
        _CACHE["nc"] = build()[0]
    nc = _CACHE["nc"]
    shared = _prep_inputs(inputs)
    in_maps = []
    for b in range(8):
        m = dict(shared)
        m["x"] = np.ascontiguousarray(x[b])
        in_maps.append(m)
    res = run_bass_kernel_spmd(nc, in_maps, core_ids=list(range(8)))
    out = np.stack([np.asarray(r["out"], dtype=np.float32) for r in res.results], axis=0)
    return out.astype(inputs["x"].dtype if hasattr(inputs["x"], "dtype") else np.float32)
```

```python
import contextlib
import math
import numpy as np
import concourse.bass as bass
import concourse.mybir as mybir
from concourse.bass_utils import run_bass_kernel_spmd

F32 = mybir.dt.float32
BF16 = mybir.dt.bfloat16
I32 = mybir.dt.int32
AF = mybir.ActivationFunctionType
ALU = mybir.AluOpType
AX = mybir.AxisListType

T = 2048
D = 1024
DIN = 2336
DFF = 4096
NORM_EPS = 1e-6
GN_EPS = 64e-5
C0 = math.exp(-0.5)
TWO_PI = 2.0 * math.pi
MAGIC = 12582912.0

ENGS = ("pe", "act", "dve", "pool", "sp")
N_DMA_SEMS = 24
DMA_POOLS = {"sp": (0, 14), "pool": (14, 8), "act": (22, 2)}
STRICT_SAME_ENGINE = True
ESZ = {F32: 4, BF16: 2, I32: 4}


class Op:
    __slots__ = ("eng", "fn", "deps", "is_dma", "dma_sem", "dma_val", "signaled", "sig_idx", "all_deps", "cost", "lat", "prio", "idx", "fin", "nsucc", "succ", "phase")

    def __init__(self, eng, fn, is_dma=False):
        self.eng = eng
        self.fn = fn
        self.deps = []
        self.is_dma = is_dma
        self.dma_sem = None
        self.dma_val = None
        self.signaled = False
        self.sig_idx = None
        self.all_deps = []
        self.cost = 0.2
        self.lat = 0.0
        self.prio = 0.0
        self.idx = 0
        self.fin = 0.0
        self.succ = []


class KB:
    def __init__(self, nc):
        self.nc = nc
        self.prog = {e: [] for e in ENGS}
        self.tinfo = {}
        self.recs = {}
        self.dma_rr = 0
        self.dma_rrs = {}
        self.dma_counts = [0] * N_DMA_SEMS
        self.final_dmas = []
        self.sb_off = 16640
        self.off_of = {}
        self.n_ops = 0
        self.all_ops = []
        self.bank_last = {}
        self.phase = 'A'

    def sb(self, name, shape, dtype=F32, at=None):
        fe = 1
        for s in shape[1:]:
            fe *= s
        nbytes = fe * ESZ[dtype]
        if at is None:
            at = self.sb_off
            self.sb_off += (nbytes + 63) // 64 * 64
        t = self.nc.alloc_sbuf_tensor_at(name, list(shape), dtype, offset=at)
        self.tinfo[t.name] = ("SB", at, fe, ESZ[dtype])
        self.off_of[name] = at
        return t

    def ps(self, name, shape, dtype=F32):
        fe = 1
        for s in shape[1:]:
            fe *= s
        t = self.nc.alloc_psum_tensor(name, list(shape), dtype)
        self.tinfo[t.name] = ("PS_" + name, 0, fe, ESZ[dtype])
        return t

    def rect(self, ap):
        info = self.tinfo.get(ap.tensor.name)
        if info is None:
            return None
        space, base, fe, es = info
        off = ap.offset
        pat = ap.ap
        p0 = off // fe
        f0 = off % fe
        pc = pat[0][1]
        ext = 1
        for s, c in pat[1:]:
            ext += (c - 1) * abs(s)
        return (space, p0, p0 + pc, base + f0 * es, base + (f0 + ext) * es)

    def op(self, eng, fn, outs=(), ins=(), is_dma=False, cost=None, extra_deps=()):
        o = Op(eng, fn, is_dma)
        deps = [d_ for d_ in extra_deps if d_ is not None]
        for ap in ins:
            if ap is None or isinstance(ap, (int, float)):
                continue
            r = self.rect(ap)
            if r is None:
                continue
            for rec in self.recs.setdefault(r[0], []):
                if rec[0] < r[2] and r[1] < rec[1] and rec[2] < r[4] and r[3] < rec[3]:
                    if rec[4] is not None:
                        deps.append(rec[4])
                    rec[5].append(o)
        for ap in outs:
            r = self.rect(ap)
            if r is None:
                continue
            lst = self.recs.setdefault(r[0], [])
            keep = []
            for rec in lst:
                if rec[0] < r[2] and r[1] < rec[1] and rec[2] < r[4] and r[3] < rec[3]:
                    if rec[4] is not None:
                        deps.append(rec[4])
                    deps.extend(rec[5])
                    if r[1] <= rec[0] and rec[1] <= r[2] and r[3] <= rec[2] and rec[3] <= r[4]:
                        continue
                keep.append(rec)
            keep.append([r[1], r[2], r[3], r[4], o, []])
            self.recs[r[0]] = keep
        seen = set()
        for d in deps:
            if d is o or id(d) in seen:
                continue
            seen.add(id(d))
            o.all_deps.append(d)
            if d.eng == eng and not d.is_dma:
                if eng == "pe" or not STRICT_SAME_ENGINE:
                    continue
            o.deps.append(d)
        n_el = 1
        ref = None
        for ap in list(outs) + list(ins):
            if ap is None or isinstance(ap, (int, float)):
                continue
            ref = ap
            break
        if ref is not None:
            for d_ in ref.shape[1:]:
                n_el *= d_
        if cost is not None:
            o.cost = cost
        elif is_dma:
            o.cost = 0.06
            o.lat = 2.0 + n_el * ref.shape[0] * 4 / 150e3
        elif eng == "pe":
            o.cost = max(0.055, n_el / 1200.0) if cost is None else cost
        elif eng == "act":
            o.cost = 0.2 + n_el / 1200.0
        elif eng == "dve":
            o.cost = 0.1 + n_el / 960.0
        elif eng == "pool":
            o.cost = 0.12 + n_el / 300.0
        o.idx = self.n_ops
        o.phase = self.phase
        self.all_ops.append(o)
        self.prog[eng].append(o)
        self.n_ops += 1
        return o

    def schedule(self):
        import heapq
        ops = self.all_ops
        for o in ops:
            o.succ = []
        for o in ops:
            for d in o.all_deps:
                d.succ.append(o)
        for o in reversed(ops):
            m = 0.0
            for s_ in o.succ:
                if s_.prio > m:
                    m = s_.prio
            o.prio = o.cost + o.lat + m
        indeg = {id(o): len(o.all_deps) for o in ops}
        ready = {e: [] for e in ENGS}
        for o in ops:
            if indeg[id(o)] == 0:
                heapq.heappush(ready[o.eng], (-o.prio, o.idx, o))
        free_at = {e: 0.0 for e in ENGS}
        running = []
        relc = [0]
        XLAT = 0.08
        order = {e: [] for e in ENGS}
        now = 0.0
        done = 0
        N = len(ops)
        while done < N:
            progressed = False
            for e in ENGS:
                if free_at[e] <= now and ready[e]:
                    _, _, o = heapq.heappop(ready[e])
                    order[e].append(o)
                    free_at[e] = now + o.cost
                    o.fin = now + o.cost + o.lat
                    heapq.heappush(running, (o.fin, o.idx, o))
                    progressed = True
            if progressed:
                continue
            cands = [free_at[e] for e in ENGS if free_at[e] > now and ready[e]]
            if running:
                cands.append(running[0][0])
            assert cands, "scheduler stuck"
            now = min(cands)
            while running and running[0][0] <= now + 1e-12:
                _, _, o = heapq.heappop(running)
                if o.phase == "__rel__":
                    s_ = o.succ[0]
                    heapq.heappush(ready[s_.eng], (-s_.prio, s_.idx, s_))
                    continue
                done += 1
                for s_ in o.succ:
                    indeg[id(s_)] -= 1
                    if indeg[id(s_)] == 0:
                        if XLAT > 0 and s_.eng != o.eng:
                            r_ = Op("x", None)
                            r_.phase = "__rel__"
                            r_.succ = [s_]
                            relc[0] += 1
                            heapq.heappush(running, (now + XLAT, -relc[0], r_))
                        else:
                            heapq.heappush(ready[s_.eng], (-s_.prio, s_.idx, s_))
        self.prog = order
        self.est_makespan = now
        tot = {e: sum(o.cost for o in order[e]) for e in ENGS}
        ph = {}
        for o in ops:
            d_ = ph.setdefault(o.phase, {'t0': 1e9, 't1': 0.0, 'pe': 0.0, 'act': 0.0, 'dve': 0.0, 'pool': 0.0, 'sp': 0.0, 'n': 0})
            d_['t0'] = min(d_['t0'], o.fin - o.cost - o.lat); d_['t1'] = max(d_['t1'], o.fin); d_[o.eng] += o.cost; d_['n'] += 1
        for k_, d_ in ph.items():
            pass
        return now

    def dma(self, out, in_, eng="sp", final=False, slow=False):
        kw = {"allow_slow_non_contiguous": True} if slow else {}
        o = self.op(eng, lambda e: e.dma_start(out=out, in_=in_, **kw), outs=[out], ins=[in_], is_dma=True)
        if final:
            self.final_dmas.append(o)
        return o

    def act(self, out, in_, func, bias=0.0, scale=1.0, eng="act"):
        return self.op(eng, lambda e: e.activation(out=out, in_=in_, func=func, bias=bias, scale=scale),
                       outs=[out], ins=[in_, bias, scale])

    def tt(self, eng, out, a, b, op):
        return self.op(eng, lambda e: e.tensor_tensor(out=out, in0=a, in1=b, op=op), outs=[out], ins=[a, b])

    def ts(self, eng, out, a, s1, op0, s2=None, op1=None):
        if op1 is None:
            return self.op(eng, lambda e: e.tensor_scalar(out=out, in0=a, scalar1=s1, scalar2=None, op0=op0),
                           outs=[out], ins=[a, s1])
        return self.op(eng, lambda e: e.tensor_scalar(out=out, in0=a, scalar1=s1, scalar2=s2, op0=op0, op1=op1),
                       outs=[out], ins=[a, s1, s2])

    def stt(self, out, in0, scalar, in1, op0, op1, accum_out=None):
        if accum_out is None:
            return self.op("dve", lambda e: e.scalar_tensor_tensor(out=out, in0=in0, scalar=scalar, in1=in1, op0=op0, op1=op1),
                           outs=[out], ins=[in0, scalar, in1])
        return self.op("dve", lambda e: e.scalar_tensor_tensor(out=out, in0=in0, scalar=scalar, in1=in1, op0=op0, op1=op1, accum_out=accum_out),
                       outs=[out, accum_out], ins=[in0, scalar, in1])

    def copy(self, eng, out, in_):
        if eng == "act":
            return self.op(eng, lambda e: e.copy(out=out, in_=in_), outs=[out], ins=[in_])
        return self.op(eng, lambda e: e.tensor_copy(out=out, in_=in_), outs=[out], ins=[in_])

    def memset(self, eng, out, val):
        return self.op(eng, lambda e: e.memset(out, val), outs=[out])

    def mm(self, out, lhsT, rhs, start=True, stop=True, tile_position=None):
        if tile_position is None:
            bp = self.rect(lhsT)[1]
            if bp == 96:
                tile_position = (96, self.rect(out)[1])
        if tile_position is None:
            fn = lambda e: e.matmul(out, lhsT=lhsT, rhs=rhs, start=start, stop=stop)
        else:
            fn = lambda e: e.matmul(out, lhsT=lhsT, rhs=rhs, start=start, stop=stop, tile_position=tile_position)
        n_ = 1
        for d_ in rhs.shape[1:]:
            n_ *= d_
        c_ = max(0.055, n_ / (2300.0 if getattr(self, 'pe_warm', False) else 1200.0)) * (4.0 if rhs.dtype == F32 else 1.0)
        bkey = out.tensor.name
        xd = [self.bank_last.get(bkey)] if start else []
        o_ = self.op("pe", fn, outs=[out], ins=[lhsT, rhs] + ([] if start else [out]), cost=c_, extra_deps=xd)
        self.bank_last[bkey] = o_
        return o_

    def mmg(self, items):
        fns = []
        outs, ins, banks_start, banks_all = [], [], [], []
        cost = 0.0
        for it in items:
            out = it["out"]
            bkey = out.tensor.name
            if bkey not in banks_all:
                banks_all.append(bkey)
            if it.get("kind", "mm") == "tr":
                in_, ident = it["in_"], it["ident"]
                fns.append(lambda e, out=out, in_=in_, ident=ident: e.transpose(out=out, in_=in_, identity=ident))
                ins += [in_, ident]
                if bkey not in banks_start:
                    banks_start.append(bkey)
                n_ = in_.shape[-1]
                cost += max(0.03, n_ / 2400.0)
            else:
                lhsT, rhs = it["lhsT"], it["rhs"]
                start, stop = it.get("start", True), it.get("stop", True)
                tp = it.get("tile_position")
                if tp is None and self.rect(lhsT)[1] == 96:
                    tp = (96, self.rect(out)[1])
                if tp is None:
                    fns.append(lambda e, out=out, lhsT=lhsT, rhs=rhs, start=start, stop=stop: e.matmul(out, lhsT=lhsT, rhs=rhs, start=start, stop=stop))
                else:
                    fns.append(lambda e, out=out, lhsT=lhsT, rhs=rhs, start=start, stop=stop, tp=tp: e.matmul(out, lhsT=lhsT, rhs=rhs, start=start, stop=stop, tile_position=tp))
                ins += [lhsT, rhs]
                if start and bkey not in banks_start:
                    banks_start.append(bkey)
                n_ = 1
                for d_ in rhs.shape[1:]:
                    n_ *= d_
                cost += max(0.03, n_ / (2300.0 if getattr(self, 'pe_warm', False) else 1200.0) * (0.6 if n_ <= 128 else 1.0)) * (4.0 if rhs.dtype == F32 else 1.0)
            outs.append(out)

        def fn(e):
            r_ = None
            for f_ in fns:
                r_ = f_(e)
            return r_
        xd = [self.bank_last.get(b_) for b_ in banks_start]
        o_ = self.op("pe", fn, outs=outs, ins=ins, cost=cost, extra_deps=xd)
        for b_ in banks_all:
            self.bank_last[b_] = o_
        return o_

    def tr(self, out, in_, ident):
        bkey = out.tensor.name
        o_ = self.op("pe", lambda e: e.transpose(out=out, in_=in_, identity=ident), outs=[out], ins=[in_, ident],
                     extra_deps=[self.bank_last.get(bkey)])
        self.bank_last[bkey] = o_
        return o_

    def recip(self, out, in_):
        return self.op("dve", lambda e: e.reciprocal(out=out, in_=in_), outs=[out], ins=[in_])

    def scan(self, out, d0, d1, initial):
        return self.op("dve", lambda e: e.tensor_tensor_scan(out=out, data0=d0, data1=d1, initial=initial, op0=ALU.mult, op1=ALU.add),
                       outs=[out], ins=[d0, d1, initial])

    def reduce(self, out, in_, axis=None):
        return self.op("dve", lambda e: e.tensor_reduce(out=out, in_=in_, axis=AX.X, op=ALU.add), outs=[out], ins=[in_])

    def emit(self):
        nc = self.nc
        rr = {}
        cnt = [0] * N_DMA_SEMS
        for e in ENGS:
            for o in self.prog[e]:
                if o.is_dma:
                    lo, n = DMA_POOLS[e]
                    k = lo + rr.get(e, 0)
                    rr[e] = (rr.get(e, 0) + 1) % n
                    cnt[k] += 1
                    o.dma_sem = k
                    o.dma_val = 16 * cnt[k]
        for e in ENGS:
            for o in self.prog[e]:
                for d in o.deps:
                    if not d.is_dma:
                        d.signaled = True
        for e in ENGS:
            c = 0
            for o in self.prog[e]:
                if o.signaled and not o.is_dma:
                    c += 1
                    o.sig_idx = c
        with contextlib.ExitStack() as st:
            sems = {e: st.enter_context(nc.semaphore("s_" + e)) for e in ENGS}
            dsems = [st.enter_context(nc.semaphore("d%d" % i)) for i in range(N_DMA_SEMS)]
            block = st.enter_context(nc.Block())
            engobj = {"pe": block.tensor, "act": block.scalar, "dve": block.vector,
                      "pool": block.gpsimd, "sp": block.sync}

            def make(e):
                def body(eng):
                    waited = {}
                    for o in self.prog[e]:
                        need = {}
                        for d in o.deps:
                            if d.is_dma:
                                key = ("d", d.dma_sem)
                                val = d.dma_val
                            else:
                                key = ("c", d.eng)
                                val = d.sig_idx
                            if waited.get(key, 0) >= val:
                                continue
                            if need.get(key, 0) < val:
                                need[key] = val
                        for key, val in need.items():
                            sem = dsems[key[1]] if key[0] == "d" else sems[key[1]]
                            eng.wait_ge(sem, val)
                            waited[key] = val
                        if o.is_dma and o.dma_val > 16:
                            key = ("d", o.dma_sem)
                            if waited.get(key, 0) < o.dma_val - 16:
                                eng.wait_ge(dsems[o.dma_sem], o.dma_val - 16)
                                waited[key] = o.dma_val - 16
                        ins = o.fn(eng)
                        if o.is_dma:
                            ins.then_inc(dsems[o.dma_sem], 16)
                        elif o.signaled:
                            ins.then_inc(sems[e], 1)
                    if e == "sp":
                        need = {}
                        for d in self.final_dmas:
                            need[d.dma_sem] = max(need.get(d.dma_sem, 0), d.dma_val)
                        for k, v in need.items():
                            eng.wait_ge(dsems[k], v)
                return body

            for e in ENGS:
                engobj[e](make(e))


IN_SHAPES = {
    "x": [T, D], "norm_mix_g": [D], "w_in": [D, DIN], "lam_re": [32, 64], "lam_im": [32, 64],
    "log_dt": [1, 32], "b_re": [32, 64, 16], "b_im": [32, 64, 16], "c_re": [32, 16, 64], "c_im": [32, 16, 64],
    "d_skip": [512], "w_glu": [512, 512], "b_glu": [512], "s5_out_g": [512], "mu_shift": [1824],
    "w0": [512], "w2": [64, 512], "a0": [512], "a2": [64, 512], "g2": [160, 512], "k_k": [512], "k_a": [512],
    "r_k": [512], "ln_x_w": [512], "ln_x_b": [512], "w_out": [D, D], "norm_ffn_g": [D],
    "w_up": [D, DFF], "w_down": [DFF, D], "norm_final_g": [1, D],
}


def build(debug=(), skip=()):
    nc = bass.Bass("TRN2", target_bir_lowering=False)
    kb = KB(nc)
    I = {k: nc.dram_tensor(k, v, F32, kind="ExternalInput").ap() for k, v in IN_SHAPES.items()}
    out_d = nc.dram_tensor("out", [T, D], F32, kind="ExternalOutput").ap()
    dbg_out = {}

    def dbg(name, ap, shape):
        if name not in debug:
            return
        d = nc.dram_tensor("dbg_" + name, list(shape), ap.dtype, kind="ExternalOutput").ap()
        dbg_out[name] = d
        kb.dma(d, ap, final=True)

    ident_f = kb.sb("ident_f", [128, 128]); ident_b = kb.sb("ident_b", [128, 128], BF16)
    ones_b = kb.sb("ones_b", [128, 128], BF16); blk_b = kb.sb("blk_b", [128, 128], BF16)
    m_su = kb.sb("m_su", [128, 8, 64], BF16); m_iu = kb.sb("m_iu", [128, 8, 64], BF16)
    m_sl = kb.sb("m_sl", [128, 8, 64], BF16); m_id = kb.sb("m_id", [128, 8, 64], BF16)
    pcol = kb.sb("pcol", [128, 96])
    gF = kb.sb("gF", [128, D])
    stat = kb.sb("stat", [128, 64])
    constc = kb.sb("constc", [128, 8])
    kb.memset("pool", constc[:, 0:1], NORM_EPS)
    kb.memset("pool", constc[:, 1:2], 1.0)
    kb.memset("pool", constc[:, 2:3], GN_EPS)
    cE = constc[:, 0:1]; c1c = constc[:, 1:2]; cG = constc[:, 2:3]
    xnT = kb.sb("xnT", [128, 8, T], BF16)
    mixT = kb.sb("mixT", [128, 8, T], BF16)
    wct = [kb.sb("wct%d" % i, [128, 8, 128], BF16) for i in range(2)]
    xst = [kb.sb("xst%d" % i, [128, D]) for i in range(2)]
    AR = kb.sb_off
    AR_END = 223 * 1024
    assert AR < 108 * 1024, AR

    class Arena:
        def __init__(self, start):
            self.cur = start

        def t(self, name, shape, dt=F32):
            fe = 1
            for s_ in shape[1:]:
                fe *= s_
            t_ = kb.sb(name, shape, dt, at=self.cur)
            self.cur += (fe * ESZ[dt] + 63) // 64 * 64
            assert self.cur <= AR_END, (name, self.cur)
            return t_

    PB = [kb.ps("pb%d" % i, [128, 512]) for i in range(5)]
    PH = kb.ps("pH", [128, 512])
    PH2 = kb.ps("pH2", [128, 512])
    PHs = [PH, PH2]
    PT = kb.ps("pT", [128, 1024], BF16)
    pbi = [0]

    def bank():
        b = PB[pbi[0] % 5]
        pbi[0] += 1
        return b

    wsti = [0]

    def stage():
        b = wst[wsti[0] % 2]
        wsti[0] += 1
        return b

    arA = Arena(AR)
    junk = arA.t("junk", [128, D])
    xnb = [arA.t("xnb%d" % i, [128, D], BF16) for i in range(2)]
    xs_t = [arA.t("xs%d" % i, [128, D]) for i in range(3)]
    onesf = arA.t("onesf", [128, 512])
    tmpf = arA.t("tmpf", [128, 512])
    kb.memset("pool", onesf[:], 1.0)
    kb.op("pool", lambda e: e.affine_select(out=ident_f[:], in_=onesf[:, 0:128], pattern=[[-1, 128]], compare_op=ALU.is_equal, fill=0.0, base=0, channel_multiplier=1), outs=[ident_f[:]], ins=[onesf[:]])
    kb.copy("pool", ident_b[:], ident_f[:])
    kb.copy("pool", ones_b[:], onesf[:, 0:128])
    kb.memset("pool", blk_b[:], 0.0)
    kb.memset("pool", blk_b[0:64, 0:64], 1.0)
    kb.memset("pool", blk_b[64:128, 64:128], 1.0)
    o3 = onesf[:].rearrange("p (a b) -> p a b", a=8)
    t3 = tmpf[:].rearrange("p (a b) -> p a b", a=8)

    def mk_mask(dst, cm, base, cmp):
        for h in range(2):
            sl = slice(64 * h, 64 * h + 64)
            kb.op("pool", lambda e, sl=sl, h=h: e.affine_select(out=t3[sl], in_=o3[sl], pattern=[[0, 8], [-cm, 64]], compare_op=cmp, fill=0.0, base=base, channel_multiplier=cm),
                  outs=[t3[sl]], ins=[o3[sl]])
        kb.copy("pool", dst[:], t3)

    mk_mask(m_su, -1, -1, ALU.is_ge)
    mk_mask(m_iu, -1, 0, ALU.is_ge)
    mk_mask(m_sl, 1, -1, ALU.is_ge)
    mk_mask(m_id, 1, 0, ALU.is_equal)

    pc_i = [0]
    PC = {}

    def col(name, ap1d, n):
        c = n // 128
        c0 = pc_i[0]
        pc_i[0] += c
        kb.dma(pcol[:, c0:c0 + c], ap1d.rearrange("(c p) -> p c", p=128), slow=True)
        PC[name] = lambda j, c0=c0: pcol[:, c0 + j:c0 + j + 1]

    col("g_mix", I["norm_mix_g"], 1024)
    col("g_ffn", I["norm_ffn_g"], 1024)
    for nm in ("d_skip", "b_glu", "s5_out_g", "w0", "a0", "k_k", "k_a", "r_k", "ln_x_w", "ln_x_b"):
        col(nm, I[nm], 512)
    col("mu_rkv", I["mu_shift"][0:1536], 1536)
    col("mu_l", I["mu_shift"][1536:1792], 256)
    omu_c0 = pc_i[0]
    pc_i[0] += 12
    mu_c0 = omu_c0 - 2 - 12
    kb.ts("dve", pcol[:, omu_c0:omu_c0 + 12], pcol[:, mu_c0:mu_c0 + 12], -1.0, ALU.mult, 1.0, ALU.add)
    PC["omu_rkv"] = lambda j, c0=omu_c0: pcol[:, c0 + j:c0 + j + 1]
    mu_g2 = pcol[0:32, pc_i[0]:pc_i[0] + 1]
    kb.dma(mu_g2, I["mu_shift"][1792:1824].rearrange("(c p) -> p c", p=32), slow=True)
    pc_i[0] += 1
    assert pc_i[0] <= 96
    gmix_T = pcol[:, 0:8]
    gffn_T = pcol[:, 8:16]
    kb.dma(gF[:], I["norm_final_g"].partition_broadcast(128))

    class Collect:
        def __enter__(self):
            self.items = []
            self.om, self.ot = kb.mm, kb.tr
            kb.mm = lambda out, lhsT, rhs, start=True, stop=True, tile_position=None: self.items.append(
                dict(out=out, lhsT=lhsT, rhs=rhs, start=start, stop=stop, tile_position=tile_position))
            kb.tr = lambda out, in_, ident: self.items.append(dict(kind="tr", out=out, in_=in_, ident=ident))
            return self

        def __exit__(self, *a):
            kb.mm, kb.tr = self.om, self.ot
            if self.items:
                kb.mmg(self.items)
            return False


    def norm_T(src_fn, gT, dstT, ss_col0, junk_, xnb_):
        for tt in range(16):
            xs = src_fn(tt)
            ss = stat[:, ss_col0 + tt:ss_col0 + tt + 1]
            kb.stt(junk_[:], xs, 1.0, xs, ALU.mult, ALU.mult, accum_out=ss)
            kb.act(ss, ss, AF.Sqrt, bias=cE, scale=1.0 / D)
            kb.recip(ss, ss)
            xb = xnb_[tt % 2]
            kb.act(xb[:], xs, AF.Copy, scale=ss)
            with Collect():
                for c in range(8):
                    kb.tr(PT[:, c * 128:(c + 1) * 128], xb[:, c * 128:(c + 1) * 128], ident_b[:])
            kb.tt("dve", dstT[:, :, tt * 128:(tt + 1) * 128], PT[:].rearrange("p (c t) -> p c t", c=8),
                  gT.unsqueeze(2).to_broadcast([128, 8, 128]), ALU.mult)

    def x_src(tt):
        b = xs_t[tt % 3]
        kb.dma(b[:], I["x"][tt * 128:(tt + 1) * 128, :])
        return b[:]

    dbg('pcol', pcol[:], [128, 96]); dbg('gF', gF[:], [128, D]); dbg('msu', m_su[:].rearrange('p a b -> p (a b)'), [128, 512])
    if 'A' not in skip:
        norm_T(x_src, gmix_T, xnT, 0, junk, xnb)
    dbg('xnT', xnT[:, 0, :], [128, T])

    def load_w(dst_bf, src2d, kc, ncols, dst_f32=False):
        per = max(1, 4096 // ncols)
        k = 0
        while k < kc:
            n = min(per, kc - k)
            kb.dma(dst_bf[:, k:k + n, :], src2d[k * 128:(k + n) * 128, :].rearrange("(c p) n -> p c n", p=128), eng="pool")
            k += n

    wcti = [0]

    def proj(c0, M, evac, bank_fn=None):
        w = wct[wcti[0] % 2]
        wcti[0] += 1
        load_w(w[:, :, 0:M], I["w_in"][:, c0:c0 + M], 8, M)
        for n in range(4):
            ps = (bank_fn or bank)()
            with Collect():
                for c in range(8):
                    kb.mm(ps[0:M, :], w[:, c, 0:M], xnT[:, c, n * 512:(n + 1) * 512], start=(c == 0), stop=(c == 7))
            evac(n, ps[0:M, :])

    def s5_phase():
        ar = Arena(AR)
        CH = 8
        NCH = T // CH
        uT = ar.t("uT", [128, T])
        uTb = ar.t("uTb", [128, T], BF16)
        yacc = ar.t("yacc", [128, T])
        gb = ar.t("gb", [128, 4, T], BF16)
        sp_small = ar.t("sp_small", [128, 32, 16])
        Mre = ar.t("Mre", [128, 16, 32]); Mim = ar.t("Mim", [128, 16, 32]); Mimn = ar.t("Mimn", [128, 16, 32])
        BT0re = ar.t("BT0re", [128, 4, 128]); BT0im = ar.t("BT0im", [128, 4, 128])
        L1re = ar.t("L1re", [128, 4, 128]); L1im = ar.t("L1im", [128, 4, 128])
        CL0re = ar.t("CL0re", [128, 16, 32]); CL0im = ar.t("CL0im", [128, 16, 32])
        bm32 = ar.t("bm32", [128, 128])
        iota_c = ar.t("iota_c", [128, NCH])
        iota_ci = ar.t("iota_ci", [128, NCH], I32)
        wglu = ar.t("wglu", [128, 4, 512], BF16)
        tsets = []
        for i_ in range(2):
            tsets.append({"WBb": [ar.t("WBb_re%d" % i_, [128, CH, 128], BF16), ar.t("WBb_im%d" % i_, [128, CH, 128], BF16)],
                          "CLb": [ar.t("CLb_re%d" % i_, [128, CH + 1, 128], BF16), ar.t("CLb_in%d" % i_, [128, CH + 1, 128], BF16)],
                          "Kbd": ar.t("Kbd%d" % i_, [128, CH, 128], BF16)})
        wr = [ar.t("wr%d" % i, [128, 128]) for i in range(2)]
        wi = [ar.t("wi%d" % i, [128, 128]) for i in range(2)]
        cr = [ar.t("cr%d" % i, [128, 128]) for i in range(2)]
        ci = [ar.t("ci%d" % i, [128, 128]) for i in range(2)]
        tw = [ar.t("tw%d" % i, [128, 128]) for i in range(4)]
        tc_ = [ar.t("tc%d" % i, [128, 128]) for i in range(4)]
        Xs = [[ar.t("Xs%d_%d" % (q_, r_), [128, NCH + 1], BF16) for r_ in range(2)] for q_ in range(4)]
        ov_ = ar.cur
        csets = []
        for i_ in range(4):
            csets.append({nm: ar.t("%s_c%d" % (nm, i_), [128, NCH]) for nm in ("tcos", "tsin", "cre", "cim", "tm1", "tm2")})
        arg = Arena(ov_)
        ysb = arg.t("ysb", [128, 4, 512])
        sigt = arg.t("sigt", [128, 512])
        sqb = arg.t("sqb", [128, 512], BF16)
        rstd = arg.t("rstd", [128, 512])
        aro = Arena(ov_)
        bre = aro.t("bre", [128, 16, 16]); bim = aro.t("bim", [128, 16, 16])
        bbr = aro.t("bbr", [128, 16, 16]); bbi = aro.t("bbi", [128, 16, 16]); btmp = aro.t("btmp", [128, 16, 16])
        MLr = aro.t("MLr", [128, 16, 32]); MLi = aro.t("MLi", [128, 16, 32])
        cdr = aro.t("cdr", [128, 4, 2, 64]); cdi = aro.t("cdi", [128, 4, 2, 64])

        def sm(i):
            return sp_small[:, i, :]

        (lre, lim, ldt, dt_, th, rl, rho, f0, rn, fr, s1, sh, c1, lbr, lbi, den, inv, fre, fim, x1, x2,
         rho16, f16) = [sm(i) for i in range(23)]
        for two in range(2):
            sl = slice(64 * two, 64 * two + 64)
            kb.dma(lre[sl], I["lam_re"].rearrange("(k two) p -> two p k", two=2)[two], slow=True)
            kb.dma(lim[sl], I["lam_im"].rearrange("(k two) p -> two p k", two=2)[two], slow=True)
            kb.dma(ldt[sl], I["log_dt"].rearrange("o (k two) -> o two k", two=2)[:, two, :].partition_broadcast(64), slow=True)
            kb.dma(bre[sl], I["b_re"].rearrange("(k two) p h -> two p k h", two=2)[two])
            kb.dma(bim[sl], I["b_im"].rearrange("(k two) p h -> two p k h", two=2)[two])
        kb.act(dt_, ldt, AF.Exp)
        kb.tt("dve", th, lim, dt_, ALU.mult)
        kb.tt("dve", rl, lre, dt_, ALU.mult)
        kb.act(rho, rl, AF.Exp)
        kb.act(rho16, rl, AF.Exp, scale=float(CH))
        kb.ts("dve", f0, th, 1.0 / TWO_PI, ALU.mult)
        kb.ts("dve", f16, f0, float(CH), ALU.mult)
        kb.ts("dve", rn, f0, MAGIC, ALU.add, -MAGIC, ALU.add)
        kb.tt("dve", fr, f0, rn, ALU.subtract)
        kb.act(s1, fr, AF.Sin, scale=TWO_PI * 0.9999999)
        kb.act(sh, fr, AF.Sin, scale=math.pi * 0.9999999)
        kb.act(c1, sh, AF.Square, scale=math.sqrt(2.0))
        kb.ts("dve", c1, c1, -1.0, ALU.mult, 1.0, ALU.add)
        kb.tt("dve", lbr, rho, c1, ALU.mult)
        kb.tt("dve", lbi, rho, s1, ALU.mult)
        kb.ts("dve", x1, lbr, -1.0, ALU.add)
        kb.tt("dve", den, lre, lre, ALU.mult)
        kb.tt("dve", x2, lim, lim, ALU.mult)
        kb.tt("dve", den, den, x2, ALU.add)
        kb.recip(inv, den)
        kb.tt("dve", fre, x1, lre, ALU.mult)
        kb.tt("dve", x2, lbi, lim, ALU.mult)
        kb.tt("dve", fre, fre, x2, ALU.add)
        kb.tt("dve", fre, fre, inv, ALU.mult)
        kb.tt("dve", fim, lbi, lre, ALU.mult)
        kb.tt("dve", x2, x1, lim, ALU.mult)
        kb.tt("dve", fim, fim, x2, ALU.subtract)
        kb.tt("dve", fim, fim, inv, ALU.mult)
        freb = fre.unsqueeze(2).to_broadcast([128, 16, 16])
        fimb = fim.unsqueeze(2).to_broadcast([128, 16, 16])
        kb.tt("dve", bbr[:], bre[:], freb, ALU.mult)
        kb.tt("dve", btmp[:], bim[:], fimb, ALU.mult)
        kb.tt("dve", bbr[:], bbr[:], btmp[:], ALU.subtract)
        kb.tt("dve", bbi[:], bim[:], freb, ALU.mult)
        kb.tt("dve", btmp[:], bre[:], fimb, ALU.mult)
        kb.tt("dve", bbi[:], bbi[:], btmp[:], ALU.add)
        lbrb = lbr.unsqueeze(2).to_broadcast([128, 16, 16])
        lbib = lbi.unsqueeze(2).to_broadcast([128, 16, 16])
        for (M_, src_) in ((Mre, bbr[:]), (Mim, bbi[:]), (MLr, lbrb), (MLi, lbib)):
            kb.memset("pool", M_[:], 0.0)
            kb.copy("pool", M_[0:64, :, 0:16], src_[0:64])
            kb.copy("pool", M_[64:128, :, 16:32], src_[64:128])
        kb.ts("pool", Mimn[:], Mim[:], -1.0, ALU.mult)
        for (M_, BT) in ((Mre, BT0re), (Mim, BT0im), (MLr, L1re), (MLi, L1im)):
            for j in range(4):
                ps = bank()
                kb.tr(ps[:, 0:128], M_[:, 4 * j:4 * j + 4, :].rearrange("p a b -> p (a b)"), ident_f[:])
                kb.copy("act", BT[:, j, :], ps[:, 0:128])
        for (cd, nm) in ((cdr, "c_re"), (cdi, "c_im")):
            src = I[nm].rearrange("(j g) h p -> (g h) j p", g=8)
            kb.dma(cd[:, :, 0, :], src)
            kb.dma(cd[:, :, 1, :], src)
        for (cd, CL0) in ((cdr, CL0re), (cdi, CL0im)):
            kb.memset("pool", CL0[:], 0.0)
            for j in range(4):
                ps = bank()
                kb.tr(ps[:, 0:128], cd[:, j].rearrange("p a b -> p (a b)"), ident_f[:])
                pv = ps[:, 0:128].rearrange("p (q t h) -> p q t h", q=4, t=2)
                kb.copy("act", CL0[0:64, 4 * j:4 * j + 4, 0:16], pv[0:64, :, 0, :])
                kb.copy("act", CL0[64:128, 4 * j:4 * j + 4, 16:32], pv[64:128, :, 1, :])
        load_w(wglu, I["w_glu"], 4, 512)
        kb.op("pool", lambda e: e.iota(iota_ci[:], pattern=[[1, NCH]], base=0, channel_multiplier=0), outs=[iota_ci[:]])
        kb.copy("pool", iota_c[:], iota_ci[:])
        kb.memset("pool", tw[0][:], 1.0)
        kb.op("pool", lambda e: e.affine_select(out=tw[1][:].rearrange("p (a b) -> p a b", a=4), in_=tw[0][:].rearrange("p (a b) -> p a b", a=4), pattern=[[-32, 4], [0, 32]], compare_op=ALU.is_ge, fill=0.0, base=0, channel_multiplier=1),
              outs=[tw[1][:]], ins=[tw[0][:]])
        kb.op("pool", lambda e: e.affine_select(out=bm32[:].rearrange("p (a b) -> p a b", a=4), in_=tw[1][:].rearrange("p (a b) -> p a b", a=4), pattern=[[32, 4], [0, 32]], compare_op=ALU.is_ge, fill=0.0, base=31, channel_multiplier=-1),
              outs=[bm32[:]], ins=[tw[1][:]])
        for q_ in range(4):
            for r_ in range(2):
                kb.memset("pool", Xs[q_][r_][:, 0:1], 0.0)

        def cmul_step(a_r, a_i, b_r, b_i, o_r, o_i, t4):
            kb.tt("dve", t4[0][:], a_r, b_r, ALU.mult)
            kb.tt("pool", t4[1][:], a_i, b_i, ALU.mult)
            kb.tt("dve", t4[2][:], a_r, b_i, ALU.mult)
            kb.tt("pool", t4[3][:], a_i, b_r, ALU.mult)
            kb.tt("dve", o_r, t4[0][:], t4[1][:], ALU.subtract)
            kb.tt("pool", o_i, t4[2][:], t4[3][:], ALU.add)

        b7 = [PB[4], PH, PH2]
        b7i = [0]

        def bank7():
            b7i[0] += 1
            return b7[b7i[0] % 3]

        def pre_gen(j, S):
            WBb, CLb, Kbd = S["WBb"], S["CLb"], S["Kbd"]
            Mre_j = Mre[:, 4 * j:4 * j + 4, :].rearrange("p a b -> p (a b)")
            Mimn_j = Mimn[:, 4 * j:4 * j + 4, :].rearrange("p a b -> p (a b)")
            lamr_b = lbr[:, 4 * j:4 * j + 4].unsqueeze(2).to_broadcast([128, 4, 32])
            lami_b = lbi[:, 4 * j:4 * j + 4].unsqueeze(2).to_broadcast([128, 4, 32])
            kb.copy("pool", wr[0][:], BT0re[:, j, :])
            kb.copy("pool", wi[0][:], BT0im[:, j, :])
            kb.copy("pool", cr[0][:], CL0re[:, 4 * j:4 * j + 4, :].rearrange("p a b -> p (a b)"))
            kb.copy("pool", ci[0][:], CL0im[:, 4 * j:4 * j + 4, :].rearrange("p a b -> p (a b)"))
            c3 = lambda t_: t_[:].rearrange("p (a b) -> p a b", a=4)
            for n in range(CH + 1):
                a, b = n % 2, (n + 1) % 2
                if n < CH:
                    kb.copy("act", WBb[0][:, n, :], wr[a][:])
                    kb.copy("act", WBb[1][:, n, :], wi[a][:])
                kb.copy("act", CLb[0][:, n, :], cr[a][:])
                kb.act(CLb[1][:, n, :], ci[a][:], AF.Copy, scale=-1.0)
                if n < CH:
                    ps = bank7()
                    with Collect():
                        kb.mm(ps[:, 0:128], Mre_j, cr[a][:], start=True, stop=False)
                        kb.mm(ps[:, 0:128], Mimn_j, ci[a][:], start=False, stop=True)
                    kb.tt("dve", Kbd[:, n, :], ps[:, 0:128], bm32[:], ALU.mult)
                    if n < CH - 1:
                        cmul_step(wr[a][:], wi[a][:], L1re[:, j, :], L1im[:, j, :], wr[b][:], wi[b][:], tw)
                    kb.tt("dve", c3(tc_[0]), c3(cr[a]), lamr_b, ALU.mult)
                    kb.tt("pool", c3(tc_[1]), c3(ci[a]), lami_b, ALU.mult)
                    kb.tt("dve", c3(tc_[2]), c3(cr[a]), lami_b, ALU.mult)
                    kb.tt("pool", c3(tc_[3]), c3(ci[a]), lamr_b, ALU.mult)
                    kb.tt("dve", cr[b][:], tc_[0][:], tc_[1][:], ALU.subtract)
                    kb.tt("pool", ci[b][:], tc_[2][:], tc_[3][:], ALU.add)
                yield

        def pair_gen(j, q, S, B, u3):
            WBb = S["WBb"]
            k = 4 * j + q
            rs = slice(32 * q, 32 * q + 32)
            pr = PB[2 * (q % 2)][:, 0:NCH]
            pi = PB[2 * (q % 2) + 1][:, 0:NCH]
            with Collect():
                for i_ in range(CH):
                    kb.mm(pr, WBb[0][rs, CH - 1 - i_, :], u3[rs, :, i_], start=(i_ == 0), stop=(i_ == CH - 1))
            with Collect():
                for i_ in range(CH):
                    kb.mm(pi, WBb[1][rs, CH - 1 - i_, :], u3[rs, :, i_], start=(i_ == 0), stop=(i_ == CH - 1))
            f16k = f16[:, k:k + 1]
            tcos, tsin, cre, cim, tm1, tm2 = (B[x_] for x_ in ("tcos", "tsin", "cre", "cim", "tm1", "tm2"))
            kb.ts("pool", tm1[:], iota_c[:], f16k, ALU.mult, MAGIC, ALU.add)
            kb.ts("pool", tm1[:], tm1[:], -1.0, ALU.mult, MAGIC, ALU.add)
            kb.stt(tm1[:], iota_c[:], f16k, tm1[:], ALU.mult, ALU.add)
            yield
            kb.act(tsin[:], tm1[:], AF.Sin, scale=TWO_PI * 0.9999999)
            kb.act(tcos[:], tm1[:], AF.Sin, scale=math.pi * 0.9999999)
            kb.act(tcos[:], tcos[:], AF.Square, scale=math.sqrt(2.0))
            kb.act(tcos[:], tcos[:], AF.Identity, scale=-1.0, bias=c1c)
            yield
            kb.tt("dve", cre[:], pr, tcos[:], ALU.mult)
            kb.tt("dve", tm1[:], pi, tsin[:], ALU.mult)
            kb.tt("dve", cre[:], cre[:], tm1[:], ALU.add)
            kb.tt("dve", cim[:], pi, tcos[:], ALU.mult)
            kb.tt("dve", tm2[:], pr, tsin[:], ALU.mult)
            kb.tt("dve", cim[:], cim[:], tm2[:], ALU.subtract)
            yield
            r16 = rho16[:, k:k + 1].to_broadcast([128, NCH])
            kb.scan(cre[:], r16, cre[:], 0.0)
            kb.scan(cim[:], r16, cim[:], 0.0)
            yield
            kb.tt("dve", tm1[:], cre[:], tcos[:], ALU.mult)
            kb.tt("pool", tm2[:], cim[:], tsin[:], ALU.mult)
            kb.tt("dve", Xs[q][0][:, 1:NCH + 1], tm1[:], tm2[:], ALU.subtract)
            yield
            kb.tt("dve", tm1[:], cre[:], tsin[:], ALU.mult)
            kb.tt("pool", tm2[:], cim[:], tcos[:], ALU.mult)
            kb.tt("dve", Xs[q][1][:, 1:NCH + 1], tm1[:], tm2[:], ALU.add)
            yield

        def main_gen(j, S):
            CLb, Kbd = S["CLb"], S["Kbd"]

            upv = uTb[:, :].rearrange("p (i c) -> p c i", i=CH)

            def ev(n, ps):
                kb.copy("act", uT[:, n * 512:(n + 1) * 512], ps)
                cpc = 512 // CH
                kb.copy("act", upv[:, n * cpc:(n + 1) * cpc, :], ps.rearrange("p (c i) -> p c i", i=CH))
            proj(128 * j, 128, ev, bank_fn=bank7)
            if j == 0:
                dbg("uT", uT[:], [128, T])
            yield
            u3 = uTb[:, :].rearrange("p (i c) -> p c i", i=CH)
            for q0 in (0, 2):
                gens = [pair_gen(j, q0 + q_, S, csets[q0 + q_], u3) for q_ in range(2)]
                alive = [True] * 2
                while any(alive):
                    for gi_ in range(2):
                        if alive[gi_]:
                            try:
                                next(gens[gi_])
                            except StopIteration:
                                alive[gi_] = False
                    yield
            uf3 = uT[:, :].rearrange("p (c i) -> p c i", i=CH)
            y3 = yacc[:, :].rearrange("p (c i) -> p c i", i=CH)
            for jl in range(CH):
                acc = bank7()
                with Collect():
                    for tau in range(jl + 1):
                        kb.mm(acc[:, 0:NCH], Kbd[:, tau, :], u3[:, :, jl - tau], start=(tau == 0), stop=False)
                    for r_ in range(2):
                        for q in range(4):
                            kb.mm(acc[32 * q:32 * q + 32, 0:NCH], CLb[r_][:, jl + 1, 32 * q:32 * q + 32], Xs[q][r_][:, 0:NCH],
                                  start=False, stop=(r_ == 1), tile_position=(0, 32 * q))
                kb.stt(y3[:, :, jl], uf3[:, :, jl], PC["d_skip"](j), acc[:, 0:NCH], ALU.mult, ALU.add)
                yield
            kb.act(gb[:, j, :], yacc[:], AF.Gelu)
            yield

        pg = pre_gen(0, tsets[0])
        for _ in pg:
            pass
        for j in range(4):
            mg = main_gen(j, tsets[j % 2])
            pg = pre_gen(j + 1, tsets[(j + 1) % 2]) if j + 1 < 4 else None
            am, ap_ = True, pg is not None
            while am or ap_:
                if am:
                    try:
                        next(mg)
                    except StopIteration:
                        am = False
                if ap_:
                    try:
                        next(pg)
                    except StopIteration:
                        ap_ = False
        for n in range(4):
            ns = slice(n * 512, (n + 1) * 512)
            for jo in range(4):
                ps = bank()
                for ji in range(4):
                    kb.mm(ps[:, :], wglu[:, ji, jo * 128:(jo + 1) * 128], gb[:, ji, ns], start=(ji == 0), stop=(ji == 3))
                kb.act(sigt[:], ps[:, :], AF.Sigmoid, bias=PC["b_glu"](jo))
                kb.tt("dve", ysb[:, jo, :], gb[:, jo, ns], sigt[:], ALU.mult)
                kb.act(sqb[:], ysb[:, jo, :], AF.Square)
                kb.mm(PH[:, :], ones_b[:], sqb[:], start=(jo == 0), stop=(jo == 3))
            kb.act(rstd[:], PH[:, :], AF.Ln, bias=cE, scale=1.0 / 512.0)
            kb.act(rstd[:], rstd[:], AF.Exp, scale=-0.5)
            for jo in range(4):
                kb.stt(mixT[:, jo, ns], ysb[:, jo, :], PC["s5_out_g"](jo), rstd[:], ALU.mult, ALU.mult)
        dbg("ys5", mixT[:, 0, :], [128, T])

    def rwkv_phase():
        BL = 512
        HS = [slice(0, 64), slice(64, 128)]

        def cc8():
            with Collect():
                for cc_ in range(8):
                    yield cc_
        ar = Arena(AR)
        twza = ar.t("twza", [128, T], BF16)
        sg1b = ar.t("sg1b", [128, T], BF16)
        sg2b = ar.t("sg2b", [128, T], BF16)
        wa2b = ar.t("wa2b", [128, 512], BF16)
        g2b1 = ar.t("g2b1", [128, 512], BF16)
        g2b2 = ar.t("g2b2", [128, 512], BF16)
        resetm = ar.t("resetm", [128, BL], BF16)
        wtl = [[ar.t("w%s%d" % (nm, i), [128, 8, 128], BF16) for i in range(1)] for nm in "krv"]
        carry = ar.t("carry", [128, 4])
        S32 = ar.t("S32", [128, 64])
        Sb = [ar.t("Sb%d" % i, [128, 64], BF16) for i in range(2)]
        gst = ar.t("gst", [128, 64])
        zf = ar.t("zf", [128, 64])
        base_pipe = ar.cur
        arl = Arena(base_pipe)
        Rf = arl.t("Rf", [128, T + 1]); X4f = arl.t("X4f", [128, T]); X5f = arl.t("X5f", [128, T])
        s1o = []
        for i in range(3):
            d_ = {}
            for nm in ("At", "Bt", "Kt", "Rt", "BP", "KP", "vb"):
                d_[nm] = ar.t("s1_%s%d" % (nm, i), [128, BL], BF16)
            d_["PCc"] = ar.t("s1_PC%d" % i, [128, 8])
            d_["bonus"] = ar.t("s1_bo%d" % i, [128, BL])
            s1o.append(d_)
        s2o = []
        for i in range(2):
            d_ = {}
            for nm in ("BPt", "KPt", "Vt", "Att", "AK", "RB", "RK", "Z", "TAk", "AKV", "U0", "GT", "H0", "R2T"):
                d_[nm] = ar.t("s2_%s%d" % (nm, i), [128, 8, 64], BF16)
            s2o.append(d_)
        Rk = [ar.t("Rk%d" % i, [128, BL + 1]) for i in range(3)]
        X1 = ar.t("X1", [128, BL]); X2 = ar.t("X2", [128, BL]); X3 = ar.t("X3", [128, BL])
        X4 = ar.t("X4", [128, BL]); X5 = ar.t("X5", [128, BL]); X6 = ar.t("X6", [128, BL]); X7 = ar.t("X7", [128, BL])
        sqb = ar.t("rsqb", [128, BL], BF16)
        sqb2 = ar.t("rsqb2", [128, BL], BF16)
        Y2 = ar.t("Y2", [128, BL]); Y3 = ar.t("Y3", [128, BL]); CS = ar.t("CS", [128, BL])
        Ee = [ar.t("Ee%d" % i, [128, 512], BF16) for i in range(2)]
        Ff = [ar.t("Ff%d" % i, [128, 512], BF16) for i in range(2)]
        Zz = [ar.t("Zz%d" % i, [128, 512], BF16) for i in range(2)]
        Ubs = [ar.t("Ub%d" % i, [128, 64], BF16) for i in range(2)]
        sqf = ar.t("sqf", [128, 512]); ytm = ar.t("ytm", [128, 512]); ynb = ar.t("ynb", [128, 512], BF16)
        ygf = ar.t("ygf", [128, 512])

        bS1 = [PB[0], PB[1]]; bS2 = [PB[2], PB[3]]; bS3 = PB[4]
        ci = {"1": 0, "2": 0}

        def bk1():
            ci["1"] += 1
            return bS1[ci["1"] % 2]

        def bk2():
            ci["2"] += 1
            return bS2[ci["2"] % 2]

        kb.dma(X4f[0:64, 0:512], I["w2"])
        kb.dma(X4f[64:128, 0:512], I["a2"])
        kb.copy("pool", wa2b[:], X4f[:, 0:512])
        kb.dma(X5f[:, 0:512], I["g2"][0:128, :])
        kb.dma(X5f[0:32, 512:1024], I["g2"][128:160, :])
        kb.copy("pool", g2b1[:], X5f[:, 0:512])
        kb.copy("pool", g2b2[0:32, :], X5f[0:32, 512:1024])
        kb.memset("pool", resetm[:], 1.0)
        kb.memset("pool", resetm[:].rearrange("p (c j) -> p c j", j=64)[:, :, 0:1], 0.0)
        kb.memset("pool", zf[:], 0.0)

        def proj_shift_full(c0, M, mu_ap, dst):
            def ev(n, ps):
                kb.copy("act", Rf[0:M, 1 + n * 512:1 + (n + 1) * 512], ps)
            proj(c0, M, ev)
            kb.memset("pool", Rf[0:M, 0:1], 0.0)
            kb.tt("pool", X4f[0:M, :], Rf[0:M, 0:T], Rf[0:M, 1:T + 1], ALU.subtract)
            kb.stt(dst[0:M, :], X4f[0:M, :], mu_ap, Rf[0:M, 1:T + 1], ALU.mult, ALU.add)

        proj_shift_full(2048, 128, PC["mu_l"](0), X5f)
        kb.act(twza[0:64, :], X5f[0:64, :], AF.Tanh)
        kb.copy("act", twza[64:128, :], X5f[64:128, :])
        proj_shift_full(2176, 128, PC["mu_l"](1), X5f)
        kb.act(sg1b[:], X5f[:], AF.Sigmoid)
        proj_shift_full(2304, 32, mu_g2, X5f)
        kb.act(sg2b[0:32, :], X5f[0:32, :], AF.Sigmoid)

        def load_hp_weights(hp):
            for xi, c0 in enumerate((1024, 512, 1536)):
                load_w(wtl[xi][0], I["w_in"][:, c0 + 128 * hp:c0 + 128 * hp + 128], 8, 128)

        def S1(u):
            hp, blk = divmod(u, 4)
            o = s1o[u % 3]
            hc = slice(hp * 128, (hp + 1) * 128)
            ts_ = slice(blk * BL, (blk + 1) * BL)
            if blk == 0:
                load_hp_weights(hp)
            dsts = (X1, X3, X5)
            mi = (4 + hp, hp, 8 + hp)
            for xi in range(3):
                w = wtl[xi][0]
                R = Rk[xi]
                ps = bk1()
                with Collect():
                    for c in range(8):
                        kb.mm(ps[:, :], w[:, c, :], xnT[:, c, ts_], start=(c == 0), stop=(c == 7))
                kb.copy("act", R[:, 1:BL + 1], ps[:, :])
                kb.act(dsts[xi][:], ps[:, :], AF.Copy, scale=PC["omu_rkv"](mi[xi]))
                if blk == 0:
                    kb.memset("pool", R[:, 0:1], 0.0)
                else:
                    kb.copy("pool", R[:, 0:1], carry[:, xi:xi + 1])
                kb.copy("pool", carry[:, xi:xi + 1], R[:, BL:BL + 1])
                kb.stt(dsts[xi][:], R[:, 0:BL], PC["mu_rkv"](mi[xi]), dsts[xi][:], ALU.mult, ALU.add)
                yield
            ps = bk1()
            kb.mm(ps[:, :], wa2b[64:128, hc], twza[64:128, ts_])
            kb.act(X2[:], ps[:, :], AF.Sigmoid, bias=PC["a0"](hp))
            ps = bk1()
            kb.mm(ps[:, :], wa2b[0:64, hc], twza[0:64, ts_])
            kb.act(X6[:], ps[:, :], AF.Sigmoid, bias=PC["w0"](hp))
            yield
            kb.scan(CS[:], resetm[:], X6[:], 0.0)
            cs3 = CS[:].rearrange("p (c j) -> p c j", j=64)
            kb.act(X4[:], X1[:], AF.Copy, scale=PC["k_k"](hp))
            kb.act(sqb[:], X4[:], AF.Square)
            ps = bk1()
            kb.mm(ps[:, :], blk_b[:], sqb[:])
            kb.act(X7[:], ps[:, :], AF.Ln)
            kb.act(X7[:], X7[:], AF.Exp, scale=-0.5)
            kb.stt(X4[:], X7[:], 1e12, X4[:], ALU.min, ALU.mult)
            yield
            kb.ts("dve", Y2[:], X2[:], -1.0, ALU.add, PC["k_a"](hp), ALU.mult)
            kb.stt(X1[:], Y2[:], 1.0, X1[:], ALU.add, ALU.mult)
            kb.tt("dve", X2[:], X4[:], X2[:], ALU.mult)
            yield
            kb.tt("dve", X6[:], CS[:], X6[:], ALU.subtract)
            kb.act(X6[:], X6[:], AF.Exp, scale=-C0)
            kb.stt(o["At"][:], X4[:], -1.0, X6[:], ALU.mult, ALU.mult)
            yield
            kb.act(Y3[:], CS[:], AF.Exp, scale=C0)
            kb.tt("dve", o["Bt"][:], X2[:], Y3[:], ALU.mult)
            kb.tt("dve", o["Kt"][:], X1[:], Y3[:], ALU.mult)
            yield
            kb.tt("dve", Y2[:].rearrange("p (c j) -> p c j", j=64), cs3[:, :, 63:64].to_broadcast([128, 8, 64]), cs3, ALU.subtract)
            kb.act(Y2[:], Y2[:], AF.Exp, scale=-C0)
            kb.tt("dve", o["BP"][:], X2[:], Y2[:], ALU.mult)
            kb.tt("dve", o["KP"][:], X1[:], Y2[:], ALU.mult)
            kb.act(o["PCc"][:, :], cs3[:, :, 63], AF.Exp, scale=-C0)
            yield
            kb.act(X7[:], CS[:], AF.Exp, scale=-C0)
            kb.tt("dve", o["Rt"][:], X3[:], X7[:], ALU.mult)
            kb.tt("dve", Y3[:], X3[:], X1[:], ALU.mult)
            kb.act(sqb2[:], Y3[:], AF.Copy, scale=PC["r_k"](hp))
            kb.copy("act", o["vb"][:], X5[:])
            ps = bk1()
            kb.mm(ps[:, :], blk_b[:], sqb2[:])
            kb.tt("dve", o["bonus"][:], ps[:, :], X5[:], ALU.mult)
            yield

        def S2(u):
            o = s1o[u % 3]
            q = s2o[u % 2]
            for (src, dstt) in ((o["BP"], q["BPt"]), (o["KP"], q["KPt"]), (o["vb"], q["Vt"]), (o["At"], q["Att"])):
                for cc in cc8():
                    for h in range(2):
                        sl = HS[h]
                        kb.tr(PT[sl, cc * 64:(cc + 1) * 64], src[sl, cc * 64:(cc + 1) * 64], ident_b[sl, sl])
                kb.copy("act", dstt[:].rearrange("p c f -> p (c f)"), PT[:, 0:512])
                yield
            combos = ((o["Bt"], o["At"], m_su, None), (o["At"], o["Bt"], m_sl, None), (o["Kt"], o["At"], m_su, q["AK"]),
                      (o["Bt"], o["Rt"], m_iu, q["RB"]), (o["Kt"], o["Rt"], m_iu, q["RK"]))
            for ci_, (lh, rh, msk, dst_) in enumerate(combos):
                ps = bk2()
                for cc in cc8():
                    cs_ = slice(cc * 64, (cc + 1) * 64)
                    for h in range(2):
                        sl = HS[h]
                        kb.mm(ps[sl, cs_], lh[sl, cs_], rh[sl, cs_])
                if ci_ == 0:
                    o_ = Ee[0][:]
                elif ci_ == 1:
                    o_ = Ff[0][:]
                else:
                    o_ = dst_[:].rearrange("p c f -> p (c f)")
                kb.tt("dve", o_, ps[:, :], msk[:].rearrange("p c f -> p (c f)"), ALU.mult)
                yield
            kb.tt("dve", Zz[0][:], Ee[0][:], m_id[:].rearrange("p c f -> p (c f)"), ALU.add)
            for lvl in range(1, 6):
                ep, fp, zp = Ee[(lvl - 1) % 2], Ff[(lvl - 1) % 2], Zz[(lvl - 1) % 2]
                en, fn_, zn = Ee[lvl % 2], Ff[lvl % 2], Zz[lvl % 2]
                ps = bk2()
                for cc in cc8():
                    c6 = slice(cc * 64, (cc + 1) * 64)
                    for h in range(2):
                        sl = HS[h]
                        kb.mm(ps[sl, c6], ep[sl, c6], fp[sl, c6])
                kb.copy("act", fn_[:], ps[:, :])
                yield
                if lvl < 5:
                    ps = bk2()
                    for cc in cc8():
                        c6 = slice(cc * 64, (cc + 1) * 64)
                        for h in range(2):
                            sl = HS[h]
                            kb.mm(ps[sl, c6], fp[sl, c6], ep[sl, c6])
                    kb.copy("act", en[:], ps[:, :])
                    yield
                ps = bk2()
                for cc in cc8():
                    c6 = slice(cc * 64, (cc + 1) * 64)
                    for h in range(2):
                        sl = HS[h]
                        kb.mm(ps[sl, c6], fn_[sl, c6], zp[sl, c6])
                zo = zn[:] if lvl < 5 else q["Z"][:].rearrange("p c f -> p (c f)")
                kb.tt("dve", zo, zp[:], ps[:, :], ALU.add)
                yield
            Z2 = q["Z"][:].rearrange("p c f -> p (c f)")
            ps = bk2()
            for cc in cc8():
                c6 = slice(cc * 64, (cc + 1) * 64)
                for h in range(2):
                    sl = HS[h]
                    kb.mm(ps[sl, c6], Z2[sl, c6], q["Att"][sl, cc, :])
            kb.copy("act", q["TAk"][:].rearrange("p c f -> p (c f)"), ps[:, :])
            yield
            ps = bk2()
            for cc in cc8():
                c6 = slice(cc * 64, (cc + 1) * 64)
                for h in range(2):
                    sl = HS[h]
                    kb.mm(ps[sl, c6], q["AK"][sl, cc, :], q["Vt"][sl, cc, :])
            kb.copy("act", q["AKV"][:].rearrange("p c f -> p (c f)"), ps[:, :])
            yield
            ps = bk2()
            for cc in cc8():
                c6 = slice(cc * 64, (cc + 1) * 64)
                for h in range(2):
                    sl = HS[h]
                    kb.mm(ps[sl, c6], Z2[sl, c6], q["AKV"][sl, cc, :])
            kb.copy("act", q["U0"][:].rearrange("p c f -> p (c f)"), ps[:, :])
            yield
            ps = bk2()
            for cc in cc8():
                c6 = slice(cc * 64, (cc + 1) * 64)
                for h in range(2):
                    sl = HS[h]
                    kb.mm(ps[sl, c6], q["TAk"][sl, cc, :], q["BPt"][sl, cc, :])
            kb.copy("act", q["GT"][:].rearrange("p c f -> p (c f)"), ps[:, :])
            yield
            ps = bk2()
            for cc in cc8():
                c6 = slice(cc * 64, (cc + 1) * 64)
                for h in range(2):
                    sl = HS[h]
                    kb.mm(ps[sl, c6], q["BPt"][sl, cc, :], q["U0"][sl, cc, :], start=True, stop=False)
                    kb.mm(ps[sl, c6], q["KPt"][sl, cc, :], q["Vt"][sl, cc, :], start=False, stop=True)
            kb.copy("act", q["H0"][:].rearrange("p c f -> p (c f)"), ps[:, :])
            yield
            ps = bk2()
            for cc in cc8():
                c6 = slice(cc * 64, (cc + 1) * 64)
                for h in range(2):
                    sl = HS[h]
                    kb.mm(ps[sl, c6], q["TAk"][sl, cc, :], q["RB"][sl, cc, :])
            kb.tt("dve", q["R2T"][:].rearrange("p c f -> p (c f)"), ps[:, :], o["Rt"][:], ALU.add)
            yield

        cglob = [0]

        def S3(u):
            hp, blk = divmod(u, 4)
            o = s1o[u % 3]
            q = s2o[u % 2]
            hc = slice(hp * 128, (hp + 1) * 128)
            ts_ = slice(blk * BL, (blk + 1) * BL)
            PY = PHs[u % 2]
            if blk == 0:
                kb.memset("pool", S32[:], 0.0)
                kb.memset("pool", Sb[cglob[0] % 2][:], 0.0)
            for cc in range(8):
                c6 = slice(cc * 64, (cc + 1) * 64)
                g = cglob[0]
                cglob[0] += 1
                sb_old = Sb[g % 2]
                sb_new = Sb[(g + 1) % 2]
                pcol_ = 0
                pS = bS3[:, pcol_:pcol_ + 64]
                with Collect():
                    for h in range(2):
                        sl = HS[h]
                        kb.mm(bS3[sl, pcol_:pcol_ + 64], ident_b[sl, sl], q["H0"][sl, cc, :], start=True, stop=False)
                    for h in range(2):
                        sl = HS[h]
                        kb.mm(bS3[sl, pcol_:pcol_ + 64], q["GT"][sl, cc, :], sb_old[sl, :], start=False, stop=True)
                with Collect():
                    for st_, (A_, B_) in enumerate(((q["RB"], q["U0"]), (q["RK"], q["Vt"]), (q["R2T"], None))):
                        for h in range(2):
                            sl = HS[h]
                            rhs_ = sb_old[sl, :] if B_ is None else B_[sl, cc, :]
                            kb.mm(PY[sl, c6], A_[sl, cc, :], rhs_, start=(st_ == 0), stop=(st_ == 2))
                kb.stt(sb_new[:], S32[:], o["PCc"][:, cc:cc + 1], pS, ALU.mult, ALU.add)
                kb.stt(S32[:], S32[:], o["PCc"][:, cc:cc + 1], pS, ALU.mult, ALU.add)
                yield
            kb.copy("act", ytm[:], PY[:, :])
            for cc in range(8):
                c6 = slice(cc * 64, (cc + 1) * 64)
                kb.stt(sqf[:, c6], ytm[:, c6], 1.0, ytm[:, c6], ALU.mult, ALU.mult, accum_out=gst[:, 8 + cc:9 + cc])
                kb.stt(sqf[:, c6], ytm[:, c6], 1.0, zf[:, :], ALU.mult, ALU.add, accum_out=gst[:, cc:cc + 1])
            yield
            kb.ts("dve", gst[:, 16:24], gst[:, 0:8], 1.0 / 64.0, ALU.mult)
            kb.tt("dve", gst[:, 24:32], gst[:, 16:24], gst[:, 16:24], ALU.mult)
            kb.stt(gst[:, 32:40], gst[:, 8:16], 1.0 / 64.0, gst[:, 24:32], ALU.mult, ALU.subtract)
            kb.act(gst[:, 40:48], gst[:, 32:40], AF.Ln, bias=cG)
            kb.act(gst[:, 40:48], gst[:, 40:48], AF.Exp, scale=-0.5)
            y3 = ytm[:].rearrange("p (c f) -> p c f", f=64)
            kb.tt("dve", y3, y3, gst[:, 16:24].unsqueeze(2).to_broadcast([128, 8, 64]), ALU.subtract)
            kb.tt("dve", ynb[:].rearrange("p (c f) -> p c f", f=64), y3, gst[:, 40:48].unsqueeze(2).to_broadcast([128, 8, 64]), ALU.mult)
            yield
            for cc in cc8():
                for h in range(2):
                    sl = HS[h]
                    kb.tr(PT[sl, 512 + cc * 64:512 + (cc + 1) * 64], ynb[sl, cc * 64:(cc + 1) * 64], ident_b[sl, sl])
            kb.act(ygf[:], PT[:, 512:1024], AF.Identity, scale=PC["ln_x_w"](hp), bias=PC["ln_x_b"](hp))
            kb.tt("pool", ygf[:], ygf[:], o["bonus"][:], ALU.add)
            pg = bk2()
            kb.mm(pg[:, :], g2b1[:, hc], sg1b[:, ts_], start=True, stop=False)
            kb.mm(pg[:, :], g2b2[0:32, hc], sg2b[0:32, ts_], start=False, stop=True)
            kb.tt("dve", mixT[:, 4 + hp, ts_], ygf[:], pg[:, :], ALU.mult)
            yield

        def drain(gen, n=None):
            k = 0
            if gen is None:
                return False
            while n is None or k < n:
                try:
                    next(gen)
                except StopIteration:
                    return False
                k += 1
            return True

        NU = 16
        g1 = {u: S1(u) for u in range(NU)}
        g2 = {u: S2(u) for u in range(NU)}
        drain(g1[0]); drain(g2[0]); drain(g1[1])
        for u in range(NU):
            g3 = S3(u)
            a2 = g2.get(u + 1)
            a1 = g1.get(u + 2)
            alive3 = True
            alive2 = a2 is not None
            alive1 = a1 is not None
            while alive3 or alive2 or alive1:
                if alive3:
                    alive3 = drain(g3, 1)
                if alive2:
                    alive2 = drain(a2, 3)
                if alive1:
                    alive1 = drain(a1, 1)
        dbg("yr", mixT[:, 4, :], [128, T])

    def tail_phase():
        ar = Arena(AR)
        H = ar.t("H", [128, 16, D])
        xnb2 = [ar.t("xnb2_%d" % i, [128, D], BF16) for i in range(2)]
        actT = ar.t("actT", [128, 4, T], BF16)
        rls = [ar.t("rl%d" % i, [128, 512]) for i in range(2)]
        wo_b = ar.t("wo_b", [128, 8, D], BF16)
        wup_l = [kb.sb("wup_b0", [128, 8, 512], BF16, at=kb.off_of["wo_b"]), ar.t("wup_b1", [128, 8, 512], BF16)]
        wdn_l = [kb.sb("wdn_b0", [128, 4, D], BF16, at=kb.off_of["wo_b"] + 8192), ar.t("wdn_b1", [128, 4, D], BF16)]
        kb.pe_warm = True
        load_w(wo_b, I["w_out"], 8, D)
        for tt in range(16):
            xb_ = xst[tt % 2]
            kb.dma(xb_[:], I["x"][tt * 128:(tt + 1) * 128, :])
            for dh in range(2):
                ps = bank()
                with Collect():
                    for c in range(8):
                        kb.mm(ps[:, :], mixT[:, c, tt * 128:(tt + 1) * 128], wo_b[:, c, dh * 512:(dh + 1) * 512], start=(c == 0), stop=(c == 7))
                hs = H[:, tt, dh * 512:(dh + 1) * 512]
                kb.tt("dve", hs, ps[:, :], xb_[:, dh * 512:(dh + 1) * 512], ALU.add)
        dbg("h1", H[:, 0, :], [128, D])
        if 'hn' not in skip:
            norm_T(lambda tt: H[:, tt, :], gffn_T, xnT, 16, actT[:, 0, 0:D], xnb2)
        for g in range(0 if 'ffn' in skip else 8):
            wup_b = wup_l[g % 2]
            wdn_b = wdn_l[g % 2]
            load_w(wup_b, I["w_up"][:, g * 512:(g + 1) * 512], 8, 512)
            load_w(wdn_b, I["w_down"][g * 512:(g + 1) * 512, :], 4, D)
            for fc in range(4):
                for n in range(4):
                    ns = slice(n * 512, (n + 1) * 512)
                    ps = bank()
                    with Collect():
                        for c in range(8):
                            kb.mm(ps[:, :], wup_b[:, c, fc * 128:(fc + 1) * 128], xnT[:, c, ns], start=(c == 0), stop=(c == 7))
                    rl_ = rls[(fc * 4 + n) % 2]
                    kb.act(rl_[:], ps[:, :], AF.Relu)
                    kb.act(actT[:, fc, ns], rl_[:], AF.Square)
            for tt in range(16):
                for dh in range(2):
                    ps = bank()
                    with Collect():
                        for fc in range(4):
                            kb.mm(ps[:, :], actT[:, fc, tt * 128:(tt + 1) * 128], wdn_b[:, fc, dh * 512:(dh + 1) * 512], start=(fc == 0), stop=(fc == 3))
                    hs = H[:, tt, dh * 512:(dh + 1) * 512]
                    kb.tt("dve", hs, hs, ps[:, :], ALU.add)
        for tt in range(16):
            ss = stat[:, 32 + tt:33 + tt]
            kb.stt(actT[:, tt % 4, 0:D], H[:, tt, :], 1.0, H[:, tt, :], ALU.mult, ALU.mult, accum_out=ss)
            kb.act(ss, ss, AF.Sqrt, bias=cE, scale=1.0 / D)
            kb.recip(ss, ss)
            jb_ = xst[tt % 2]
            kb.act(jb_[:], H[:, tt, :], AF.Copy, scale=ss)
            kb.tt("dve", H[:, tt, :], jb_[:], gF[:], ALU.mult)
            kb.dma(out_d[tt * 128:(tt + 1) * 128, :], H[:, tt, :], final=True)

    if "s5" not in skip:
        kb.phase = 'S5'
        s5_phase()
    else:
        kb.memset("pool", mixT[:, 0:4, :], 0.0)
    if "rwkv" not in skip:
        kb.phase = 'RWKV'
        rwkv_phase()
    else:
        kb.memset("pool", mixT[:, 4:8, :], 0.0)
    if 'tail' not in skip:
        kb.phase = 'TAIL'
        tail_phase()
    if 'nosched' not in skip:
        est = kb.schedule()
    kb.emit()
    return nc, dbg_out


_CACHE = {}


def _prep_inputs(inputs):
    sq = {}
    for k, shp in IN_SHAPES.items():
        if k == "x":
            continue
        a = np.asarray(inputs[k], dtype=np.float32)
        sq[k] = np.ascontiguousarray(a.reshape(shp))
    return sq


def kernel(**inputs):
    x = np.asarray(inputs["x"], dtype=np.float32)
    if "nc" not in _CACHE:
        _CACHE["nc"] = build()[0]
    nc = _CACHE["nc"]
    shared = _prep_inputs(inputs)
    in_maps = []
    for b in range(8):
        m = dict(shared)
        m["x"] = np.ascontiguousarray(x[b])
        in_maps.append(m)
    res = run_bass_kernel_spmd(nc, in_maps, core_ids=list(range(8)))
    out = np.stack([np.asarray(r["out"], dtype=np.float32) for r in res.results], axis=0)
    return out.astype(inputs["x"].dtype if hasattr(inputs["x"], "dtype") else np.float32)
```

```python
import contextlib
import math
import numpy as np
import concourse.bass as bass
import concourse.mybir as mybir
from concourse.bass_utils import run_bass_kernel_spmd

F32 = mybir.dt.float32
BF16 = mybir.dt.bfloat16
I32 = mybir.dt.int32
AF = mybir.ActivationFunctionType
ALU = mybir.AluOpType
AX = mybir.AxisListType

T = 2048
D = 1024
DIN = 2336
DFF = 4096
NORM_EPS = 1e-6
GN_EPS = 64e-5
C0 = math.exp(-0.5)
TWO_PI = 2.0 * math.pi
MAGIC = 12582912.0

ENGS = ("pe", "act", "dve", "pool", "sp")
N_DMA_SEMS = 24
DMA_POOLS = {"sp": (0, 14), "pool": (14, 8), "act": (22, 2)}
STRICT_SAME_ENGINE = True
ESZ = {F32: 4, BF16: 2, I32: 4}


class Op:
    __slots__ = ("eng", "fn", "deps", "is_dma", "dma_sem", "dma_val", "signaled", "sig_idx", "all_deps", "cost", "lat", "prio", "idx", "fin", "nsucc", "succ", "phase")

    def __init__(self, eng, fn, is_dma=False):
        self.eng = eng
        self.fn = fn
        self.deps = []
        self.is_dma = is_dma
        self.dma_sem = None
        self.dma_val = None
        self.signaled = False
        self.sig_idx = None
        self.all_deps = []
        self.cost = 0.2
        self.lat = 0.0
        self.prio = 0.0
        self.idx = 0
        self.fin = 0.0
        self.succ = []


class KB:
    def __init__(self, nc):
        self.nc = nc
        self.prog = {e: [] for e in ENGS}
        self.tinfo = {}
        self.recs = {}
        self.dma_rr = 0
        self.dma_rrs = {}
        self.dma_counts = [0] * N_DMA_SEMS
        self.final_dmas = []
        self.sb_off = 16640
        self.off_of = {}
        self.n_ops = 0
        self.all_ops = []
        self.bank_last = {}
        self.phase = 'A'

    def sb(self, name, shape, dtype=F32, at=None):
        fe = 1
        for s in shape[1:]:
            fe *= s
        nbytes = fe * ESZ[dtype]
        if at is None:
            at = self.sb_off
            self.sb_off += (nbytes + 63) // 64 * 64
        t = self.nc.alloc_sbuf_tensor_at(name, list(shape), dtype, offset=at)
        self.tinfo[t.name] = ("SB", at, fe, ESZ[dtype])
        self.off_of[name] = at
        return t

    def ps(self, name, shape, dtype=F32):
        fe = 1
        for s in shape[1:]:
            fe *= s
        t = self.nc.alloc_psum_tensor(name, list(shape), dtype)
        self.tinfo[t.name] = ("PS_" + name, 0, fe, ESZ[dtype])
        return t

    def rect(self, ap):
        info = self.tinfo.get(ap.tensor.name)
        if info is None:
            return None
        space, base, fe, es = info
        off = ap.offset
        pat = ap.ap
        p0 = off // fe
        f0 = off % fe
        pc = pat[0][1]
        ext = 1
        for s, c in pat[1:]:
            ext += (c - 1) * abs(s)
        return (space, p0, p0 + pc, base + f0 * es, base + (f0 + ext) * es)

    def op(self, eng, fn, outs=(), ins=(), is_dma=False, cost=None, extra_deps=()):
        o = Op(eng, fn, is_dma)
        deps = [d_ for d_ in extra_deps if d_ is not None]
        for ap in ins:
            if ap is None or isinstance(ap, (int, float)):
                continue
            r = self.rect(ap)
            if r is None:
                continue
            for rec in self.recs.setdefault(r[0], []):
                if rec[0] < r[2] and r[1] < rec[1] and rec[2] < r[4] and r[3] < rec[3]:
                    if rec[4] is not None:
                        deps.append(rec[4])
                    rec[5].append(o)
        for ap in outs:
            r = self.rect(ap)
            if r is None:
                continue
            lst = self.recs.setdefault(r[0], [])
            keep = []
            for rec in lst:
                if rec[0] < r[2] and r[1] < rec[1] and rec[2] < r[4] and r[3] < rec[3]:
                    if rec[4] is not None:
                        deps.append(rec[4])
                    deps.extend(rec[5])
                    if r[1] <= rec[0] and rec[1] <= r[2] and r[3] <= rec[2] and rec[3] <= r[4]:
                        continue
                keep.append(rec)
            keep.append([r[1], r[2], r[3], r[4], o, []])
            self.recs[r[0]] = keep
        seen = set()
        for d in deps:
            if d is o or id(d) in seen:
                continue
            seen.add(id(d))
            o.all_deps.append(d)
            if d.eng == eng and not d.is_dma:
                if eng == "pe" or not STRICT_SAME_ENGINE:
                    continue
            o.deps.append(d)
        n_el = 1
        ref = None
        for ap in list(outs) + list(ins):
            if ap is None or isinstance(ap, (int, float)):
                continue
            ref = ap
            break
        if ref is not None:
            for d_ in ref.shape[1:]:
                n_el *= d_
        if cost is not None:
            o.cost = cost
        elif is_dma:
            o.cost = 0.06
            o.lat = 2.0 + n_el * ref.shape[0] * 4 / 150e3
        elif eng == "pe":
            o.cost = max(0.055, n_el / 1200.0) if cost is None else cost
        elif eng == "act":
            o.cost = 0.2 + n_el / 1200.0
        elif eng == "dve":
            o.cost = 0.1 + n_el / 960.0
        elif eng == "pool":
            o.cost = 0.12 + n_el / 300.0
        o.idx = self.n_ops
        o.phase = self.phase
        self.all_ops.append(o)
        self.prog[eng].append(o)
        self.n_ops += 1
        return o

    def schedule(self):
        import heapq
        ops = self.all_ops
        for o in ops:
            o.succ = []
        for o in ops:
            for d in o.all_deps:
                d.succ.append(o)
        for o in reversed(ops):
            m = 0.0
            for s_ in o.succ:
                if s_.prio > m:
                    m = s_.prio
            o.prio = o.cost + o.lat + m
        indeg = {id(o): len(o.all_deps) for o in ops}
        ready = {e: [] for e in ENGS}
        for o in ops:
            if indeg[id(o)] == 0:
                heapq.heappush(ready[o.eng], (-o.prio, o.idx, o))
        free_at = {e: 0.0 for e in ENGS}
        running = []
        relc = [0]
        XLAT = 0.08
        order = {e: [] for e in ENGS}
        now = 0.0
        done = 0
        N = len(ops)
        while done < N:
            progressed = False
            for e in ENGS:
                if free_at[e] <= now and ready[e]:
                    _, _, o = heapq.heappop(ready[e])
                    order[e].append(o)
                    free_at[e] = now + o.cost
                    o.fin = now + o.cost + o.lat
                    heapq.heappush(running, (o.fin, o.idx, o))
                    progressed = True
            if progressed:
                continue
            cands = [free_at[e] for e in ENGS if free_at[e] > now and ready[e]]
            if running:
                cands.append(running[0][0])
            assert cands, "scheduler stuck"
            now = min(cands)
            while running and running[0][0] <= now + 1e-12:
                _, _, o = heapq.heappop(running)
                if o.phase == "__rel__":
                    s_ = o.succ[0]
                    heapq.heappush(ready[s_.eng], (-s_.prio, s_.idx, s_))
                    continue
                done += 1
                for s_ in o.succ:
                    indeg[id(s_)] -= 1
                    if indeg[id(s_)] == 0:
                        if XLAT > 0 and s_.eng != o.eng:
                            r_ = Op("x", None)
                            r_.phase = "__rel__"
                            r_.succ = [s_]
                            relc[0] += 1
                            heapq.heappush(running, (now + XLAT, -relc[0], r_))
                        else:
                            heapq.heappush(ready[s_.eng], (-s_.prio, s_.idx, s_))
        self.prog = order
        self.est_makespan = now
        tot = {e: sum(o.cost for o in order[e]) for e in ENGS}
        ph = {}
        for o in ops:
            d_ = ph.setdefault(o.phase, {'t0': 1e9, 't1': 0.0, 'pe': 0.0, 'act': 0.0, 'dve': 0.0, 'pool': 0.0, 'sp': 0.0, 'n': 0})
            d_['t0'] = min(d_['t0'], o.fin - o.cost - o.lat); d_['t1'] = max(d_['t1'], o.fin); d_[o.eng] += o.cost; d_['n'] += 1
        for k_, d_ in ph.items():
            pass
        return now

    def dma(self, out, in_, eng="sp", final=False, slow=False):
        kw = {"allow_slow_non_contiguous": True} if slow else {}
        o = self.op(eng, lambda e: e.dma_start(out=out, in_=in_, **kw), outs=[out], ins=[in_], is_dma=True)
        if final:
            self.final_dmas.append(o)
        return o

    def act(self, out, in_, func, bias=0.0, scale=1.0, eng="act"):
        return self.op(eng, lambda e: e.activation(out=out, in_=in_, func=func, bias=bias, scale=scale),
                       outs=[out], ins=[in_, bias, scale])

    def tt(self, eng, out, a, b, op):
        return self.op(eng, lambda e: e.tensor_tensor(out=out, in0=a, in1=b, op=op), outs=[out], ins=[a, b])

    def ts(self, eng, out, a, s1, op0, s2=None, op1=None):
        if op1 is None:
            return self.op(eng, lambda e: e.tensor_scalar(out=out, in0=a, scalar1=s1, scalar2=None, op0=op0),
                           outs=[out], ins=[a, s1])
        return self.op(eng, lambda e: e.tensor_scalar(out=out, in0=a, scalar1=s1, scalar2=s2, op0=op0, op1=op1),
                       outs=[out], ins=[a, s1, s2])

    def stt(self, out, in0, scalar, in1, op0, op1, accum_out=None):
        if accum_out is None:
            return self.op("dve", lambda e: e.scalar_tensor_tensor(out=out, in0=in0, scalar=scalar, in1=in1, op0=op0, op1=op1),
                           outs=[out], ins=[in0, scalar, in1])
        return self.op("dve", lambda e: e.scalar_tensor_tensor(out=out, in0=in0, scalar=scalar, in1=in1, op0=op0, op1=op1, accum_out=accum_out),
                       outs=[out, accum_out], ins=[in0, scalar, in1])

    def copy(self, eng, out, in_):
        if eng == "act":
            return self.op(eng, lambda e: e.copy(out=out, in_=in_), outs=[out], ins=[in_])
        return self.op(eng, lambda e: e.tensor_copy(out=out, in_=in_), outs=[out], ins=[in_])

    def memset(self, eng, out, val):
        return self.op(eng, lambda e: e.memset(out, val), outs=[out])

    def mm(self, out, lhsT, rhs, start=True, stop=True, tile_position=None):
        if tile_position is None:
            bp = self.rect(lhsT)[1]
            if bp == 96:
                tile_position = (96, self.rect(out)[1])
        if tile_position is None:
            fn = lambda e: e.matmul(out, lhsT=lhsT, rhs=rhs, start=start, stop=stop)
        else:
            fn = lambda e: e.matmul(out, lhsT=lhsT, rhs=rhs, start=start, stop=stop, tile_position=tile_position)
        n_ = 1
        for d_ in rhs.shape[1:]:
            n_ *= d_
        c_ = max(0.055, n_ / (2300.0 if getattr(self, 'pe_warm', False) else 1200.0)) * (4.0 if rhs.dtype == F32 else 1.0)
        bkey = out.tensor.name
        xd = [self.bank_last.get(bkey)] if start else []
        o_ = self.op("pe", fn, outs=[out], ins=[lhsT, rhs] + ([] if start else [out]), cost=c_, extra_deps=xd)
        self.bank_last[bkey] = o_
        return o_

    def mmg(self, items):
        fns = []
        outs, ins, banks_start, banks_all = [], [], [], []
        cost = 0.0
        for it in items:
            out = it["out"]
            bkey = out.tensor.name
            if bkey not in banks_all:
                banks_all.append(bkey)
            if it.get("kind", "mm") == "tr":
                in_, ident = it["in_"], it["ident"]
                fns.append(lambda e, out=out, in_=in_, ident=ident: e.transpose(out=out, in_=in_, identity=ident))
                ins += [in_, ident]
                if bkey not in banks_start:
                    banks_start.append(bkey)
                n_ = in_.shape[-1]
                cost += max(0.03, n_ / 2400.0)
            else:
                lhsT, rhs = it["lhsT"], it["rhs"]
                start, stop = it.get("start", True), it.get("stop", True)
                tp = it.get("tile_position")
                if tp is None and self.rect(lhsT)[1] == 96:
                    tp = (96, self.rect(out)[1])
                if tp is None:
                    fns.append(lambda e, out=out, lhsT=lhsT, rhs=rhs, start=start, stop=stop: e.matmul(out, lhsT=lhsT, rhs=rhs, start=start, stop=stop))
                else:
                    fns.append(lambda e, out=out, lhsT=lhsT, rhs=rhs, start=start, stop=stop, tp=tp: e.matmul(out, lhsT=lhsT, rhs=rhs, start=start, stop=stop, tile_position=tp))
                ins += [lhsT, rhs]
                if start and bkey not in banks_start:
                    banks_start.append(bkey)
                n_ = 1
                for d_ in rhs.shape[1:]:
                    n_ *= d_
                cost += max(0.03, n_ / (2300.0 if getattr(self, 'pe_warm', False) else 1200.0) * (0.6 if n_ <= 128 else 1.0)) * (4.0 if rhs.dtype == F32 else 1.0)
            outs.append(out)

        def fn(e):
            r_ = None
            for f_ in fns:
                r_ = f_(e)
            return r_
        xd = [self.bank_last.get(b_) for b_ in banks_start]
        o_ = self.op("pe", fn, outs=outs, ins=ins, cost=cost, extra_deps=xd)
        for b_ in banks_all:
            self.bank_last[b_] = o_
        return o_

    def tr(self, out, in_, ident):
        bkey = out.tensor.name
        o_ = self.op("pe", lambda e: e.transpose(out=out, in_=in_, identity=ident), outs=[out], ins=[in_, ident],
                     extra_deps=[self.bank_last.get(bkey)])
        self.bank_last[bkey] = o_
        return o_

    def recip(self, out, in_):
        return self.op("dve", lambda e: e.reciprocal(out=out, in_=in_), outs=[out], ins=[in_])

    def scan(self, out, d0, d1, initial):
        return self.op("dve", lambda e: e.tensor_tensor_scan(out=out, data0=d0, data1=d1, initial=initial, op0=ALU.mult, op1=ALU.add),
                       outs=[out], ins=[d0, d1, initial])

    def reduce(self, out, in_, axis=None):
        return self.op("dve", lambda e: e.tensor_reduce(out=out, in_=in_, axis=AX.X, op=ALU.add), outs=[out], ins=[in_])

    def emit(self):
        nc = self.nc
        rr = {}
        cnt = [0] * N_DMA_SEMS
        for e in ENGS:
            for o in self.prog[e]:
                if o.is_dma:
                    lo, n = DMA_POOLS[e]
                    k = lo + rr.get(e, 0)
                    rr[e] = (rr.get(e, 0) + 1) % n
                    cnt[k] += 1
                    o.dma_sem = k
                    o.dma_val = 16 * cnt[k]
        for e in ENGS:
            for o in self.prog[e]:
                for d in o.deps:
                    if not d.is_dma:
                        d.signaled = True
        for e in ENGS:
            c = 0
            for o in self.prog[e]:
                if o.signaled and not o.is_dma:
                    c += 1
                    o.sig_idx = c
        with contextlib.ExitStack() as st:
            sems = {e: st.enter_context(nc.semaphore("s_" + e)) for e in ENGS}
            dsems = [st.enter_context(nc.semaphore("d%d" % i)) for i in range(N_DMA_SEMS)]
            block = st.enter_context(nc.Block())
            engobj = {"pe": block.tensor, "act": block.scalar, "dve": block.vector,
                      "pool": block.gpsimd, "sp": block.sync}

            def make(e):
                def body(eng):
                    waited = {}
                    for o in self.prog[e]:
                        need = {}
                        for d in o.deps:
                            if d.is_dma:
                                key = ("d", d.dma_sem)
                                val = d.dma_val
                            else:
                                key = ("c", d.eng)
                                val = d.sig_idx
                            if waited.get(key, 0) >= val:
                                continue
                            if need.get(key, 0) < val:
                                need[key] = val
                        for key, val in need.items():
                            sem = dsems[key[1]] if key[0] == "d" else sems[key[1]]
                            eng.wait_ge(sem, val)
                            waited[key] = val
                        if o.is_dma and o.dma_val > 16:
                            key = ("d", o.dma_sem)
                            if waited.get(key, 0) < o.dma_val - 16:
                                eng.wait_ge(dsems[o.dma_sem], o.dma_val - 16)
                                waited[key] = o.dma_val - 16
                        ins = o.fn(eng)
                        if o.is_dma:
                            ins.then_inc(dsems[o.dma_sem], 16)
                        elif o.signaled:
                            ins.then_inc(sems[e], 1)
                    if e == "sp":
                        need = {}
                        for d in self.final_dmas:
                            need[d.dma_sem] = max(need.get(d.dma_sem, 0), d.dma_val)
                        for k, v in need.items():
                            eng.wait_ge(dsems[k], v)
                return body

            for e in ENGS:
                engobj[e](make(e))


IN_SHAPES = {
    "x": [T, D], "norm_mix_g": [D], "w_in": [D, DIN], "lam_re": [32, 64], "lam_im": [32, 64],
    "log_dt": [1, 32], "b_re": [32, 64, 16], "b_im": [32, 64, 16], "c_re": [32, 16, 64], "c_im": [32, 16, 64],
    "d_skip": [512], "w_glu": [512, 512], "b_glu": [512], "s5_out_g": [512], "mu_shift": [1824],
    "w0": [512], "w2": [64, 512], "a0": [512], "a2": [64, 512], "g2": [160, 512], "k_k": [512], "k_a": [512],
    "r_k": [512], "ln_x_w": [512], "ln_x_b": [512], "w_out": [D, D], "norm_ffn_g": [D],
    "w_up": [D, DFF], "w_down": [DFF, D], "norm_final_g": [1, D],
}


def build(debug=(), skip=()):
    nc = bass.Bass("TRN2", target_bir_lowering=False)
    kb = KB(nc)
    I = {k: nc.dram_tensor(k, v, F32, kind="ExternalInput").ap() for k, v in IN_SHAPES.items()}
    out_d = nc.dram_tensor("out", [T, D], F32, kind="ExternalOutput").ap()
    dbg_out = {}

    def dbg(name, ap, shape):
        if name not in debug:
            return
        d = nc.dram_tensor("dbg_" + name, list(shape), ap.dtype, kind="ExternalOutput").ap()
        dbg_out[name] = d
        kb.dma(d, ap, final=True)

    ident_f = kb.sb("ident_f", [128, 128]); ident_b = kb.sb("ident_b", [128, 128], BF16)
    ones_b = kb.sb("ones_b", [128, 128], BF16); blk_b = kb.sb("blk_b", [128, 128], BF16)
    m_su = kb.sb("m_su", [128, 8, 64], BF16); m_iu = kb.sb("m_iu", [128, 8, 64], BF16)
    m_sl = kb.sb("m_sl", [128, 8, 64], BF16); m_id = kb.sb("m_id", [128, 8, 64], BF16)
    pcol = kb.sb("pcol", [128, 96])
    gF = kb.sb("gF", [128, D])
    stat = kb.sb("stat", [128, 64])
    constc = kb.sb("constc", [128, 8])
    kb.memset("pool", constc[:, 0:1], NORM_EPS)
    kb.memset("pool", constc[:, 1:2], 1.0)
    kb.memset("pool", constc[:, 2:3], GN_EPS)
    cE = constc[:, 0:1]; c1c = constc[:, 1:2]; cG = constc[:, 2:3]
    xnT = kb.sb("xnT", [128, 8, T], BF16)
    mixT = kb.sb("mixT", [128, 8, T], BF16)
    wct = [kb.sb("wct%d" % i, [128, 8, 128], BF16) for i in range(2)]
    xst = [kb.sb("xst%d" % i, [128, D]) for i in range(2)]
    AR = kb.sb_off
    AR_END = 223 * 1024
    assert AR < 108 * 1024, AR

    class Arena:
        def __init__(self, start):
            self.cur = start

        def t(self, name, shape, dt=F32):
            fe = 1
            for s_ in shape[1:]:
                fe *= s_
            t_ = kb.sb(name, shape, dt, at=self.cur)
            self.cur += (fe * ESZ[dt] + 63) // 64 * 64
            assert self.cur <= AR_END, (name, self.cur)
            return t_

    PB = [kb.ps("pb%d" % i, [128, 512]) for i in range(5)]
    PH = kb.ps("pH", [128, 512])
    PH2 = kb.ps("pH2", [128, 512])
    PHs = [PH, PH2]
    PT = kb.ps("pT", [128, 1024], BF16)
    pbi = [0]

    def bank():
        b = PB[pbi[0] % 5]
        pbi[0] += 1
        return b

    wsti = [0]

    def stage():
        b = wst[wsti[0] % 2]
        wsti[0] += 1
        return b

    arA = Arena(AR)
    junk = arA.t("junk", [128, D])
    xnb = [arA.t("xnb%d" % i, [128, D], BF16) for i in range(2)]
    xs_t = [arA.t("xs%d" % i, [128, D]) for i in range(3)]
    onesf = arA.t("onesf", [128, 512])
    tmpf = arA.t("tmpf", [128, 512])
    kb.memset("pool", onesf[:], 1.0)
    kb.op("pool", lambda e: e.affine_select(out=ident_f[:], in_=onesf[:, 0:128], pattern=[[-1, 128]], compare_op=ALU.is_equal, fill=0.0, base=0, channel_multiplier=1), outs=[ident_f[:]], ins=[onesf[:]])
    kb.copy("pool", ident_b[:], ident_f[:])
    kb.copy("pool", ones_b[:], onesf[:, 0:128])
    kb.memset("pool", blk_b[:], 0.0)
    kb.memset("pool", blk_b[0:64, 0:64], 1.0)
    kb.memset("pool", blk_b[64:128, 64:128], 1.0)
    o3 = onesf[:].rearrange("p (a b) -> p a b", a=8)
    t3 = tmpf[:].rearrange("p (a b) -> p a b", a=8)

    def mk_mask(dst, cm, base, cmp):
        for h in range(2):
            sl = slice(64 * h, 64 * h + 64)
            kb.op("pool", lambda e, sl=sl, h=h: e.affine_select(out=t3[sl], in_=o3[sl], pattern=[[0, 8], [-cm, 64]], compare_op=cmp, fill=0.0, base=base, channel_multiplier=cm),
                  outs=[t3[sl]], ins=[o3[sl]])
        kb.copy("pool", dst[:], t3)

    mk_mask(m_su, -1, -1, ALU.is_ge)
    mk_mask(m_iu, -1, 0, ALU.is_ge)
    mk_mask(m_sl, 1, -1, ALU.is_ge)
    mk_mask(m_id, 1, 0, ALU.is_equal)

    pc_i = [0]
    PC = {}

    def col(name, ap1d, n):
        c = n // 128
        c0 = pc_i[0]
        pc_i[0] += c
        kb.dma(pcol[:, c0:c0 + c], ap1d.rearrange("(c p) -> p c", p=128), slow=True)
        PC[name] = lambda j, c0=c0: pcol[:, c0 + j:c0 + j + 1]

    col("g_mix", I["norm_mix_g"], 1024)
    col("g_ffn", I["norm_ffn_g"], 1024)
    for nm in ("d_skip", "b_glu", "s5_out_g", "w0", "a0", "k_k", "k_a", "r_k", "ln_x_w", "ln_x_b"):
        col(nm, I[nm], 512)
    col("mu_rkv", I["mu_shift"][0:1536], 1536)
    col("mu_l", I["mu_shift"][1536:1792], 256)
    omu_c0 = pc_i[0]
    pc_i[0] += 12
    mu_c0 = omu_c0 - 2 - 12
    kb.ts("dve", pcol[:, omu_c0:omu_c0 + 12], pcol[:, mu_c0:mu_c0 + 12], -1.0, ALU.mult, 1.0, ALU.add)
    PC["omu_rkv"] = lambda j, c0=omu_c0: pcol[:, c0 + j:c0 + j + 1]
    mu_g2 = pcol[0:32, pc_i[0]:pc_i[0] + 1]
    kb.dma(mu_g2, I["mu_shift"][1792:1824].rearrange("(c p) -> p c", p=32), slow=True)
    pc_i[0] += 1
    assert pc_i[0] <= 96
    gmix_T = pcol[:, 0:8]
    gffn_T = pcol[:, 8:16]
    kb.dma(gF[:], I["norm_final_g"].partition_broadcast(128))

    class Collect:
        def __enter__(self):
            self.items = []
            self.om, self.ot = kb.mm, kb.tr
            kb.mm = lambda out, lhsT, rhs, start=True, stop=True, tile_position=None: self.items.append(
                dict(out=out, lhsT=lhsT, rhs=rhs, start=start, stop=stop, tile_position=tile_position))
            kb.tr = lambda out, in_, ident: self.items.append(dict(kind="tr", out=out, in_=in_, ident=ident))
            return self

        def __exit__(self, *a):
            kb.mm, kb.tr = self.om, self.ot
            if self.items:
                kb.mmg(self.items)
            return False


    def norm_T(src_fn, gT, dstT, ss_col0, junk_, xnb_):
        for tt in range(16):
            xs = src_fn(tt)
            ss = stat[:, ss_col0 + tt:ss_col0 + tt + 1]
            kb.stt(junk_[:], xs, 1.0, xs, ALU.mult, ALU.mult, accum_out=ss)
            kb.act(ss, ss, AF.Sqrt, bias=cE, scale=1.0 / D)
            kb.recip(ss, ss)
            xb = xnb_[tt % 2]
            kb.act(xb[:], xs, AF.Copy, scale=ss)
            with Collect():
                for c in range(8):
                    kb.tr(PT[:, c * 128:(c + 1) * 128], xb[:, c * 128:(c + 1) * 128], ident_b[:])
            kb.tt("dve", dstT[:, :, tt * 128:(tt + 1) * 128], PT[:].rearrange("p (c t) -> p c t", c=8),
                  gT.unsqueeze(2).to_broadcast([128, 8, 128]), ALU.mult)

    def x_src(tt):
        b = xs_t[tt % 3]
        kb.dma(b[:], I["x"][tt * 128:(tt + 1) * 128, :])
        return b[:]

    dbg('pcol', pcol[:], [128, 96]); dbg('gF', gF[:], [128, D]); dbg('msu', m_su[:].rearrange('p a b -> p (a b)'), [128, 512])
    if 'A' not in skip:
        norm_T(x_src, gmix_T, xnT, 0, junk, xnb)
    dbg('xnT', xnT[:, 0, :], [128, T])

    def load_w(dst_bf, src2d, kc, ncols, dst_f32=False):
        per = max(1, 4096 // ncols)
        k = 0
        while k < kc:
            n = min(per, kc - k)
            kb.dma(dst_bf[:, k:k + n, :], src2d[k * 128:(k + n) * 128, :].rearrange("(c p) n -> p c n", p=128), eng="pool")
            k += n

    wcti = [0]

    def proj(c0, M, evac, bank_fn=None):
        w = wct[wcti[0] % 2]
        wcti[0] += 1
        load_w(w[:, :, 0:M], I["w_in"][:, c0:c0 + M], 8, M)
        for n in range(4):
            ps = (bank_fn or bank)()
            with Collect():
                for c in range(8):
                    kb.mm(ps[0:M, :], w[:, c, 0:M], xnT[:, c, n * 512:(n + 1) * 512], start=(c == 0), stop=(c == 7))
            evac(n, ps[0:M, :])

    def s5_phase():
        ar = Arena(AR)
        CH = 8
        NCH = T // CH
        uT = ar.t("uT", [128, T])
        uTb = ar.t("uTb", [128, T], BF16)
        yacc = ar.t("yacc", [128, T])
        gb = ar.t("gb", [128, 4, T], BF16)
        sp_small = ar.t("sp_small", [128, 32, 16])
        Mre = ar.t("Mre", [128, 16, 32]); Mim = ar.t("Mim", [128, 16, 32]); Mimn = ar.t("Mimn", [128, 16, 32])
        BT0re = ar.t("BT0re", [128, 4, 128]); BT0im = ar.t("BT0im", [128, 4, 128])
        L1re = ar.t("L1re", [128, 4, 128]); L1im = ar.t("L1im", [128, 4, 128])
        CL0re = ar.t("CL0re", [128, 16, 32]); CL0im = ar.t("CL0im", [128, 16, 32])
        bm32 = ar.t("bm32", [128, 128])
        iota_c = ar.t("iota_c", [128, NCH])
        iota_ci = ar.t("iota_ci", [128, NCH], I32)
        wglu = ar.t("wglu", [128, 4, 512], BF16)
        tsets = []
        for i_ in range(2):
            tsets.append({"WBb": [ar.t("WBb_re%d" % i_, [128, CH, 128], BF16), ar.t("WBb_im%d" % i_, [128, CH, 128], BF16)],
                          "CLb": [ar.t("CLb_re%d" % i_, [128, CH + 1, 128], BF16), ar.t("CLb_in%d" % i_, [128, CH + 1, 128], BF16)],
                          "Kbd": ar.t("Kbd%d" % i_, [128, CH, 128], BF16)})
        wr = [ar.t("wr%d" % i, [128, 128]) for i in range(2)]
        wi = [ar.t("wi%d" % i, [128, 128]) for i in range(2)]
        cr = [ar.t("cr%d" % i, [128, 128]) for i in range(2)]
        ci = [ar.t("ci%d" % i, [128, 128]) for i in range(2)]
        tw = [ar.t("tw%d" % i, [128, 128]) for i in range(4)]
        tc_ = [ar.t("tc%d" % i, [128, 128]) for i in range(4)]
        Xs = [[ar.t("Xs%d_%d" % (q_, r_), [128, NCH + 1], BF16) for r_ in range(2)] for q_ in range(4)]
        ov_ = ar.cur
        csets = []
        for i_ in range(4):
            csets.append({nm: ar.t("%s_c%d" % (nm, i_), [128, NCH]) for nm in ("tcos", "tsin", "cre", "cim", "tm1", "tm2")})
        arg = Arena(ov_)
        ysb = arg.t("ysb", [128, 4, 512])
        sigt = arg.t("sigt", [128, 512])
        sqb = arg.t("sqb", [128, 512], BF16)
        rstd = arg.t("rstd", [128, 512])
        aro = Arena(ov_)
        bre = aro.t("bre", [128, 16, 16]); bim = aro.t("bim", [128, 16, 16])
        bbr = aro.t("bbr", [128, 16, 16]); bbi = aro.t("bbi", [128, 16, 16]); btmp = aro.t("btmp", [128, 16, 16])
        MLr = aro.t("MLr", [128, 16, 32]); MLi = aro.t("MLi", [128, 16, 32])
        cdr = aro.t("cdr", [128, 4, 2, 64]); cdi = aro.t("cdi", [128, 4, 2, 64])

        def sm(i):
            return sp_small[:, i, :]

        (lre, lim, ldt, dt_, th, rl, rho, f0, rn, fr, s1, sh, c1, lbr, lbi, den, inv, fre, fim, x1, x2,
         rho16, f16) = [sm(i) for i in range(23)]
        for two in range(2):
            sl = slice(64 * two, 64 * two + 64)
            kb.dma(lre[sl], I["lam_re"].rearrange("(k two) p -> two p k", two=2)[two], slow=True)
            kb.dma(lim[sl], I["lam_im"].rearrange("(k two) p -> two p k", two=2)[two], slow=True)
            kb.dma(ldt[sl], I["log_dt"].rearrange("o (k two) -> o two k", two=2)[:, two, :].partition_broadcast(64), slow=True)
            kb.dma(bre[sl], I["b_re"].rearrange("(k two) p h -> two p k h", two=2)[two])
            kb.dma(bim[sl], I["b_im"].rearrange("(k two) p h -> two p k h", two=2)[two])
        kb.act(dt_, ldt, AF.Exp)
        kb.tt("dve", th, lim, dt_, ALU.mult)
        kb.tt("dve", rl, lre, dt_, ALU.mult)
        kb.act(rho, rl, AF.Exp)
        kb.act(rho16, rl, AF.Exp, scale=float(CH))
        kb.ts("dve", f0, th, 1.0 / TWO_PI, ALU.mult)
        kb.ts("dve", f16, f0, float(CH), ALU.mult)
        kb.ts("dve", rn, f0, MAGIC, ALU.add, -MAGIC, ALU.add)
        kb.tt("dve", fr, f0, rn, ALU.subtract)
        kb.act(s1, fr, AF.Sin, scale=TWO_PI * 0.9999999)
        kb.act(sh, fr, AF.Sin, scale=math.pi * 0.9999999)
        kb.act(c1, sh, AF.Square, scale=math.sqrt(2.0))
        kb.ts("dve", c1, c1, -1.0, ALU.mult, 1.0, ALU.add)
        kb.tt("dve", lbr, rho, c1, ALU.mult)
        kb.tt("dve", lbi, rho, s1, ALU.mult)
        kb.ts("dve", x1, lbr, -1.0, ALU.add)
        kb.tt("dve", den, lre, lre, ALU.mult)
        kb.tt("dve", x2, lim, lim, ALU.mult)
        kb.tt("dve", den, den, x2, ALU.add)
        kb.recip(inv, den)
        kb.tt("dve", fre, x1, lre, ALU.mult)
        kb.tt("dve", x2, lbi, lim, ALU.mult)
        kb.tt("dve", fre, fre, x2, ALU.add)
        kb.tt("dve", fre, fre, inv, ALU.mult)
        kb.tt("dve", fim, lbi, lre, ALU.mult)
        kb.tt("dve", x2, x1, lim, ALU.mult)
        kb.tt("dve", fim, fim, x2, ALU.subtract)
        kb.tt("dve", fim, fim, inv, ALU.mult)
        freb = fre.unsqueeze(2).to_broadcast([128, 16, 16])
        fimb = fim.unsqueeze(2).to_broadcast([128, 16, 16])
        kb.tt("dve", bbr[:], bre[:], freb, ALU.mult)
        kb.tt("dve", btmp[:], bim[:], fimb, ALU.mult)
        kb.tt("dve", bbr[:], bbr[:], btmp[:], ALU.subtract)
        kb.tt("dve", bbi[:], bim[:], freb, ALU.mult)
        kb.tt("dve", btmp[:], bre[:], fimb, ALU.mult)
        kb.tt("dve", bbi[:], bbi[:], btmp[:], ALU.add)
        lbrb = lbr.unsqueeze(2).to_broadcast([128, 16, 16])
        lbib = lbi.unsqueeze(2).to_broadcast([128, 16, 16])
        for (M_, src_) in ((Mre, bbr[:]), (Mim, bbi[:]), (MLr, lbrb), (MLi, lbib)):
            kb.memset("pool", M_[:], 0.0)
            kb.copy("pool", M_[0:64, :, 0:16], src_[0:64])
            kb.copy("pool", M_[64:128, :, 16:32], src_[64:128])
        kb.ts("pool", Mimn[:], Mim[:], -1.0, ALU.mult)
        for (M_, BT) in ((Mre, BT0re), (Mim, BT0im), (MLr, L1re), (MLi, L1im)):
            for j in range(4):
                ps = bank()
                kb.tr(ps[:, 0:128], M_[:, 4 * j:4 * j + 4, :].rearrange("p a b -> p (a b)"), ident_f[:])
                kb.copy("act", BT[:, j, :], ps[:, 0:128])
        for (cd, nm) in ((cdr, "c_re"), (cdi, "c_im")):
            src = I[nm].rearrange("(j g) h p -> (g h) j p", g=8)
            kb.dma(cd[:, :, 0, :], src)
            kb.dma(cd[:, :, 1, :], src)
        for (cd, CL0) in ((cdr, CL0re), (cdi, CL0im)):
            kb.memset("pool", CL0[:], 0.0)
            for j in range(4):
                ps = bank()
                kb.tr(ps[:, 0:128], cd[:, j].rearrange("p a b -> p (a b)"), ident_f[:])
                pv = ps[:, 0:128].rearrange("p (q t h) -> p q t h", q=4, t=2)
                kb.copy("act", CL0[0:64, 4 * j:4 * j + 4, 0:16], pv[0:64, :, 0, :])
                kb.copy("act", CL0[64:128, 4 * j:4 * j + 4, 16:32], pv[64:128, :, 1, :])
        load_w(wglu, I["w_glu"], 4, 512)
        kb.op("pool", lambda e: e.iota(iota_ci[:], pattern=[[1, NCH]], base=0, channel_multiplier=0), outs=[iota_ci[:]])
        kb.copy("pool", iota_c[:], iota_ci[:])
        kb.memset("pool", tw[0][:], 1.0)
        kb.op("pool", lambda e: e.affine_select(out=tw[1][:].rearrange("p (a b) -> p a b", a=4), in_=tw[0][:].rearrange("p (a b) -> p a b", a=4), pattern=[[-32, 4], [0, 32]], compare_op=ALU.is_ge, fill=0.0, base=0, channel_multiplier=1),
              outs=[tw[1][:]], ins=[tw[0][:]])
        kb.op("pool", lambda e: e.affine_select(out=bm32[:].rearrange("p (a b) -> p a b", a=4), in_=tw[1][:].rearrange("p (a b) -> p a b", a=4), pattern=[[32, 4], [0, 32]], compare_op=ALU.is_ge, fill=0.0, base=31, channel_multiplier=-1),
              outs=[bm32[:]], ins=[tw[1][:]])
        for q_ in range(4):
            for r_ in range(2):
                kb.memset("pool", Xs[q_][r_][:, 0:1], 0.0)

        def cmul_step(a_r, a_i, b_r, b_i, o_r, o_i, t4):
            kb.tt("dve", t4[0][:], a_r, b_r, ALU.mult)
            kb.tt("pool", t4[1][:], a_i, b_i, ALU.mult)
            kb.tt("dve", t4[2][:], a_r, b_i, ALU.mult)
            kb.tt("pool", t4[3][:], a_i, b_r, ALU.mult)
            kb.tt("dve", o_r, t4[0][:], t4[1][:], ALU.subtract)
            kb.tt("pool", o_i, t4[2][:], t4[3][:], ALU.add)

        b7 = [PB[4], PH, PH2]
        b7i = [0]

        def bank7():
            b7i[0] += 1
            return b7[b7i[0] % 3]

        def pre_gen(j, S):
            WBb, CLb, Kbd = S["WBb"], S["CLb"], S["Kbd"]
            Mre_j = Mre[:, 4 * j:4 * j + 4, :].rearrange("p a b -> p (a b)")
            Mimn_j = Mimn[:, 4 * j:4 * j + 4, :].rearrange("p a b -> p (a b)")
            lamr_b = lbr[:, 4 * j:4 * j + 4].unsqueeze(2).to_broadcast([128, 4, 32])
            lami_b = lbi[:, 4 * j:4 * j + 4].unsqueeze(2).to_broadcast([128, 4, 32])
            kb.copy("pool", wr[0][:], BT0re[:, j, :])
            kb.copy("pool", wi[0][:], BT0im[:, j, :])
            kb.copy("pool", cr[0][:], CL0re[:, 4 * j:4 * j + 4, :].rearrange("p a b -> p (a b)"))
            kb.copy("pool", ci[0][:], CL0im[:, 4 * j:4 * j + 4, :].rearrange("p a b -> p (a b)"))
            c3 = lambda t_: t_[:].rearrange("p (a b) -> p a b", a=4)
            for n in range(CH + 1):
                a, b = n % 2, (n + 1) % 2
                if n < CH:
                    kb.copy("act", WBb[0][:, n, :], wr[a][:])
                    kb.copy("act", WBb[1][:, n, :], wi[a][:])
                kb.copy("act", CLb[0][:, n, :], cr[a][:])
                kb.act(CLb[1][:, n, :], ci[a][:], AF.Copy, scale=-1.0)
                if n < CH:
                    ps = bank7()
                    with Collect():
                        kb.mm(ps[:, 0:128], Mre_j, cr[a][:], start=True, stop=False)
                        kb.mm(ps[:, 0:128], Mimn_j, ci[a][:], start=False, stop=True)
                    kb.tt("dve", Kbd[:, n, :], ps[:, 0:128], bm32[:], ALU.mult)
                    if n < CH - 1:
                        cmul_step(wr[a][:], wi[a][:], L1re[:, j, :], L1im[:, j, :], wr[b][:], wi[b][:], tw)
                    kb.tt("dve", c3(tc_[0]), c3(cr[a]), lamr_b, ALU.mult)
                    kb.tt("pool", c3(tc_[1]), c3(ci[a]), lami_b, ALU.mult)
                    kb.tt("dve", c3(tc_[2]), c3(cr[a]), lami_b, ALU.mult)
                    kb.tt("pool", c3(tc_[3]), c3(ci[a]), lamr_b, ALU.mult)
                    kb.tt("dve", cr[b][:], tc_[0][:], tc_[1][:], ALU.subtract)
                    kb.tt("pool", ci[b][:], tc_[2][:], tc_[3][:], ALU.add)
                yield

        def pair_gen(j, q, S, B, u3):
            WBb = S["WBb"]
            k = 4 * j + q
            rs = slice(32 * q, 32 * q + 32)
            pr = PB[2 * (q % 2)][:, 0:NCH]
            pi = PB[2 * (q % 2) + 1][:, 0:NCH]
            with Collect():
                for i_ in range(CH):
                    kb.mm(pr, WBb[0][rs, CH - 1 - i_, :], u3[rs, :, i_], start=(i_ == 0), stop=(i_ == CH - 1))
            with Collect():
                for i_ in range(CH):
                    kb.mm(pi, WBb[1][rs, CH - 1 - i_, :], u3[rs, :, i_], start=(i_ == 0), stop=(i_ == CH - 1))
            f16k = f16[:, k:k + 1]
            tcos, tsin, cre, cim, tm1, tm2 = (B[x_] for x_ in ("tcos", "tsin", "cre", "cim", "tm1", "tm2"))
            kb.ts("pool", tm1[:], iota_c[:], f16k, ALU.mult, MAGIC, ALU.add)
            kb.ts("pool", tm1[:], tm1[:], -1.0, ALU.mult, MAGIC, ALU.add)
            kb.stt(tm1[:], iota_c[:], f16k, tm1[:], ALU.mult, ALU.add)
            yield
            kb.act(tsin[:], tm1[:], AF.Sin, scale=TWO_PI * 0.9999999)
            kb.act(tcos[:], tm1[:], AF.Sin, scale=math.pi * 0.9999999)
            kb.act(tcos[:], tcos[:], AF.Square, scale=math.sqrt(2.0))
            kb.act(tcos[:], tcos[:], AF.Identity, scale=-1.0, bias=c1c)
            yield
            kb.tt("dve", cre[:], pr, tcos[:], ALU.mult)
            kb.tt("dve", tm1[:], pi, tsin[:], ALU.mult)
            kb.tt("dve", cre[:], cre[:], tm1[:], ALU.add)
            kb.tt("dve", cim[:], pi, tcos[:], ALU.mult)
            kb.tt("dve", tm2[:], pr, tsin[:], ALU.mult)
            kb.tt("dve", cim[:], cim[:], tm2[:], ALU.subtract)
            yield
            r16 = rho16[:, k:k + 1].to_broadcast([128, NCH])
            kb.scan(cre[:], r16, cre[:], 0.0)
            kb.scan(cim[:], r16, cim[:], 0.0)
            yield
            kb.tt("dve", tm1[:], cre[:], tcos[:], ALU.mult)
            kb.tt("pool", tm2[:], cim[:], tsin[:], ALU.mult)
            kb.tt("dve", Xs[q][0][:, 1:NCH + 1], tm1[:], tm2[:], ALU.subtract)
            yield
            kb.tt("dve", tm1[:], cre[:], tsin[:], ALU.mult)
            kb.tt("pool", tm2[:], cim[:], tcos[:], ALU.mult)
            kb.tt("dve", Xs[q][1][:, 1:NCH + 1], tm1[:], tm2[:], ALU.add)
            yield

        def main_gen(j, S):
            CLb, Kbd = S["CLb"], S["Kbd"]

            upv = uTb[:, :].rearrange("p (i c) -> p c i", i=CH)

            def ev(n, ps):
                kb.copy("act", uT[:, n * 512:(n + 1) * 512], ps)
                cpc = 512 // CH
                kb.copy("act", upv[:, n * cpc:(n + 1) * cpc, :], ps.rearrange("p (c i) -> p c i", i=CH))
            proj(128 * j, 128, ev, bank_fn=bank7)
            if j == 0:
                dbg("uT", uT[:], [128, T])
            yield
            u3 = uTb[:, :].rearrange("p (i c) -> p c i", i=CH)
            for q0 in (0, 2):
                gens = [pair_gen(j, q0 + q_, S, csets[q0 + q_], u3) for q_ in range(2)]
                alive = [True] * 2
                while any(alive):
                    for gi_ in range(2):
                        if alive[gi_]:
                            try:
                                next(gens[gi_])
                            except StopIteration:
                                alive[gi_] = False
                    yield
            uf3 = uT[:, :].rearrange("p (c i) -> p c i", i=CH)
            y3 = yacc[:, :].rearrange("p (c i) -> p c i", i=CH)
            for jl in range(CH):
                acc = bank7()
                with Collect():
                    for tau in range(jl + 1):
                        kb.mm(acc[:, 0:NCH], Kbd[:, tau, :], u3[:, :, jl - tau], start=(tau == 0), stop=False)
                    for r_ in range(2):
                        for q in range(4):
                            kb.mm(acc[32 * q:32 * q + 32, 0:NCH], CLb[r_][:, jl + 1, 32 * q:32 * q + 32], Xs[q][r_][:, 0:NCH],
                                  start=False, stop=(r_ == 1), tile_position=(0, 32 * q))
                kb.stt(y3[:, :, jl], uf3[:, :, jl], PC["d_skip"](j), acc[:, 0:NCH], ALU.mult, ALU.add)
                yield
            kb.act(gb[:, j, :], yacc[:], AF.Gelu)
            yield

        pg = pre_gen(0, tsets[0])
        for _ in pg:
            pass
        for j in range(4):
            mg = main_gen(j, tsets[j % 2])
            pg = pre_gen(j + 1, tsets[(j + 1) % 2]) if j + 1 < 4 else None
            am, ap_ = True, pg is not None
            while am or ap_:
                if am:
                    try:
                        next(mg)
                    except StopIteration:
                        am = False
                if ap_:
                    try:
                        next(pg)
                    except StopIteration:
                        ap_ = False
        for n in range(4):
            ns = slice(n * 512, (n + 1) * 512)
            for jo in range(4):
                ps = bank()
                for ji in range(4):
                    kb.mm(ps[:, :], wglu[:, ji, jo * 128:(jo + 1) * 128], gb[:, ji, ns], start=(ji == 0), stop=(ji == 3))
                kb.act(sigt[:], ps[:, :], AF.Sigmoid, bias=PC["b_glu"](jo))
                kb.tt("dve", ysb[:, jo, :], gb[:, jo, ns], sigt[:], ALU.mult)
                kb.act(sqb[:], ysb[:, jo, :], AF.Square)
                kb.mm(PH[:, :], ones_b[:], sqb[:], start=(jo == 0), stop=(jo == 3))
            kb.act(rstd[:], PH[:, :], AF.Ln, bias=cE, scale=1.0 / 512.0)
            kb.act(rstd[:], rstd[:], AF.Exp, scale=-0.5)
            for jo in range(4):
                kb.stt(mixT[:, jo, ns], ysb[:, jo, :], PC["s5_out_g"](jo), rstd[:], ALU.mult, ALU.mult)
        dbg("ys5", mixT[:, 0, :], [128, T])

    def rwkv_phase():
        BL = 512
        HS = [slice(0, 64), slice(64, 128)]

        def cc8():
            with Collect():
                for cc_ in range(8):
                    yield cc_
        ar = Arena(AR)
        twza = ar.t("twza", [128, T], BF16)
        sg1b = ar.t("sg1b", [128, T], BF16)
        sg2b = ar.t("sg2b", [128, T], BF16)
        wa2b = ar.t("wa2b", [128, 512], BF16)
        g2b1 = ar.t("g2b1", [128, 512], BF16)
        g2b2 = ar.t("g2b2", [128, 512], BF16)
        resetm = ar.t("resetm", [128, BL], BF16)
        wtl = [[ar.t("w%s%d" % (nm, i), [128, 8, 128], BF16) for i in range(1)] for nm in "krv"]
        carry = ar.t("carry", [128, 4])
        S32 = ar.t("S32", [128, 64])
        Sb = [ar.t("Sb%d" % i, [128, 64], BF16) for i in range(2)]
        gst = ar.t("gst", [128, 64])
        zf = ar.t("zf", [128, 64])
        base_pipe = ar.cur
        arl = Arena(base_pipe)
        Rf = arl.t("Rf", [128, T + 1]); X4f = arl.t("X4f", [128, T]); X5f = arl.t("X5f", [128, T])
        s1o = []
        for i in range(3):
            d_ = {}
            for nm in ("At", "Bt", "Kt", "Rt", "BP", "KP", "vb"):
                d_[nm] = ar.t("s1_%s%d" % (nm, i), [128, BL], BF16)
            d_["PCc"] = ar.t("s1_PC%d" % i, [128, 8])
            d_["bonus"] = ar.t("s1_bo%d" % i, [128, BL])
            s1o.append(d_)
        s2o = []
        for i in range(2):
            d_ = {}
            for nm in ("BPt", "KPt", "Vt", "Att", "AK", "RB", "RK", "Z", "TAk", "AKV", "U0", "GT", "H0", "R2T"):
                d_[nm] = ar.t("s2_%s%d" % (nm, i), [128, 8, 64], BF16)
            s2o.append(d_)
        Rk = [ar.t("Rk%d" % i, [128, BL + 1]) for i in range(3)]
        X1 = ar.t("X1", [128, BL]); X2 = ar.t("X2", [128, BL]); X3 = ar.t("X3", [128, BL])
        X4 = ar.t("X4", [128, BL]); X5 = ar.t("X5", [128, BL]); X6 = ar.t("X6", [128, BL]); X7 = ar.t("X7", [128, BL])
        sqb = ar.t("rsqb", [128, BL], BF16)
        sqb2 = ar.t("rsqb2", [128, BL], BF16)
        Y2 = ar.t("Y2", [128, BL]); Y3 = ar.t("Y3", [128, BL]); CS = ar.t("CS", [128, BL])
        Ee = [ar.t("Ee%d" % i, [128, 512], BF16) for i in range(2)]
        Ff = [ar.t("Ff%d" % i, [128, 512], BF16) for i in range(2)]
        Zz = [ar.t("Zz%d" % i, [128, 512], BF16) for i in range(2)]
        Ubs = [ar.t("Ub%d" % i, [128, 64], BF16) for i in range(2)]
        sqf = ar.t("sqf", [128, 512]); ytm = ar.t("ytm", [128, 512]); ynb = ar.t("ynb", [128, 512], BF16)
        ygf = ar.t("ygf", [128, 512])

        bS1 = [PB[0], PB[1]]; bS2 = [PB[2], PB[3]]; bS3 = PB[4]
        ci = {"1": 0, "2": 0}

        def bk1():
            ci["1"] += 1
            return bS1[ci["1"] % 2]

        def bk2():
            ci["2"] += 1
            return bS2[ci["2"] % 2]

        kb.dma(X4f[0:64, 0:512], I["w2"])
        kb.dma(X4f[64:128, 0:512], I["a2"])
        kb.copy("pool", wa2b[:], X4f[:, 0:512])
        kb.dma(X5f[:, 0:512], I["g2"][0:128, :])
        kb.dma(X5f[0:32, 512:1024], I["g2"][128:160, :])
        kb.copy("pool", g2b1[:], X5f[:, 0:512])
        kb.copy("pool", g2b2[0:32, :], X5f[0:32, 512:1024])
        kb.memset("pool", resetm[:], 1.0)
        kb.memset("pool", resetm[:].rearrange("p (c j) -> p c j", j=64)[:, :, 0:1], 0.0)
        kb.memset("pool", zf[:], 0.0)

        def proj_shift_full(c0, M, mu_ap, dst):
            def ev(n, ps):
                kb.copy("act", Rf[0:M, 1 + n * 512:1 + (n + 1) * 512], ps)
            proj(c0, M, ev)
            kb.memset("pool", Rf[0:M, 0:1], 0.0)
            kb.tt("pool", X4f[0:M, :], Rf[0:M, 0:T], Rf[0:M, 1:T + 1], ALU.subtract)
            kb.stt(dst[0:M, :], X4f[0:M, :], mu_ap, Rf[0:M, 1:T + 1], ALU.mult, ALU.add)

        proj_shift_full(2048, 128, PC["mu_l"](0), X5f)
        kb.act(twza[0:64, :], X5f[0:64, :], AF.Tanh)
        kb.copy("act", twza[64:128, :], X5f[64:128, :])
        proj_shift_full(2176, 128, PC["mu_l"](1), X5f)
        kb.act(sg1b[:], X5f[:], AF.Sigmoid)
        proj_shift_full(2304, 32, mu_g2, X5f)
        kb.act(sg2b[0:32, :], X5f[0:32, :], AF.Sigmoid)

        def load_hp_weights(hp):
            for xi, c0 in enumerate((1024, 512, 1536)):
                load_w(wtl[xi][0], I["w_in"][:, c0 + 128 * hp:c0 + 128 * hp + 128], 8, 128)

        def S1(u):
            hp, blk = divmod(u, 4)
            o = s1o[u % 3]
            hc = slice(hp * 128, (hp + 1) * 128)
            ts_ = slice(blk * BL, (blk + 1) * BL)
            if blk == 0:
                load_hp_weights(hp)
            dsts = (X1, X3, X5)
            mi = (4 + hp, hp, 8 + hp)
            for xi in range(3):
                w = wtl[xi][0]
                R = Rk[xi]
                ps = bk1()
                with Collect():
                    for c in range(8):
                        kb.mm(ps[:, :], w[:, c, :], xnT[:, c, ts_], start=(c == 0), stop=(c == 7))
                kb.copy("act", R[:, 1:BL + 1], ps[:, :])
                kb.act(dsts[xi][:], ps[:, :], AF.Copy, scale=PC["omu_rkv"](mi[xi]))
                if blk == 0:
                    kb.memset("pool", R[:, 0:1], 0.0)
                else:
                    kb.copy("pool", R[:, 0:1], carry[:, xi:xi + 1])
                kb.copy("pool", carry[:, xi:xi + 1], R[:, BL:BL + 1])
                kb.stt(dsts[xi][:], R[:, 0:BL], PC["mu_rkv"](mi[xi]), dsts[xi][:], ALU.mult, ALU.add)
                yield
            ps = bk1()
            kb.mm(ps[:, :], wa2b[64:128, hc], twza[64:128, ts_])
            kb.act(X2[:], ps[:, :], AF.Sigmoid, bias=PC["a0"](hp))
            ps = bk1()
            kb.mm(ps[:, :], wa2b[0:64, hc], twza[0:64, ts_])
            kb.act(X6[:], ps[:, :], AF.Sigmoid, bias=PC["w0"](hp))
            yield
            kb.scan(CS[:], resetm[:], X6[:], 0.0)
            cs3 = CS[:].rearrange("p (c j) -> p c j", j=64)
            kb.act(X4[:], X1[:], AF.Copy, scale=PC["k_k"](hp))
            kb.act(sqb[:], X4[:], AF.Square)
            ps = bk1()
            kb.mm(ps[:, :], blk_b[:], sqb[:])
            kb.act(X7[:], ps[:, :], AF.Ln)
            kb.act(X7[:], X7[:], AF.Exp, scale=-0.5)
            kb.stt(X4[:], X7[:], 1e12, X4[:], ALU.min, ALU.mult)
            yield
            kb.ts("dve", Y2[:], X2[:], -1.0, ALU.add, PC["k_a"](hp), ALU.mult)
            kb.stt(X1[:], Y2[:], 1.0, X1[:], ALU.add, ALU.mult)
            kb.tt("dve", X2[:], X4[:], X2[:], ALU.mult)
            yield
            kb.tt("dve", X6[:], CS[:], X6[:], ALU.subtract)
            kb.act(X6[:], X6[:], AF.Exp, scale=-C0)
            kb.stt(o["At"][:], X4[:], -1.0, X6[:], ALU.mult, ALU.mult)
            yield
            kb.act(Y3[:], CS[:], AF.Exp, scale=C0)
            kb.tt("dve", o["Bt"][:], X2[:], Y3[:], ALU.mult)
            kb.tt("dve", o["Kt"][:], X1[:], Y3[:], ALU.mult)
            yield
            kb.tt("dve", Y2[:].rearrange("p (c j) -> p c j", j=64), cs3[:, :, 63:64].to_broadcast([128, 8, 64]), cs3, ALU.subtract)
            kb.act(Y2[:], Y2[:], AF.Exp, scale=-C0)
            kb.tt("dve", o["BP"][:], X2[:], Y2[:], ALU.mult)
            kb.tt("dve", o["KP"][:], X1[:], Y2[:], ALU.mult)
            kb.act(o["PCc"][:, :], cs3[:, :, 63], AF.Exp, scale=-C0)
            yield
            kb.act(X7[:], CS[:], AF.Exp, scale=-C0)
            kb.tt("dve", o["Rt"][:], X3[:], X7[:], ALU.mult)
            kb.tt("dve", Y3[:], X3[:], X1[:], ALU.mult)
            kb.act(sqb2[:], Y3[:], AF.Copy, scale=PC["r_k"](hp))
            kb.copy("act", o["vb"][:], X5[:])
            ps = bk1()
            kb.mm(ps[:, :], blk_b[:], sqb2[:])
            kb.tt("dve", o["bonus"][:], ps[:, :], X5[:], ALU.mult)
            yield

        def S2(u):
            o = s1o[u % 3]
            q = s2o[u % 2]
            for (src, dstt) in ((o["BP"], q["BPt"]), (o["KP"], q["KPt"]), (o["vb"], q["Vt"]), (o["At"], q["Att"])):
                for cc in cc8():
                    for h in range(2):
                        sl = HS[h]
                        kb.tr(PT[sl, cc * 64:(cc + 1) * 64], src[sl, cc * 64:(cc + 1) * 64], ident_b[sl, sl])
                kb.copy("act", dstt[:].rearrange("p c f -> p (c f)"), PT[:, 0:512])
                yield
            combos = ((o["Bt"], o["At"], m_su, None), (o["At"], o["Bt"], m_sl, None), (o["Kt"], o["At"], m_su, q["AK"]),
                      (o["Bt"], o["Rt"], m_iu, q["RB"]), (o["Kt"], o["Rt"], m_iu, q["RK"]))
            for ci_, (lh, rh, msk, dst_) in enumerate(combos):
                ps = bk2()
                for cc in cc8():
                    cs_ = slice(cc * 64, (cc + 1) * 64)
                    for h in range(2):
                        sl = HS[h]
                        kb.mm(ps[sl, cs_], lh[sl, cs_], rh[sl, cs_])
                if ci_ == 0:
                    o_ = Ee[0][:]
                elif ci_ == 1:
                    o_ = Ff[0][:]
                else:
                    o_ = dst_[:].rearrange("p c f -> p (c f)")
                kb.tt("dve", o_, ps[:, :], msk[:].rearrange("p c f -> p (c f)"), ALU.mult)
                yield
            kb.tt("dve", Zz[0][:], Ee[0][:], m_id[:].rearrange("p c f -> p (c f)"), ALU.add)
            for lvl in range(1, 6):
                ep, fp, zp = Ee[(lvl - 1) % 2], Ff[(lvl - 1) % 2], Zz[(lvl - 1) % 2]
                en, fn_, zn = Ee[lvl % 2], Ff[lvl % 2], Zz[lvl % 2]
                ps = bk2()
                for cc in cc8():
                    c6 = slice(cc * 64, (cc + 1) * 64)
                    for h in range(2):
                        sl = HS[h]
                        kb.mm(ps[sl, c6], ep[sl, c6], fp[sl, c6])
                kb.copy("act", fn_[:], ps[:, :])
                yield
                if lvl < 5:
                    ps = bk2()
                    for cc in cc8():
                        c6 = slice(cc * 64, (cc + 1) * 64)
                        for h in range(2):
                            sl = HS[h]
                            kb.mm(ps[sl, c6], fp[sl, c6], ep[sl, c6])
                    kb.copy("act", en[:], ps[:, :])
                    yield
                ps = bk2()
                for cc in cc8():
                    c6 = slice(cc * 64, (cc + 1) * 64)
                    for h in range(2):
                        sl = HS[h]
                        kb.mm(ps[sl, c6], fn_[sl, c6], zp[sl, c6])
                zo = zn[:] if lvl < 5 else q["Z"][:].rearrange("p c f -> p (c f)")
                kb.tt("dve", zo, zp[:], ps[:, :], ALU.add)
                yield
            Z2 = q["Z"][:].rearrange("p c f -> p (c f)")
            ps = bk2()
            for cc in cc8():
                c6 = slice(cc * 64, (cc + 1) * 64)
                for h in range(2):
                    sl = HS[h]
                    kb.mm(ps[sl, c6], Z2[sl, c6], q["Att"][sl, cc, :])
            kb.copy("act", q["TAk"][:].rearrange("p c f -> p (c f)"), ps[:, :])
            yield
            ps = bk2()
            for cc in cc8():
                c6 = slice(cc * 64, (cc + 1) * 64)
                for h in range(2):
                    sl = HS[h]
                    kb.mm(ps[sl, c6], q["AK"][sl, cc, :], q["Vt"][sl, cc, :])
            kb.copy("act", q["AKV"][:].rearrange("p c f -> p (c f)"), ps[:, :])
            yield
            ps = bk2()
            for cc in cc8():
                c6 = slice(cc * 64, (cc + 1) * 64)
                for h in range(2):
                    sl = HS[h]
                    kb.mm(ps[sl, c6], Z2[sl, c6], q["AKV"][sl, cc, :])
            kb.copy("act", q["U0"][:].rearrange("p c f -> p (c f)"), ps[:, :])
            yield
            ps = bk2()
            for cc in cc8():
                c6 = slice(cc * 64, (cc + 1) * 64)
                for h in range(2):
                    sl = HS[h]
                    kb.mm(ps[sl, c6], q["TAk"][sl, cc, :], q["BPt"][sl, cc, :])
            kb.copy("act", q["GT"][:].rearrange("p c f -> p (c f)"), ps[:, :])
            yield
            ps = bk2()
            for cc in cc8():
                c6 = slice(cc * 64, (cc + 1) * 64)
                for h in range(2):
                    sl = HS[h]
                    kb.mm(ps[sl, c6], q["BPt"][sl, cc, :], q["U0"][sl, cc, :], start=True, stop=False)
                    kb.mm(ps[sl, c6], q["KPt"][sl, cc, :], q["Vt"][sl, cc, :], start=False, stop=True)
            kb.copy("act", q["H0"][:].rearrange("p c f -> p (c f)"), ps[:, :])
            yield
            ps = bk2()
            for cc in cc8():
                c6 = slice(cc * 64, (cc + 1) * 64)
                for h in range(2):
                    sl = HS[h]
                    kb.mm(ps[sl, c6], q["TAk"][sl, cc, :], q["RB"][sl, cc, :])
            kb.tt("dve", q["R2T"][:].rearrange("p c f -> p (c f)"), ps[:, :], o["Rt"][:], ALU.add)
            yield

        cglob = [0]

        def S3(u):
            hp, blk = divmod(u, 4)
            o = s1o[u % 3]
            q = s2o[u % 2]
            hc = slice(hp * 128, (hp + 1) * 128)
            ts_ = slice(blk * BL, (blk + 1) * BL)
            PY = PHs[u % 2]
            if blk == 0:
                kb.memset("pool", S32[:], 0.0)
                kb.memset("pool", Sb[cglob[0] % 2][:], 0.0)
            for cc in range(8):
                c6 = slice(cc * 64, (cc + 1) * 64)
                g = cglob[0]
                cglob[0] += 1
                sb_old = Sb[g % 2]
                sb_new = Sb[(g + 1) % 2]
                pcol_ = 0
                pS = bS3[:, pcol_:pcol_ + 64]
                with Collect():
                    for h in range(2):
                        sl = HS[h]
                        kb.mm(bS3[sl, pcol_:pcol_ + 64], ident_b[sl, sl], q["H0"][sl, cc, :], start=True, stop=False)
                    for h in range(2):
                        sl = HS[h]
                        kb.mm(bS3[sl, pcol_:pcol_ + 64], q["GT"][sl, cc, :], sb_old[sl, :], start=False, stop=True)
                with Collect():
                    for st_, (A_, B_) in enumerate(((q["RB"], q["U0"]), (q["RK"], q["Vt"]), (q["R2T"], None))):
                        for h in range(2):
                            sl = HS[h]
                            rhs_ = sb_old[sl, :] if B_ is None else B_[sl, cc, :]
                            kb.mm(PY[sl, c6], A_[sl, cc, :], rhs_, start=(st_ == 0), stop=(st_ == 2))
                kb.stt(sb_new[:], S32[:], o["PCc"][:, cc:cc + 1], pS, ALU.mult, ALU.add)
                kb.stt(S32[:], S32[:], o["PCc"][:, cc:cc + 1], pS, ALU.mult, ALU.add)
                yield
            kb.copy("act", ytm[:], PY[:, :])
            for cc in range(8):
                c6 = slice(cc * 64, (cc + 1) * 64)
                kb.stt(sqf[:, c6], ytm[:, c6], 1.0, ytm[:, c6], ALU.mult, ALU.mult, accum_out=gst[:, 8 + cc:9 + cc])
                kb.stt(sqf[:, c6], ytm[:, c6], 1.0, zf[:, :], ALU.mult, ALU.add, accum_out=gst[:, cc:cc + 1])
            yield
            kb.ts("dve", gst[:, 16:24], gst[:, 0:8], 1.0 / 64.0, ALU.mult)
            kb.tt("dve", gst[:, 24:32], gst[:, 16:24], gst[:, 16:24], ALU.mult)
            kb.stt(gst[:, 32:40], gst[:, 8:16], 1.0 / 64.0, gst[:, 24:32], ALU.mult, ALU.subtract)
            kb.act(gst[:, 40:48], gst[:, 32:40], AF.Ln, bias=cG)
            kb.act(gst[:, 40:48], gst[:, 40:48], AF.Exp, scale=-0.5)
            y3 = ytm[:].rearrange("p (c f) -> p c f", f=64)
            kb.tt("dve", y3, y3, gst[:, 16:24].unsqueeze(2).to_broadcast([128, 8, 64]), ALU.subtract)
            kb.tt("dve", ynb[:].rearrange("p (c f) -> p c f", f=64), y3, gst[:, 40:48].unsqueeze(2).to_broadcast([128, 8, 64]), ALU.mult)
            yield
            for cc in cc8():
                for h in range(2):
                    sl = HS[h]
                    kb.tr(PT[sl, 512 + cc * 64:512 + (cc + 1) * 64], ynb[sl, cc * 64:(cc + 1) * 64], ident_b[sl, sl])
            kb.act(ygf[:], PT[:, 512:1024], AF.Identity, scale=PC["ln_x_w"](hp), bias=PC["ln_x_b"](hp))
            kb.tt("pool", ygf[:], ygf[:], o["bonus"][:], ALU.add)
            pg = bk2()
            kb.mm(pg[:, :], g2b1[:, hc], sg1b[:, ts_], start=True, stop=False)
            kb.mm(pg[:, :], g2b2[0:32, hc], sg2b[0:32, ts_], start=False, stop=True)
            kb.tt("dve", mixT[:, 4 + hp, ts_], ygf[:], pg[:, :], ALU.mult)
            yield

        def drain(gen, n=None):
            k = 0
            if gen is None:
                return False
            while n is None or k < n:
                try:
                    next(gen)
                except StopIteration:
                    return False
                k += 1
            return True

        NU = 16
        g1 = {u: S1(u) for u in range(NU)}
        g2 = {u: S2(u) for u in range(NU)}
        drain(g1[0]); drain(g2[0]); drain(g1[1])
        for u in range(NU):
            g3 = S3(u)
            a2 = g2.get(u + 1)
            a1 = g1.get(u + 2)
            alive3 = True
            alive2 = a2 is not None
            alive1 = a1 is not None
            while alive3 or alive2 or alive1:
                if alive3:
                    alive3 = drain(g3, 1)
                if alive2:
                    alive2 = drain(a2, 3)
                if alive1:
                    alive1 = drain(a1, 1)
        dbg("yr", mixT[:, 4, :], [128, T])

    def tail_phase():
        ar = Arena(AR)
        H = ar.t("H", [128, 16, D])
        xnb2 = [ar.t("xnb2_%d" % i, [128, D], BF16) for i in range(2)]
        actT = ar.t("actT", [128, 4, T], BF16)
        rls = [ar.t("rl%d" % i, [128, 512]) for i in range(2)]
        wo_b = ar.t("wo_b", [128, 8, D], BF16)
        wup_l = [kb.sb("wup_b0", [128, 8, 512], BF16, at=kb.off_of["wo_b"]), ar.t("wup_b1", [128, 8, 512], BF16)]
        wdn_l = [kb.sb("wdn_b0", [128, 4, D], BF16, at=kb.off_of["wo_b"] + 8192), ar.t("wdn_b1", [128, 4, D], BF16)]
        kb.pe_warm = True
        load_w(wo_b, I["w_out"], 8, D)
        for tt in range(16):
            xb_ = xst[tt % 2]
            kb.dma(xb_[:], I["x"][tt * 128:(tt + 1) * 128, :])
            for dh in range(2):
                ps = bank()
                with Collect():
                    for c in range(8):
                        kb.mm(ps[:, :], mixT[:, c, tt * 128:(tt + 1) * 128], wo_b[:, c, dh * 512:(dh + 1) * 512], start=(c == 0), stop=(c == 7))
                hs = H[:, tt, dh * 512:(dh + 1) * 512]
                kb.tt("dve", hs, ps[:, :], xb_[:, dh * 512:(dh + 1) * 512], ALU.add)
        dbg("h1", H[:, 0, :], [128, D])
        if 'hn' not in skip:
            norm_T(lambda tt: H[:, tt, :], gffn_T, xnT, 16, actT[:, 0, 0:D], xnb2)
        for g in range(0 if 'ffn' in skip else 8):
            wup_b = wup_l[(g + 1) % 2]
            wdn_b = wdn_l[(g + 1) % 2]
            load_w(wup_b, I["w_up"][:, g * 512:(g + 1) * 512], 8, 512)
            load_w(wdn_b, I["w_down"][g * 512:(g + 1) * 512, :], 4, D)
            for fc in range(4):
                for n in range(4):
                    ns = slice(n * 512, (n + 1) * 512)
                    ps = bank()
                    with Collect():
                        for c in range(8):
                            kb.mm(ps[:, :], wup_b[:, c, fc * 128:(fc + 1) * 128], xnT[:, c, ns], start=(c == 0), stop=(c == 7))
                    rl_ = rls[(fc * 4 + n) % 2]
                    kb.act(rl_[:], ps[:, :], AF.Relu)
                    kb.act(actT[:, fc, ns], rl_[:], AF.Square)
            for tt in range(16):
                for dh in range(2):
                    ps = bank()
                    with Collect():
                        for fc in range(4):
                            kb.mm(ps[:, :], actT[:, fc, tt * 128:(tt + 1) * 128], wdn_b[:, fc, dh * 512:(dh + 1) * 512], start=(fc == 0), stop=(fc == 3))
                    hs = H[:, tt, dh * 512:(dh + 1) * 512]
                    kb.tt("dve", hs, hs, ps[:, :], ALU.add)
        for tt in range(16):
            ss = stat[:, 32 + tt:33 + tt]
            kb.stt(actT[:, tt % 4, 0:D], H[:, tt, :], 1.0, H[:, tt, :], ALU.mult, ALU.mult, accum_out=ss)
            kb.act(ss, ss, AF.Sqrt, bias=cE, scale=1.0 / D)
            kb.recip(ss, ss)
            jb_ = xst[tt % 2]
            kb.act(jb_[:], H[:, tt, :], AF.Copy, scale=ss)
            kb.tt("dve", H[:, tt, :], jb_[:], gF[:], ALU.mult)
            kb.dma(out_d[tt * 128:(tt + 1) * 128, :], H[:, tt, :], final=True)

    if "s5" not in skip:
        kb.phase = 'S5'
        s5_phase()
    else:
        kb.memset("pool", mixT[:, 0:4, :], 0.0)
    if "rwkv" not in skip:
        kb.phase = 'RWKV'
        rwkv_phase()
    else:
        kb.memset("pool", mixT[:, 4:8, :], 0.0)
    if 'tail' not in skip:
        kb.phase = 'TAIL'
        tail_phase()
    if 'nosched' not in skip:
        est = kb.schedule()
    kb.emit()
    return nc, dbg_out


_CACHE = {}


def _prep_inputs(inputs):
    sq = {}
    for k, shp in IN_SHAPES.items():
        if k == "x":
            continue
        a = np.asarray(inputs[k], dtype=np.float32)
        sq[k] = np.ascontiguousarray(a.reshape(shp))
    return sq


def kernel(**inputs):
    x = np.asarray(inputs["x"], dtype=np.float32)
    if "nc" not in _CACHE:
        _CACHE["nc"] = build()[0]
    nc = _CACHE["nc"]
    shared = _prep_inputs(inputs)
    in_maps = []
    for b in range(8):
        m = dict(shared)
        m["x"] = np.ascontiguousarray(x[b])
        in_maps.append(m)
    res = run_bass_kernel_spmd(nc, in_maps, core_ids=list(range(8)))
    out = np.stack([np.asarray(r["out"], dtype=np.float32) for r in res.results], axis=0)
    return out.astype(inputs["x"].dtype if hasattr(inputs["x"], "dtype") else np.float32)
```

```python
import contextlib
import math
import numpy as np
import concourse.bass as bass
import concourse.mybir as mybir
from concourse.bass_utils import run_bass_kernel_spmd

F32 = mybir.dt.float32
BF16 = mybir.dt.bfloat16
I32 = mybir.dt.int32
AF = mybir.ActivationFunctionType
ALU = mybir.AluOpType
AX = mybir.AxisListType

T = 2048
D = 1024
DIN = 2336
DFF = 4096
NORM_EPS = 1e-6
GN_EPS = 64e-5
C0 = math.exp(-0.5)
TWO_PI = 2.0 * math.pi
MAGIC = 12582912.0

ENGS = ("pe", "act", "dve", "pool", "sp")
N_DMA_SEMS = 24
DMA_POOLS = {"sp": (0, 14), "pool": (14, 8), "act": (22, 2)}
STRICT_SAME_ENGINE = True
ESZ = {F32: 4, BF16: 2, I32: 4}


class Op:
    __slots__ = ("eng", "fn", "deps", "is_dma", "dma_sem", "dma_val", "signaled", "sig_idx", "all_deps", "cost", "lat", "prio", "idx", "fin", "nsucc", "succ", "phase")

    def __init__(self, eng, fn, is_dma=False):
        self.eng = eng
        self.fn = fn
        self.deps = []
        self.is_dma = is_dma
        self.dma_sem = None
        self.dma_val = None
        self.signaled = False
        self.sig_idx = None
        self.all_deps = []
        self.cost = 0.2
        self.lat = 0.0
        self.prio = 0.0
        self.idx = 0
        self.fin = 0.0
        self.succ = []


class KB:
    def __init__(self, nc):
        self.nc = nc
        self.prog = {e: [] for e in ENGS}
        self.tinfo = {}
        self.recs = {}
        self.dma_rr = 0
        self.dma_rrs = {}
        self.dma_counts = [0] * N_DMA_SEMS
        self.final_dmas = []
        self.sb_off = 16640
        self.off_of = {}
        self.n_ops = 0
        self.all_ops = []
        self.bank_last = {}
        self.phase = 'A'

    def sb(self, name, shape, dtype=F32, at=None):
        fe = 1
        for s in shape[1:]:
            fe *= s
        nbytes = fe * ESZ[dtype]
        if at is None:
            at = self.sb_off
            self.sb_off += (nbytes + 63) // 64 * 64
        t = self.nc.alloc_sbuf_tensor_at(name, list(shape), dtype, offset=at)
        self.tinfo[t.name] = ("SB", at, fe, ESZ[dtype])
        self.off_of[name] = at
        return t

    def ps(self, name, shape, dtype=F32):
        fe = 1
        for s in shape[1:]:
            fe *= s
        t = self.nc.alloc_psum_tensor(name, list(shape), dtype)
        self.tinfo[t.name] = ("PS_" + name, 0, fe, ESZ[dtype])
        return t

    def rect(self, ap):
        info = self.tinfo.get(ap.tensor.name)
        if info is None:
            return None
        space, base, fe, es = info
        off = ap.offset
        pat = ap.ap
        p0 = off // fe
        f0 = off % fe
        pc = pat[0][1]
        ext = 1
        for s, c in pat[1:]:
            ext += (c - 1) * abs(s)
        return (space, p0, p0 + pc, base + f0 * es, base + (f0 + ext) * es)

    def op(self, eng, fn, outs=(), ins=(), is_dma=False, cost=None, extra_deps=()):
        o = Op(eng, fn, is_dma)
        deps = [d_ for d_ in extra_deps if d_ is not None]
        for ap in ins:
            if ap is None or isinstance(ap, (int, float)):
                continue
            r = self.rect(ap)
            if r is None:
                continue
            for rec in self.recs.setdefault(r[0], []):
                if rec[0] < r[2] and r[1] < rec[1] and rec[2] < r[4] and r[3] < rec[3]:
                    if rec[4] is not None:
                        deps.append(rec[4])
                    rec[5].append(o)
        for ap in outs:
            r = self.rect(ap)
            if r is None:
                continue
            lst = self.recs.setdefault(r[0], [])
            keep = []
            for rec in lst:
                if rec[0] < r[2] and r[1] < rec[1] and rec[2] < r[4] and r[3] < rec[3]:
                    if rec[4] is not None:
                        deps.append(rec[4])
                    deps.extend(rec[5])
                    if r[1] <= rec[0] and rec[1] <= r[2] and r[3] <= rec[2] and rec[3] <= r[4]:
                        continue
                keep.append(rec)
            keep.append([r[1], r[2], r[3], r[4], o, []])
            self.recs[r[0]] = keep
        seen = set()
        for d in deps:
            if d is o or id(d) in seen:
                continue
            seen.add(id(d))
            o.all_deps.append(d)
            if d.eng == eng and not d.is_dma:
                if eng == "pe" or not STRICT_SAME_ENGINE:
                    continue
            o.deps.append(d)
        n_el = 1
        ref = None
        for ap in list(outs) + list(ins):
            if ap is None or isinstance(ap, (int, float)):
                continue
            ref = ap
            break
        if ref is not None:
            for d_ in ref.shape[1:]:
                n_el *= d_
        if cost is not None:
            o.cost = cost
        elif is_dma:
            o.cost = 0.06
            o.lat = 2.0 + n_el * ref.shape[0] * 4 / 150e3
        elif eng == "pe":
            o.cost = max(0.055, n_el / 1200.0) if cost is None else cost
        elif eng == "act":
            o.cost = 0.2 + n_el / 1200.0
        elif eng == "dve":
            o.cost = 0.1 + n_el / 960.0
        elif eng == "pool":
            o.cost = 0.12 + n_el / 300.0
        o.idx = self.n_ops
        o.phase = self.phase
        self.all_ops.append(o)
        self.prog[eng].append(o)
        self.n_ops += 1
        return o

    def schedule(self):
        import heapq
        ops = self.all_ops
        for o in ops:
            o.succ = []
        for o in ops:
            for d in o.all_deps:
                d.succ.append(o)
        for o in reversed(ops):
            m = 0.0
            for s_ in o.succ:
                if s_.prio > m:
                    m = s_.prio
            o.prio = o.cost + o.lat + m
        indeg = {id(o): len(o.all_deps) for o in ops}
        ready = {e: [] for e in ENGS}
        for o in ops:
            if indeg[id(o)] == 0:
                heapq.heappush(ready[o.eng], (-o.prio, o.idx, o))
        free_at = {e: 0.0 for e in ENGS}
        running = []
        relc = [0]
        XLAT = 0.08
        order = {e: [] for e in ENGS}
        now = 0.0
        done = 0
        N = len(ops)
        while done < N:
            progressed = False
            for e in ENGS:
                if free_at[e] <= now and ready[e]:
                    _, _, o = heapq.heappop(ready[e])
                    order[e].append(o)
                    free_at[e] = now + o.cost
                    o.fin = now + o.cost + o.lat
                    heapq.heappush(running, (o.fin, o.idx, o))
                    progressed = True
            if progressed:
                continue
            cands = [free_at[e] for e in ENGS if free_at[e] > now and ready[e]]
            if running:
                cands.append(running[0][0])
            assert cands, "scheduler stuck"
            now = min(cands)
            while running and running[0][0] <= now + 1e-12:
                _, _, o = heapq.heappop(running)
                if o.phase == "__rel__":
                    s_ = o.succ[0]
                    heapq.heappush(ready[s_.eng], (-s_.prio, s_.idx, s_))
                    continue
                done += 1
                for s_ in o.succ:
                    indeg[id(s_)] -= 1
                    if indeg[id(s_)] == 0:
                        if XLAT > 0 and s_.eng != o.eng:
                            r_ = Op("x", None)
                            r_.phase = "__rel__"
                            r_.succ = [s_]
                            relc[0] += 1
                            heapq.heappush(running, (now + XLAT, -relc[0], r_))
                        else:
                            heapq.heappush(ready[s_.eng], (-s_.prio, s_.idx, s_))
        self.prog = order
        self.est_makespan = now
        tot = {e: sum(o.cost for o in order[e]) for e in ENGS}
        ph = {}
        for o in ops:
            d_ = ph.setdefault(o.phase, {'t0': 1e9, 't1': 0.0, 'pe': 0.0, 'act': 0.0, 'dve': 0.0, 'pool': 0.0, 'sp': 0.0, 'n': 0})
            d_['t0'] = min(d_['t0'], o.fin - o.cost - o.lat); d_['t1'] = max(d_['t1'], o.fin); d_[o.eng] += o.cost; d_['n'] += 1
        for k_, d_ in ph.items():
            pass
        return now

    def dma(self, out, in_, eng="sp", final=False, slow=False):
        kw = {"allow_slow_non_contiguous": True} if slow else {}
        o = self.op(eng, lambda e: e.dma_start(out=out, in_=in_, **kw), outs=[out], ins=[in_], is_dma=True)
        if final:
            self.final_dmas.append(o)
        return o

    def act(self, out, in_, func, bias=0.0, scale=1.0, eng="act"):
        return self.op(eng, lambda e: e.activation(out=out, in_=in_, func=func, bias=bias, scale=scale),
                       outs=[out], ins=[in_, bias, scale])

    def tt(self, eng, out, a, b, op):
        return self.op(eng, lambda e: e.tensor_tensor(out=out, in0=a, in1=b, op=op), outs=[out], ins=[a, b])

    def ts(self, eng, out, a, s1, op0, s2=None, op1=None):
        if op1 is None:
            return self.op(eng, lambda e: e.tensor_scalar(out=out, in0=a, scalar1=s1, scalar2=None, op0=op0),
                           outs=[out], ins=[a, s1])
        return self.op(eng, lambda e: e.tensor_scalar(out=out, in0=a, scalar1=s1, scalar2=s2, op0=op0, op1=op1),
                       outs=[out], ins=[a, s1, s2])

    def stt(self, out, in0, scalar, in1, op0, op1, accum_out=None):
        if accum_out is None:
            return self.op("dve", lambda e: e.scalar_tensor_tensor(out=out, in0=in0, scalar=scalar, in1=in1, op0=op0, op1=op1),
                           outs=[out], ins=[in0, scalar, in1])
        return self.op("dve", lambda e: e.scalar_tensor_tensor(out=out, in0=in0, scalar=scalar, in1=in1, op0=op0, op1=op1, accum_out=accum_out),
                       outs=[out, accum_out], ins=[in0, scalar, in1])

    def copy(self, eng, out, in_):
        if eng == "act":
            return self.op(eng, lambda e: e.copy(out=out, in_=in_), outs=[out], ins=[in_])
        return self.op(eng, lambda e: e.tensor_copy(out=out, in_=in_), outs=[out], ins=[in_])

    def memset(self, eng, out, val):
        return self.op(eng, lambda e: e.memset(out, val), outs=[out])

    def mm(self, out, lhsT, rhs, start=True, stop=True, tile_position=None):
        if tile_position is None:
            bp = self.rect(lhsT)[1]
            if bp == 96:
                tile_position = (96, self.rect(out)[1])
        if tile_position is None:
            fn = lambda e: e.matmul(out, lhsT=lhsT, rhs=rhs, start=start, stop=stop)
        else:
            fn = lambda e: e.matmul(out, lhsT=lhsT, rhs=rhs, start=start, stop=stop, tile_position=tile_position)
        n_ = 1
        for d_ in rhs.shape[1:]:
            n_ *= d_
        c_ = max(0.055, n_ / (2300.0 if getattr(self, 'pe_warm', False) else 1200.0)) * (4.0 if rhs.dtype == F32 else 1.0)
        bkey = out.tensor.name
        xd = [self.bank_last.get(bkey)] if start else []
        o_ = self.op("pe", fn, outs=[out], ins=[lhsT, rhs] + ([] if start else [out]), cost=c_, extra_deps=xd)
        self.bank_last[bkey] = o_
        return o_

    def mmg(self, items):
        fns = []
        outs, ins, banks_start, banks_all = [], [], [], []
        cost = 0.0
        for it in items:
            out = it["out"]
            bkey = out.tensor.name
            if bkey not in banks_all:
                banks_all.append(bkey)
            if it.get("kind", "mm") == "tr":
                in_, ident = it["in_"], it["ident"]
                fns.append(lambda e, out=out, in_=in_, ident=ident: e.transpose(out=out, in_=in_, identity=ident))
                ins += [in_, ident]
                if bkey not in banks_start:
                    banks_start.append(bkey)
                n_ = in_.shape[-1]
                cost += max(0.03, n_ / 2400.0)
            else:
                lhsT, rhs = it["lhsT"], it["rhs"]
                start, stop = it.get("start", True), it.get("stop", True)
                tp = it.get("tile_position")
                if tp is None and self.rect(lhsT)[1] == 96:
                    tp = (96, self.rect(out)[1])
                if tp is None:
                    fns.append(lambda e, out=out, lhsT=lhsT, rhs=rhs, start=start, stop=stop: e.matmul(out, lhsT=lhsT, rhs=rhs, start=start, stop=stop))
                else:
                    fns.append(lambda e, out=out, lhsT=lhsT, rhs=rhs, start=start, stop=stop, tp=tp: e.matmul(out, lhsT=lhsT, rhs=rhs, start=start, stop=stop, tile_position=tp))
                ins += [lhsT, rhs]
                if start and bkey not in banks_start:
                    banks_start.append(bkey)
                n_ = 1
                for d_ in rhs.shape[1:]:
                    n_ *= d_
                cost += max(0.03, n_ / (2300.0 if getattr(self, 'pe_warm', False) else 1200.0) * (0.6 if n_ <= 128 else 1.0)) * (4.0 if rhs.dtype == F32 else 1.0)
            outs.append(out)

        def fn(e):
            r_ = None
            for f_ in fns:
                r_ = f_(e)
            return r_
        xd = [self.bank_last.get(b_) for b_ in banks_start]
        o_ = self.op("pe", fn, outs=outs, ins=ins, cost=cost, extra_deps=xd)
        for b_ in banks_all:
            self.bank_last[b_] = o_
        return o_

    def tr(self, out, in_, ident):
        bkey = out.tensor.name
        o_ = self.op("pe", lambda e: e.transpose(out=out, in_=in_, identity=ident), outs=[out], ins=[in_, ident],
                     extra_deps=[self.bank_last.get(bkey)])
        self.bank_last[bkey] = o_
        return o_

    def recip(self, out, in_):
        return self.op("dve", lambda e: e.reciprocal(out=out, in_=in_), outs=[out], ins=[in_])

    def scan(self, out, d0, d1, initial):
        return self.op("dve", lambda e: e.tensor_tensor_scan(out=out, data0=d0, data1=d1, initial=initial, op0=ALU.mult, op1=ALU.add),
                       outs=[out], ins=[d0, d1, initial])

    def reduce(self, out, in_, axis=None):
        return self.op("dve", lambda e: e.tensor_reduce(out=out, in_=in_, axis=AX.X, op=ALU.add), outs=[out], ins=[in_])

    def emit(self):
        nc = self.nc
        rr = {}
        cnt = [0] * N_DMA_SEMS
        for e in ENGS:
            for o in self.prog[e]:
                if o.is_dma:
                    lo, n = DMA_POOLS[e]
                    k = lo + rr.get(e, 0)
                    rr[e] = (rr.get(e, 0) + 1) % n
                    cnt[k] += 1
                    o.dma_sem = k
                    o.dma_val = 16 * cnt[k]
        for e in ENGS:
            for o in self.prog[e]:
                for d in o.deps:
                    if not d.is_dma:
                        d.signaled = True
        for e in ENGS:
            c = 0
            for o in self.prog[e]:
                if o.signaled and not o.is_dma:
                    c += 1
                    o.sig_idx = c
        with contextlib.ExitStack() as st:
            sems = {e: st.enter_context(nc.semaphore("s_" + e)) for e in ENGS}
            dsems = [st.enter_context(nc.semaphore("d%d" % i)) for i in range(N_DMA_SEMS)]
            block = st.enter_context(nc.Block())
            engobj = {"pe": block.tensor, "act": block.scalar, "dve": block.vector,
                      "pool": block.gpsimd, "sp": block.sync}

            def make(e):
                def body(eng):
                    waited = {}
                    for o in self.prog[e]:
                        need = {}
                        for d in o.deps:
                            if d.is_dma:
                                key = ("d", d.dma_sem)
                                val = d.dma_val
                            else:
                                key = ("c", d.eng)
                                val = d.sig_idx
                            if waited.get(key, 0) >= val:
                                continue
                            if need.get(key, 0) < val:
                                need[key] = val
                        for key, val in need.items():
                            sem = dsems[key[1]] if key[0] == "d" else sems[key[1]]
                            eng.wait_ge(sem, val)
                            waited[key] = val
                        if o.is_dma and o.dma_val > 16:
                            key = ("d", o.dma_sem)
                            if waited.get(key, 0) < o.dma_val - 16:
                                eng.wait_ge(dsems[o.dma_sem], o.dma_val - 16)
                                waited[key] = o.dma_val - 16
                        ins = o.fn(eng)
                        if o.is_dma:
                            ins.then_inc(dsems[o.dma_sem], 16)
                        elif o.signaled:
                            ins.then_inc(sems[e], 1)
                    if e == "sp":
                        need = {}
                        for d in self.final_dmas:
                            need[d.dma_sem] = max(need.get(d.dma_sem, 0), d.dma_val)
                        for k, v in need.items():
                            eng.wait_ge(dsems[k], v)
                return body

            for e in ENGS:
                engobj[e](make(e))


IN_SHAPES = {
    "x": [T, D], "norm_mix_g": [D], "w_in": [D, DIN], "lam_re": [32, 64], "lam_im": [32, 64],
    "log_dt": [1, 32], "b_re": [32, 64, 16], "b_im": [32, 64, 16], "c_re": [32, 16, 64], "c_im": [32, 16, 64],
    "d_skip": [512], "w_glu": [512, 512], "b_glu": [512], "s5_out_g": [512], "mu_shift": [1824],
    "w0": [512], "w2": [64, 512], "a0": [512], "a2": [64, 512], "g2": [160, 512], "k_k": [512], "k_a": [512],
    "r_k": [512], "ln_x_w": [512], "ln_x_b": [512], "w_out": [D, D], "norm_ffn_g": [D],
    "w_up": [D, DFF], "w_down": [DFF, D], "norm_final_g": [1, D],
}


def build(debug=(), skip=()):
    nc = bass.Bass("TRN2", target_bir_lowering=False)
    kb = KB(nc)
    I = {k: nc.dram_tensor(k, v, F32, kind="ExternalInput").ap() for k, v in IN_SHAPES.items()}
    out_d = nc.dram_tensor("out", [T, D], F32, kind="ExternalOutput").ap()
    dbg_out = {}

    def dbg(name, ap, shape):
        if name not in debug:
            return
        d = nc.dram_tensor("dbg_" + name, list(shape), ap.dtype, kind="ExternalOutput").ap()
        dbg_out[name] = d
        kb.dma(d, ap, final=True)

    ident_f = kb.sb("ident_f", [128, 128]); ident_b = kb.sb("ident_b", [128, 128], BF16)
    ones_b = kb.sb("ones_b", [128, 128], BF16); blk_b = kb.sb("blk_b", [128, 128], BF16)
    m_su = kb.sb("m_su", [128, 8, 64], BF16); m_iu = kb.sb("m_iu", [128, 8, 64], BF16)
    m_sl = kb.sb("m_sl", [128, 8, 64], BF16); m_id = kb.sb("m_id", [128, 8, 64], BF16)
    pcol = kb.sb("pcol", [128, 96])
    gF = kb.sb("gF", [128, D])
    stat = kb.sb("stat", [128, 64])
    constc = kb.sb("constc", [128, 8])
    kb.memset("pool", constc[:, 0:1], NORM_EPS)
    kb.memset("pool", constc[:, 1:2], 1.0)
    kb.memset("pool", constc[:, 2:3], GN_EPS)
    cE = constc[:, 0:1]; c1c = constc[:, 1:2]; cG = constc[:, 2:3]
    xnT = kb.sb("xnT", [128, 8, T], BF16)
    mixT = kb.sb("mixT", [128, 8, T], BF16)
    wct = [kb.sb("wct%d" % i, [128, 8, 128], BF16) for i in range(2)]
    xst = [kb.sb("xst%d" % i, [128, D]) for i in range(2)]
    AR = kb.sb_off
    AR_END = 223 * 1024
    assert AR < 108 * 1024, AR

    class Arena:
        def __init__(self, start):
            self.cur = start

        def t(self, name, shape, dt=F32):
            fe = 1
            for s_ in shape[1:]:
                fe *= s_
            t_ = kb.sb(name, shape, dt, at=self.cur)
            self.cur += (fe * ESZ[dt] + 63) // 64 * 64
            assert self.cur <= AR_END, (name, self.cur)
            return t_

    PB = [kb.ps("pb%d" % i, [128, 512]) for i in range(5)]
    PH = kb.ps("pH", [128, 512])
    PH2 = kb.ps("pH2", [128, 512])
    PHs = [PH, PH2]
    PT = kb.ps("pT", [128, 1024], BF16)
    pbi = [0]

    def bank():
        b = PB[pbi[0] % 5]
        pbi[0] += 1
        return b

    wsti = [0]

    def stage():
        b = wst[wsti[0] % 2]
        wsti[0] += 1
        return b

    arA = Arena(AR)
    junk = arA.t("junk", [128, D])
    xnb = [arA.t("xnb%d" % i, [128, D], BF16) for i in range(2)]
    xs_t = [arA.t("xs%d" % i, [128, D]) for i in range(3)]
    onesf = arA.t("onesf", [128, 512])
    tmpf = arA.t("tmpf", [128, 512])
    kb.memset("pool", onesf[:], 1.0)
    kb.op("pool", lambda e: e.affine_select(out=ident_f[:], in_=onesf[:, 0:128], pattern=[[-1, 128]], compare_op=ALU.is_equal, fill=0.0, base=0, channel_multiplier=1), outs=[ident_f[:]], ins=[onesf[:]])
    kb.copy("pool", ident_b[:], ident_f[:])
    kb.copy("pool", ones_b[:], onesf[:, 0:128])
    kb.memset("pool", blk_b[:], 0.0)
    kb.memset("pool", blk_b[0:64, 0:64], 1.0)
    kb.memset("pool", blk_b[64:128, 64:128], 1.0)
    o3 = onesf[:].rearrange("p (a b) -> p a b", a=8)
    t3 = tmpf[:].rearrange("p (a b) -> p a b", a=8)

    def mk_mask(dst, cm, base, cmp):
        for h in range(2):
            sl = slice(64 * h, 64 * h + 64)
            kb.op("pool", lambda e, sl=sl, h=h: e.affine_select(out=t3[sl], in_=o3[sl], pattern=[[0, 8], [-cm, 64]], compare_op=cmp, fill=0.0, base=base, channel_multiplier=cm),
                  outs=[t3[sl]], ins=[o3[sl]])
        kb.copy("pool", dst[:], t3)

    mk_mask(m_su, -1, -1, ALU.is_ge)
    mk_mask(m_iu, -1, 0, ALU.is_ge)
    mk_mask(m_sl, 1, -1, ALU.is_ge)
    mk_mask(m_id, 1, 0, ALU.is_equal)

    pc_i = [0]
    PC = {}

    def col(name, ap1d, n):
        c = n // 128
        c0 = pc_i[0]
        pc_i[0] += c
        kb.dma(pcol[:, c0:c0 + c], ap1d.rearrange("(c p) -> p c", p=128), slow=True)
        PC[name] = lambda j, c0=c0: pcol[:, c0 + j:c0 + j + 1]

    col("g_mix", I["norm_mix_g"], 1024)
    col("g_ffn", I["norm_ffn_g"], 1024)
    for nm in ("d_skip", "b_glu", "s5_out_g", "w0", "a0", "k_k", "k_a", "r_k", "ln_x_w", "ln_x_b"):
        col(nm, I[nm], 512)
    col("mu_rkv", I["mu_shift"][0:1536], 1536)
    col("mu_l", I["mu_shift"][1536:1792], 256)
    omu_c0 = pc_i[0]
    pc_i[0] += 12
    mu_c0 = omu_c0 - 2 - 12
    kb.ts("dve", pcol[:, omu_c0:omu_c0 + 12], pcol[:, mu_c0:mu_c0 + 12], -1.0, ALU.mult, 1.0, ALU.add)
    PC["omu_rkv"] = lambda j, c0=omu_c0: pcol[:, c0 + j:c0 + j + 1]
    mu_g2 = pcol[0:32, pc_i[0]:pc_i[0] + 1]
    kb.dma(mu_g2, I["mu_shift"][1792:1824].rearrange("(c p) -> p c", p=32), slow=True)
    pc_i[0] += 1
    assert pc_i[0] <= 96
    gmix_T = pcol[:, 0:8]
    gffn_T = pcol[:, 8:16]
    kb.dma(gF[:], I["norm_final_g"].partition_broadcast(128))

    class Collect:
        def __enter__(self):
            self.items = []
            self.om, self.ot = kb.mm, kb.tr
            kb.mm = lambda out, lhsT, rhs, start=True, stop=True, tile_position=None: self.items.append(
                dict(out=out, lhsT=lhsT, rhs=rhs, start=start, stop=stop, tile_position=tile_position))
            kb.tr = lambda out, in_, ident: self.items.append(dict(kind="tr", out=out, in_=in_, ident=ident))
            return self

        def __exit__(self, *a):
            kb.mm, kb.tr = self.om, self.ot
            if self.items:
                kb.mmg(self.items)
            return False


    def norm_T(src_fn, gT, dstT, ss_col0, junk_, xnb_):
        for tt in range(16):
            xs = src_fn(tt)
            ss = stat[:, ss_col0 + tt:ss_col0 + tt + 1]
            kb.stt(junk_[:], xs, 1.0, xs, ALU.mult, ALU.mult, accum_out=ss)
            kb.act(ss, ss, AF.Sqrt, bias=cE, scale=1.0 / D)
            kb.recip(ss, ss)
            xb = xnb_[tt % 2]
            kb.act(xb[:], xs, AF.Copy, scale=ss)
            with Collect():
                for c in range(8):
                    kb.tr(PT[:, c * 128:(c + 1) * 128], xb[:, c * 128:(c + 1) * 128], ident_b[:])
            kb.tt("dve", dstT[:, :, tt * 128:(tt + 1) * 128], PT[:].rearrange("p (c t) -> p c t", c=8),
                  gT.unsqueeze(2).to_broadcast([128, 8, 128]), ALU.mult)

    def x_src(tt):
        b = xs_t[tt % 3]
        kb.dma(b[:], I["x"][tt * 128:(tt + 1) * 128, :])
        return b[:]

    dbg('pcol', pcol[:], [128, 96]); dbg('gF', gF[:], [128, D]); dbg('msu', m_su[:].rearrange('p a b -> p (a b)'), [128, 512])
    if 'A' not in skip:
        norm_T(x_src, gmix_T, xnT, 0, junk, xnb)
    dbg('xnT', xnT[:, 0, :], [128, T])

    def load_w(dst_bf, src2d, kc, ncols, dst_f32=False):
        per = max(1, 4096 // ncols)
        k = 0
        while k < kc:
            n = min(per, kc - k)
            kb.dma(dst_bf[:, k:k + n, :], src2d[k * 128:(k + n) * 128, :].rearrange("(c p) n -> p c n", p=128), eng="pool")
            k += n

    wcti = [0]

    def proj(c0, M, evac, bank_fn=None):
        w = wct[wcti[0] % 2]
        wcti[0] += 1
        load_w(w[:, :, 0:M], I["w_in"][:, c0:c0 + M], 8, M)
        for n in range(4):
            ps = (bank_fn or bank)()
            with Collect():
                for c in range(8):
                    kb.mm(ps[0:M, :], w[:, c, 0:M], xnT[:, c, n * 512:(n + 1) * 512], start=(c == 0), stop=(c == 7))
            evac(n, ps[0:M, :])

    def s5_phase():
        ar = Arena(AR)
        CH = 8
        NCH = T // CH
        uT = ar.t("uT", [128, T])
        uTb = ar.t("uTb", [128, T], BF16)
        yacc = ar.t("yacc", [128, T])
        gb = ar.t("gb", [128, 4, T], BF16)
        sp_small = ar.t("sp_small", [128, 32, 16])
        Mre = ar.t("Mre", [128, 16, 32]); Mim = ar.t("Mim", [128, 16, 32]); Mimn = ar.t("Mimn", [128, 16, 32])
        BT0re = ar.t("BT0re", [128, 4, 128]); BT0im = ar.t("BT0im", [128, 4, 128])
        L1re = ar.t("L1re", [128, 4, 128]); L1im = ar.t("L1im", [128, 4, 128])
        CL0re = ar.t("CL0re", [128, 16, 32]); CL0im = ar.t("CL0im", [128, 16, 32])
        bm32 = ar.t("bm32", [128, 128])
        iota_c = ar.t("iota_c", [128, NCH])
        iota_ci = ar.t("iota_ci", [128, NCH], I32)
        wglu = ar.t("wglu", [128, 4, 512], BF16)
        tsets = []
        for i_ in range(2):
            tsets.append({"WBb": [ar.t("WBb_re%d" % i_, [128, CH, 128], BF16), ar.t("WBb_im%d" % i_, [128, CH, 128], BF16)],
                          "CLb": [ar.t("CLb_re%d" % i_, [128, CH + 1, 128], BF16), ar.t("CLb_in%d" % i_, [128, CH + 1, 128], BF16)],
                          "Kbd": ar.t("Kbd%d" % i_, [128, CH, 128], BF16)})
        wr = [ar.t("wr%d" % i, [128, 128]) for i in range(2)]
        wi = [ar.t("wi%d" % i, [128, 128]) for i in range(2)]
        cr = [ar.t("cr%d" % i, [128, 128]) for i in range(2)]
        ci = [ar.t("ci%d" % i, [128, 128]) for i in range(2)]
        tw = [ar.t("tw%d" % i, [128, 128]) for i in range(4)]
        tc_ = [ar.t("tc%d" % i, [128, 128]) for i in range(4)]
        Xs = [[ar.t("Xs%d_%d" % (q_, r_), [128, NCH + 1], BF16) for r_ in range(2)] for q_ in range(4)]
        ov_ = ar.cur
        csets = []
        for i_ in range(4):
            csets.append({nm: ar.t("%s_c%d" % (nm, i_), [128, NCH]) for nm in ("tcos", "tsin", "cre", "cim", "tm1", "tm2")})
        arg = Arena(ov_)
        ysb = arg.t("ysb", [128, 4, 512])
        sigt = arg.t("sigt", [128, 512])
        sqb = arg.t("sqb", [128, 512], BF16)
        rstd = arg.t("rstd", [128, 512])
        aro = Arena(ov_)
        bre = aro.t("bre", [128, 16, 16]); bim = aro.t("bim", [128, 16, 16])
        bbr = aro.t("bbr", [128, 16, 16]); bbi = aro.t("bbi", [128, 16, 16]); btmp = aro.t("btmp", [128, 16, 16])
        MLr = aro.t("MLr", [128, 16, 32]); MLi = aro.t("MLi", [128, 16, 32])
        cdr = aro.t("cdr", [128, 4, 2, 64]); cdi = aro.t("cdi", [128, 4, 2, 64])

        def sm(i):
            return sp_small[:, i, :]

        (lre, lim, ldt, dt_, th, rl, rho, f0, rn, fr, s1, sh, c1, lbr, lbi, den, inv, fre, fim, x1, x2,
         rho16, f16) = [sm(i) for i in range(23)]
        for two in range(2):
            sl = slice(64 * two, 64 * two + 64)
            kb.dma(lre[sl], I["lam_re"].rearrange("(k two) p -> two p k", two=2)[two], slow=True)
            kb.dma(lim[sl], I["lam_im"].rearrange("(k two) p -> two p k", two=2)[two], slow=True)
            kb.dma(ldt[sl], I["log_dt"].rearrange("o (k two) -> o two k", two=2)[:, two, :].partition_broadcast(64), slow=True)
            kb.dma(bre[sl], I["b_re"].rearrange("(k two) p h -> two p k h", two=2)[two])
            kb.dma(bim[sl], I["b_im"].rearrange("(k two) p h -> two p k h", two=2)[two])
        kb.act(dt_, ldt, AF.Exp)
        kb.tt("dve", th, lim, dt_, ALU.mult)
        kb.tt("dve", rl, lre, dt_, ALU.mult)
        kb.act(rho, rl, AF.Exp)
        kb.act(rho16, rl, AF.Exp, scale=float(CH))
        kb.ts("dve", f0, th, 1.0 / TWO_PI, ALU.mult)
        kb.ts("dve", f16, f0, float(CH), ALU.mult)
        kb.ts("dve", rn, f0, MAGIC, ALU.add, -MAGIC, ALU.add)
        kb.tt("dve", fr, f0, rn, ALU.subtract)
        kb.act(s1, fr, AF.Sin, scale=TWO_PI * 0.9999999)
        kb.act(sh, fr, AF.Sin, scale=math.pi * 0.9999999)
        kb.act(c1, sh, AF.Square, scale=math.sqrt(2.0))
        kb.ts("dve", c1, c1, -1.0, ALU.mult, 1.0, ALU.add)
        kb.tt("dve", lbr, rho, c1, ALU.mult)
        kb.tt("dve", lbi, rho, s1, ALU.mult)
        kb.ts("dve", x1, lbr, -1.0, ALU.add)
        kb.tt("dve", den, lre, lre, ALU.mult)
        kb.tt("dve", x2, lim, lim, ALU.mult)
        kb.tt("dve", den, den, x2, ALU.add)
        kb.recip(inv, den)
        kb.tt("dve", fre, x1, lre, ALU.mult)
        kb.tt("dve", x2, lbi, lim, ALU.mult)
        kb.tt("dve", fre, fre, x2, ALU.add)
        kb.tt("dve", fre, fre, inv, ALU.mult)
        kb.tt("dve", fim, lbi, lre, ALU.mult)
        kb.tt("dve", x2, x1, lim, ALU.mult)
        kb.tt("dve", fim, fim, x2, ALU.subtract)
        kb.tt("dve", fim, fim, inv, ALU.mult)
        freb = fre.unsqueeze(2).to_broadcast([128, 16, 16])
        fimb = fim.unsqueeze(2).to_broadcast([128, 16, 16])
        kb.tt("dve", bbr[:], bre[:], freb, ALU.mult)
        kb.tt("dve", btmp[:], bim[:], fimb, ALU.mult)
        kb.tt("dve", bbr[:], bbr[:], btmp[:], ALU.subtract)
        kb.tt("dve", bbi[:], bim[:], freb, ALU.mult)
        kb.tt("dve", btmp[:], bre[:], fimb, ALU.mult)
        kb.tt("dve", bbi[:], bbi[:], btmp[:], ALU.add)
        lbrb = lbr.unsqueeze(2).to_broadcast([128, 16, 16])
        lbib = lbi.unsqueeze(2).to_broadcast([128, 16, 16])
        for (M_, src_) in ((Mre, bbr[:]), (Mim, bbi[:]), (MLr, lbrb), (MLi, lbib)):
            kb.memset("pool", M_[:], 0.0)
            kb.copy("pool", M_[0:64, :, 0:16], src_[0:64])
            kb.copy("pool", M_[64:128, :, 16:32], src_[64:128])
        kb.ts("pool", Mimn[:], Mim[:], -1.0, ALU.mult)
        for (M_, BT) in ((Mre, BT0re), (Mim, BT0im), (MLr, L1re), (MLi, L1im)):
            for j in range(4):
                ps = bank()
                kb.tr(ps[:, 0:128], M_[:, 4 * j:4 * j + 4, :].rearrange("p a b -> p (a b)"), ident_f[:])
                kb.copy("act", BT[:, j, :], ps[:, 0:128])
        for (cd, nm) in ((cdr, "c_re"), (cdi, "c_im")):
            src = I[nm].rearrange("(j g) h p -> (g h) j p", g=8)
            kb.dma(cd[:, :, 0, :], src)
            kb.dma(cd[:, :, 1, :], src)
        for (cd, CL0) in ((cdr, CL0re), (cdi, CL0im)):
            kb.memset("pool", CL0[:], 0.0)
            for j in range(4):
                ps = bank()
                kb.tr(ps[:, 0:128], cd[:, j].rearrange("p a b -> p (a b)"), ident_f[:])
                pv = ps[:, 0:128].rearrange("p (q t h) -> p q t h", q=4, t=2)
                kb.copy("act", CL0[0:64, 4 * j:4 * j + 4, 0:16], pv[0:64, :, 0, :])
                kb.copy("act", CL0[64:128, 4 * j:4 * j + 4, 16:32], pv[64:128, :, 1, :])
        load_w(wglu, I["w_glu"], 4, 512)
        kb.op("pool", lambda e: e.iota(iota_ci[:], pattern=[[1, NCH]], base=0, channel_multiplier=0), outs=[iota_ci[:]])
        kb.copy("pool", iota_c[:], iota_ci[:])
        kb.memset("pool", tw[0][:], 1.0)
        kb.op("pool", lambda e: e.affine_select(out=tw[1][:].rearrange("p (a b) -> p a b", a=4), in_=tw[0][:].rearrange("p (a b) -> p a b", a=4), pattern=[[-32, 4], [0, 32]], compare_op=ALU.is_ge, fill=0.0, base=0, channel_multiplier=1),
              outs=[tw[1][:]], ins=[tw[0][:]])
        kb.op("pool", lambda e: e.affine_select(out=bm32[:].rearrange("p (a b) -> p a b", a=4), in_=tw[1][:].rearrange("p (a b) -> p a b", a=4), pattern=[[32, 4], [0, 32]], compare_op=ALU.is_ge, fill=0.0, base=31, channel_multiplier=-1),
              outs=[bm32[:]], ins=[tw[1][:]])
        for q_ in range(4):
            for r_ in range(2):
                kb.memset("pool", Xs[q_][r_][:, 0:1], 0.0)

        def cmul_step(a_r, a_i, b_r, b_i, o_r, o_i, t4):
            kb.tt("dve", t4[0][:], a_r, b_r, ALU.mult)
            kb.tt("pool", t4[1][:], a_i, b_i, ALU.mult)
            kb.tt("dve", t4[2][:], a_r, b_i, ALU.mult)
            kb.tt("pool", t4[3][:], a_i, b_r, ALU.mult)
            kb.tt("dve", o_r, t4[0][:], t4[1][:], ALU.subtract)
            kb.tt("pool", o_i, t4[2][:], t4[3][:], ALU.add)

        b7 = [PB[4], PH, PH2]
        b7i = [0]

        def bank7():
            b7i[0] += 1
            return b7[b7i[0] % 3]

        def pre_gen(j, S):
            WBb, CLb, Kbd = S["WBb"], S["CLb"], S["Kbd"]
            Mre_j = Mre[:, 4 * j:4 * j + 4, :].rearrange("p a b -> p (a b)")
            Mimn_j = Mimn[:, 4 * j:4 * j + 4, :].rearrange("p a b -> p (a b)")
            lamr_b = lbr[:, 4 * j:4 * j + 4].unsqueeze(2).to_broadcast([128, 4, 32])
            lami_b = lbi[:, 4 * j:4 * j + 4].unsqueeze(2).to_broadcast([128, 4, 32])
            kb.copy("pool", wr[0][:], BT0re[:, j, :])
            kb.copy("pool", wi[0][:], BT0im[:, j, :])
            kb.copy("pool", cr[0][:], CL0re[:, 4 * j:4 * j + 4, :].rearrange("p a b -> p (a b)"))
            kb.copy("pool", ci[0][:], CL0im[:, 4 * j:4 * j + 4, :].rearrange("p a b -> p (a b)"))
            c3 = lambda t_: t_[:].rearrange("p (a b) -> p a b", a=4)
            for n in range(CH + 1):
                a, b = n % 2, (n + 1) % 2
                if n < CH:
                    kb.copy("act", WBb[0][:, n, :], wr[a][:])
                    kb.copy("act", WBb[1][:, n, :], wi[a][:])
                kb.copy("act", CLb[0][:, n, :], cr[a][:])
                kb.act(CLb[1][:, n, :], ci[a][:], AF.Copy, scale=-1.0)
                if n < CH:
                    ps = bank7()
                    with Collect():
                        kb.mm(ps[:, 0:128], Mre_j, cr[a][:], start=True, stop=False)
                        kb.mm(ps[:, 0:128], Mimn_j, ci[a][:], start=False, stop=True)
                    kb.tt("dve", Kbd[:, n, :], ps[:, 0:128], bm32[:], ALU.mult)
                    if n < CH - 1:
                        cmul_step(wr[a][:], wi[a][:], L1re[:, j, :], L1im[:, j, :], wr[b][:], wi[b][:], tw)
                    kb.tt("dve", c3(tc_[0]), c3(cr[a]), lamr_b, ALU.mult)
                    kb.tt("pool", c3(tc_[1]), c3(ci[a]), lami_b, ALU.mult)
                    kb.tt("dve", c3(tc_[2]), c3(cr[a]), lami_b, ALU.mult)
                    kb.tt("pool", c3(tc_[3]), c3(ci[a]), lamr_b, ALU.mult)
                    kb.tt("dve", cr[b][:], tc_[0][:], tc_[1][:], ALU.subtract)
                    kb.tt("pool", ci[b][:], tc_[2][:], tc_[3][:], ALU.add)
                yield

        def pair_gen(j, q, S, B, u3):
            WBb = S["WBb"]
            k = 4 * j + q
            rs = slice(32 * q, 32 * q + 32)
            pr = PB[2 * (q % 2)][:, 0:NCH]
            pi = PB[2 * (q % 2) + 1][:, 0:NCH]
            with Collect():
                for i_ in range(CH):
                    kb.mm(pr, WBb[0][rs, CH - 1 - i_, :], u3[rs, :, i_], start=(i_ == 0), stop=(i_ == CH - 1))
            with Collect():
                for i_ in range(CH):
                    kb.mm(pi, WBb[1][rs, CH - 1 - i_, :], u3[rs, :, i_], start=(i_ == 0), stop=(i_ == CH - 1))
            f16k = f16[:, k:k + 1]
            tcos, tsin, cre, cim, tm1, tm2 = (B[x_] for x_ in ("tcos", "tsin", "cre", "cim", "tm1", "tm2"))
            kb.ts("pool", tm1[:], iota_c[:], f16k, ALU.mult, MAGIC, ALU.add)
            kb.ts("pool", tm1[:], tm1[:], -1.0, ALU.mult, MAGIC, ALU.add)
            kb.stt(tm1[:], iota_c[:], f16k, tm1[:], ALU.mult, ALU.add)
            yield
            kb.act(tsin[:], tm1[:], AF.Sin, scale=TWO_PI * 0.9999999)
            kb.act(tcos[:], tm1[:], AF.Sin, scale=math.pi * 0.9999999)
            kb.act(tcos[:], tcos[:], AF.Square, scale=math.sqrt(2.0))
            kb.act(tcos[:], tcos[:], AF.Identity, scale=-1.0, bias=c1c)
            yield
            kb.tt("dve", cre[:], pr, tcos[:], ALU.mult)
            kb.tt("dve", tm1[:], pi, tsin[:], ALU.mult)
            kb.tt("dve", cre[:], cre[:], tm1[:], ALU.add)
            kb.tt("dve", cim[:], pi, tcos[:], ALU.mult)
            kb.tt("dve", tm2[:], pr, tsin[:], ALU.mult)
            kb.tt("dve", cim[:], cim[:], tm2[:], ALU.subtract)
            yield
            r16 = rho16[:, k:k + 1].to_broadcast([128, NCH])
            kb.scan(cre[:], r16, cre[:], 0.0)
            kb.scan(cim[:], r16, cim[:], 0.0)
            yield
            kb.tt("dve", tm1[:], cre[:], tcos[:], ALU.mult)
            kb.tt("pool", tm2[:], cim[:], tsin[:], ALU.mult)
            kb.tt("dve", Xs[q][0][:, 1:NCH + 1], tm1[:], tm2[:], ALU.subtract)
            yield
            kb.tt("dve", tm1[:], cre[:], tsin[:], ALU.mult)
            kb.tt("pool", tm2[:], cim[:], tcos[:], ALU.mult)
            kb.tt("dve", Xs[q][1][:, 1:NCH + 1], tm1[:], tm2[:], ALU.add)
            yield

        def main_gen(j, S):
            CLb, Kbd = S["CLb"], S["Kbd"]

            upv = uTb[:, :].rearrange("p (i c) -> p c i", i=CH)

            def ev(n, ps):
                kb.copy("act", uT[:, n * 512:(n + 1) * 512], ps)
                cpc = 512 // CH
                kb.copy("act", upv[:, n * cpc:(n + 1) * cpc, :], ps.rearrange("p (c i) -> p c i", i=CH))
            proj(128 * j, 128, ev, bank_fn=bank7)
            if j == 0:
                dbg("uT", uT[:], [128, T])
            yield
            u3 = uTb[:, :].rearrange("p (i c) -> p c i", i=CH)
            for q0 in (0, 2):
                gens = [pair_gen(j, q0 + q_, S, csets[q0 + q_], u3) for q_ in range(2)]
                alive = [True] * 2
                while any(alive):
                    for gi_ in range(2):
                        if alive[gi_]:
                            try:
                                next(gens[gi_])
                            except StopIteration:
                                alive[gi_] = False
                    yield
            uf3 = uT[:, :].rearrange("p (c i) -> p c i", i=CH)
            y3 = yacc[:, :].rearrange("p (c i) -> p c i", i=CH)
            for jl in range(CH):
                acc = bank7()
                with Collect():
                    for tau in range(jl + 1):
                        kb.mm(acc[:, 0:NCH], Kbd[:, tau, :], u3[:, :, jl - tau], start=(tau == 0), stop=False)
                    for r_ in range(2):
                        for q in range(4):
                            kb.mm(acc[32 * q:32 * q + 32, 0:NCH], CLb[r_][:, jl + 1, 32 * q:32 * q + 32], Xs[q][r_][:, 0:NCH],
                                  start=False, stop=(r_ == 1), tile_position=(0, 32 * q))
                kb.stt(y3[:, :, jl], uf3[:, :, jl], PC["d_skip"](j), acc[:, 0:NCH], ALU.mult, ALU.add)
                yield
            kb.act(gb[:, j, :], yacc[:], AF.Gelu)
            yield

        pg = pre_gen(0, tsets[0])
        for _ in pg:
            pass
        for j in range(4):
            mg = main_gen(j, tsets[j % 2])
            pg = pre_gen(j + 1, tsets[(j + 1) % 2]) if j + 1 < 4 else None
            am, ap_ = True, pg is not None
            while am or ap_:
                if am:
                    try:
                        next(mg)
                    except StopIteration:
                        am = False
                if ap_:
                    try:
                        next(pg)
                    except StopIteration:
                        ap_ = False
        for n in range(4):
            ns = slice(n * 512, (n + 1) * 512)
            for jo in range(4):
                ps = bank()
                for ji in range(4):
                    kb.mm(ps[:, :], wglu[:, ji, jo * 128:(jo + 1) * 128], gb[:, ji, ns], start=(ji == 0), stop=(ji == 3))
                kb.act(sigt[:], ps[:, :], AF.Sigmoid, bias=PC["b_glu"](jo))
                kb.tt("dve", ysb[:, jo, :], gb[:, jo, ns], sigt[:], ALU.mult)
                kb.act(sqb[:], ysb[:, jo, :], AF.Square)
                kb.mm(PH[:, :], ones_b[:], sqb[:], start=(jo == 0), stop=(jo == 3))
            kb.act(rstd[:], PH[:, :], AF.Ln, bias=cE, scale=1.0 / 512.0)
            kb.act(rstd[:], rstd[:], AF.Exp, scale=-0.5)
            for jo in range(4):
                kb.stt(mixT[:, jo, ns], ysb[:, jo, :], PC["s5_out_g"](jo), rstd[:], ALU.mult, ALU.mult)
        dbg("ys5", mixT[:, 0, :], [128, T])

    def rwkv_phase():
        BL = 512
        HS = [slice(0, 64), slice(64, 128)]

        def cc8():
            with Collect():
                for cc_ in range(8):
                    yield cc_
        ar = Arena(AR)
        twza = ar.t("twza", [128, T], BF16)
        sg1b = ar.t("sg1b", [128, T], BF16)
        sg2b = ar.t("sg2b", [128, T], BF16)
        wa2b = ar.t("wa2b", [128, 512], BF16)
        g2b1 = ar.t("g2b1", [128, 512], BF16)
        g2b2 = ar.t("g2b2", [128, 512], BF16)
        resetm = ar.t("resetm", [128, BL], BF16)
        wtl = [[ar.t("w%s%d" % (nm, i), [128, 8, 128], BF16) for i in range(1)] for nm in "krv"]
        carry = ar.t("carry", [128, 4])
        S32 = ar.t("S32", [128, 64])
        Sb = [ar.t("Sb%d" % i, [128, 64], BF16) for i in range(2)]
        gst = ar.t("gst", [128, 64])
        zf = ar.t("zf", [128, 64])
        base_pipe = ar.cur
        arl = Arena(base_pipe)
        Rf = arl.t("Rf", [128, T + 1]); X4f = arl.t("X4f", [128, T]); X5f = arl.t("X5f", [128, T])
        s1o = []
        for i in range(3):
            d_ = {}
            for nm in ("At", "Bt", "Kt", "Rt", "BP", "KP", "vb"):
                d_[nm] = ar.t("s1_%s%d" % (nm, i), [128, BL], BF16)
            d_["PCc"] = ar.t("s1_PC%d" % i, [128, 8])
            d_["bonus"] = ar.t("s1_bo%d" % i, [128, BL])
            s1o.append(d_)
        s2o = []
        for i in range(2):
            d_ = {}
            for nm in ("BPt", "KPt", "Vt", "Att", "AK", "RB", "RK", "Z", "TAk", "AKV", "U0", "GT", "H0", "R2T"):
                d_[nm] = ar.t("s2_%s%d" % (nm, i), [128, 8, 64], BF16)
            s2o.append(d_)
        Rk = [ar.t("Rk%d" % i, [128, BL + 1]) for i in range(3)]
        X1 = ar.t("X1", [128, BL]); X2 = ar.t("X2", [128, BL]); X3 = ar.t("X3", [128, BL])
        X4 = ar.t("X4", [128, BL]); X5 = ar.t("X5", [128, BL]); X6 = ar.t("X6", [128, BL]); X7 = ar.t("X7", [128, BL])
        sqb = ar.t("rsqb", [128, BL], BF16)
        sqb2 = ar.t("rsqb2", [128, BL], BF16)
        Y2 = ar.t("Y2", [128, BL]); Y3 = ar.t("Y3", [128, BL]); CS = ar.t("CS", [128, BL])
        Ee = [ar.t("Ee%d" % i, [128, 512], BF16) for i in range(2)]
        Ff = [ar.t("Ff%d" % i, [128, 512], BF16) for i in range(2)]
        Zz = [ar.t("Zz%d" % i, [128, 512], BF16) for i in range(2)]
        Ubs = [ar.t("Ub%d" % i, [128, 64], BF16) for i in range(2)]
        sqf = ar.t("sqf", [128, 512]); ytm = ar.t("ytm", [128, 512]); ynb = ar.t("ynb", [128, 512], BF16)
        ygf = ar.t("ygf", [128, 512])

        bS1 = [PB[0], PB[1]]; bS2 = [PB[2], PB[3]]; bS3 = PB[4]
        ci = {"1": 0, "2": 0}

        def bk1():
            ci["1"] += 1
            return bS1[ci["1"] % 2]

        def bk2():
            ci["2"] += 1
            return bS2[ci["2"] % 2]

        kb.dma(X4f[0:64, 0:512], I["w2"])
        kb.dma(X4f[64:128, 0:512], I["a2"])
        kb.copy("pool", wa2b[:], X4f[:, 0:512])
        kb.dma(X5f[:, 0:512], I["g2"][0:128, :])
        kb.dma(X5f[0:32, 512:1024], I["g2"][128:160, :])
        kb.copy("pool", g2b1[:], X5f[:, 0:512])
        kb.copy("pool", g2b2[0:32, :], X5f[0:32, 512:1024])
        kb.memset("pool", resetm[:], 1.0)
        kb.memset("pool", resetm[:].rearrange("p (c j) -> p c j", j=64)[:, :, 0:1], 0.0)
        kb.memset("pool", zf[:], 0.0)

        def proj_shift_full(c0, M, mu_ap, dst):
            def ev(n, ps):
                kb.copy("act", Rf[0:M, 1 + n * 512:1 + (n + 1) * 512], ps)
            proj(c0, M, ev)
            kb.memset("pool", Rf[0:M, 0:1], 0.0)
            kb.tt("pool", X4f[0:M, :], Rf[0:M, 0:T], Rf[0:M, 1:T + 1], ALU.subtract)
            kb.stt(dst[0:M, :], X4f[0:M, :], mu_ap, Rf[0:M, 1:T + 1], ALU.mult, ALU.add)

        proj_shift_full(2048, 128, PC["mu_l"](0), X5f)
        kb.act(twza[0:64, :], X5f[0:64, :], AF.Tanh)
        kb.copy("act", twza[64:128, :], X5f[64:128, :])
        proj_shift_full(2176, 128, PC["mu_l"](1), X5f)
        kb.act(sg1b[:], X5f[:], AF.Sigmoid)
        proj_shift_full(2304, 32, mu_g2, X5f)
        kb.act(sg2b[0:32, :], X5f[0:32, :], AF.Sigmoid)

        def load_hp_weights(hp):
            for xi, c0 in enumerate((1024, 512, 1536)):
                load_w(wtl[xi][0], I["w_in"][:, c0 + 128 * hp:c0 + 128 * hp + 128], 8, 128)

        def S1(u):
            hp, blk = divmod(u, 4)
            o = s1o[u % 3]
            hc = slice(hp * 128, (hp + 1) * 128)
            ts_ = slice(blk * BL, (blk + 1) * BL)
            if blk == 0:
                load_hp_weights(hp)
            dsts = (X1, X3, X5)
            mi = (4 + hp, hp, 8 + hp)
            for xi in range(3):
                w = wtl[xi][0]
                R = Rk[xi]
                ps = bk1()
                with Collect():
                    for c in range(8):
                        kb.mm(ps[:, :], w[:, c, :], xnT[:, c, ts_], start=(c == 0), stop=(c == 7))
                kb.copy("act", R[:, 1:BL + 1], ps[:, :])
                kb.act(dsts[xi][:], ps[:, :], AF.Copy, scale=PC["omu_rkv"](mi[xi]))
                if blk == 0:
                    kb.memset("pool", R[:, 0:1], 0.0)
                else:
                    kb.copy("pool", R[:, 0:1], carry[:, xi:xi + 1])
                kb.copy("pool", carry[:, xi:xi + 1], R[:, BL:BL + 1])
                kb.stt(dsts[xi][:], R[:, 0:BL], PC["mu_rkv"](mi[xi]), dsts[xi][:], ALU.mult, ALU.add)
                yield
            ps = bk1()
            kb.mm(ps[:, :], wa2b[64:128, hc], twza[64:128, ts_])
            kb.act(X2[:], ps[:, :], AF.Sigmoid, bias=PC["a0"](hp))
            ps = bk1()
            kb.mm(ps[:, :], wa2b[0:64, hc], twza[0:64, ts_])
            kb.act(X6[:], ps[:, :], AF.Sigmoid, bias=PC["w0"](hp))
            yield
            kb.scan(CS[:], resetm[:], X6[:], 0.0)
            cs3 = CS[:].rearrange("p (c j) -> p c j", j=64)
            kb.act(X4[:], X1[:], AF.Copy, scale=PC["k_k"](hp))
            kb.act(sqb[:], X4[:], AF.Square)
            ps = bk1()
            kb.mm(ps[:, :], blk_b[:], sqb[:])
            kb.act(X7[:], ps[:, :], AF.Ln)
            kb.act(X7[:], X7[:], AF.Exp, scale=-0.5)
            kb.stt(X4[:], X7[:], 1e12, X4[:], ALU.min, ALU.mult)
            yield
            kb.ts("dve", Y2[:], X2[:], -1.0, ALU.add, PC["k_a"](hp), ALU.mult)
            kb.stt(X1[:], Y2[:], 1.0, X1[:], ALU.add, ALU.mult)
            kb.tt("dve", X2[:], X4[:], X2[:], ALU.mult)
            yield
            kb.tt("dve", X6[:], CS[:], X6[:], ALU.subtract)
            kb.act(X6[:], X6[:], AF.Exp, scale=-C0)
            kb.stt(o["At"][:], X4[:], -1.0, X6[:], ALU.mult, ALU.mult)
            yield
            kb.act(Y3[:], CS[:], AF.Exp, scale=C0)
            kb.tt("dve", o["Bt"][:], X2[:], Y3[:], ALU.mult)
            kb.tt("dve", o["Kt"][:], X1[:], Y3[:], ALU.mult)
            yield
            kb.tt("dve", Y2[:].rearrange("p (c j) -> p c j", j=64), cs3[:, :, 63:64].to_broadcast([128, 8, 64]), cs3, ALU.subtract)
            kb.act(Y2[:], Y2[:], AF.Exp, scale=-C0)
            kb.tt("dve", o["BP"][:], X2[:], Y2[:], ALU.mult)
            kb.tt("dve", o["KP"][:], X1[:], Y2[:], ALU.mult)
            kb.act(o["PCc"][:, :], cs3[:, :, 63], AF.Exp, scale=-C0)
            yield
            kb.act(X7[:], CS[:], AF.Exp, scale=-C0)
            kb.tt("dve", o["Rt"][:], X3[:], X7[:], ALU.mult)
            kb.tt("dve", Y3[:], X3[:], X1[:], ALU.mult)
            kb.act(sqb2[:], Y3[:], AF.Copy, scale=PC["r_k"](hp))
            kb.copy("act", o["vb"][:], X5[:])
            ps = bk1()
            kb.mm(ps[:, :], blk_b[:], sqb2[:])
            kb.tt("dve", o["bonus"][:], ps[:, :], X5[:], ALU.mult)
            yield

        def S2(u):
            o = s1o[u % 3]
            q = s2o[u % 2]
            for (src, dstt) in ((o["BP"], q["BPt"]), (o["KP"], q["KPt"]), (o["vb"], q["Vt"]), (o["At"], q["Att"])):
                for cc in cc8():
                    for h in range(2):
                        sl = HS[h]
                        kb.tr(PT[sl, cc * 64:(cc + 1) * 64], src[sl, cc * 64:(cc + 1) * 64], ident_b[sl, sl])
                kb.copy("act", dstt[:].rearrange("p c f -> p (c f)"), PT[:, 0:512])
                yield
            combos = ((o["Bt"], o["At"], m_su, None), (o["At"], o["Bt"], m_sl, None), (o["Kt"], o["At"], m_su, q["AK"]),
                      (o["Bt"], o["Rt"], m_iu, q["RB"]), (o["Kt"], o["Rt"], m_iu, q["RK"]))
            for ci_, (lh, rh, msk, dst_) in enumerate(combos):
                ps = bk2()
                for cc in cc8():
                    cs_ = slice(cc * 64, (cc + 1) * 64)
                    for h in range(2):
                        sl = HS[h]
                        kb.mm(ps[sl, cs_], lh[sl, cs_], rh[sl, cs_])
                if ci_ == 0:
                    o_ = Ee[0][:]
                elif ci_ == 1:
                    o_ = Ff[0][:]
                else:
                    o_ = dst_[:].rearrange("p c f -> p (c f)")
                kb.tt("dve", o_, ps[:, :], msk[:].rearrange("p c f -> p (c f)"), ALU.mult)
                yield
            kb.tt("dve", Zz[0][:], Ee[0][:], m_id[:].rearrange("p c f -> p (c f)"), ALU.add)
            for lvl in range(1, 6):
                ep, fp, zp = Ee[(lvl - 1) % 2], Ff[(lvl - 1) % 2], Zz[(lvl - 1) % 2]
                en, fn_, zn = Ee[lvl % 2], Ff[lvl % 2], Zz[lvl % 2]
                ps = bk2()
                for cc in cc8():
                    c6 = slice(cc * 64, (cc + 1) * 64)
                    for h in range(2):
                        sl = HS[h]
                        kb.mm(ps[sl, c6], ep[sl, c6], fp[sl, c6])
                kb.copy("act", fn_[:], ps[:, :])
                yield
                if lvl < 5:
                    ps = bk2()
                    for cc in cc8():
                        c6 = slice(cc * 64, (cc + 1) * 64)
                        for h in range(2):
                            sl = HS[h]
                            kb.mm(ps[sl, c6], fp[sl, c6], ep[sl, c6])
                    kb.copy("act", en[:], ps[:, :])
                    yield
                ps = bk2()
                for cc in cc8():
                    c6 = slice(cc * 64, (cc + 1) * 64)
                    for h in range(2):
                        sl = HS[h]
                        kb.mm(ps[sl, c6], fn_[sl, c6], zp[sl, c6])
                zo = zn[:] if lvl < 5 else q["Z"][:].rearrange("p c f -> p (c f)")
                kb.tt("dve", zo, zp[:], ps[:, :], ALU.add)
                yield
            Z2 = q["Z"][:].rearrange("p c f -> p (c f)")
            ps = bk2()
            for cc in cc8():
                c6 = slice(cc * 64, (cc + 1) * 64)
                for h in range(2):
                    sl = HS[h]
                    kb.mm(ps[sl, c6], Z2[sl, c6], q["Att"][sl, cc, :])
            kb.copy("act", q["TAk"][:].rearrange("p c f -> p (c f)"), ps[:, :])
            yield
            ps = bk2()
            for cc in cc8():
                c6 = slice(cc * 64, (cc + 1) * 64)
                for h in range(2):
                    sl = HS[h]
                    kb.mm(ps[sl, c6], q["AK"][sl, cc, :], q["Vt"][sl, cc, :])
            kb.copy("act", q["AKV"][:].rearrange("p c f -> p (c f)"), ps[:, :])
            yield
            ps = bk2()
            for cc in cc8():
                c6 = slice(cc * 64, (cc + 1) * 64)
                for h in range(2):
                    sl = HS[h]
                    kb.mm(ps[sl, c6], Z2[sl, c6], q["AKV"][sl, cc, :])
            kb.copy("act", q["U0"][:].rearrange("p c f -> p (c f)"), ps[:, :])
            yield
            ps = bk2()
            for cc in cc8():
                c6 = slice(cc * 64, (cc + 1) * 64)
                for h in range(2):
                    sl = HS[h]
                    kb.mm(ps[sl, c6], q["TAk"][sl, cc, :], q["BPt"][sl, cc, :])
            kb.copy("act", q["GT"][:].rearrange("p c f -> p (c f)"), ps[:, :])
            yield
            ps = bk2()
            for cc in cc8():
                c6 = slice(cc * 64, (cc + 1) * 64)
                for h in range(2):
                    sl = HS[h]
                    kb.mm(ps[sl, c6], q["BPt"][sl, cc, :], q["U0"][sl, cc, :], start=True, stop=False)
                    kb.mm(ps[sl, c6], q["KPt"][sl, cc, :], q["Vt"][sl, cc, :], start=False, stop=True)
            kb.copy("act", q["H0"][:].rearrange("p c f -> p (c f)"), ps[:, :])
            yield
            ps = bk2()
            for cc in cc8():
                c6 = slice(cc * 64, (cc + 1) * 64)
                for h in range(2):
                    sl = HS[h]
                    kb.mm(ps[sl, c6], q["TAk"][sl, cc, :], q["RB"][sl, cc, :])
            kb.tt("dve", q["R2T"][:].rearrange("p c f -> p (c f)"), ps[:, :], o["Rt"][:], ALU.add)
            yield

        cglob = [0]

        def S3(u):
            hp, blk = divmod(u, 4)
            o = s1o[u % 3]
            q = s2o[u % 2]
            hc = slice(hp * 128, (hp + 1) * 128)
            ts_ = slice(blk * BL, (blk + 1) * BL)
            PY = PHs[u % 2]
            if blk == 0:
                kb.memset("pool", S32[:], 0.0)
                kb.memset("pool", Sb[cglob[0] % 2][:], 0.0)
            for cc in range(8):
                c6 = slice(cc * 64, (cc + 1) * 64)
                g = cglob[0]
                cglob[0] += 1
                sb_old = Sb[g % 2]
                sb_new = Sb[(g + 1) % 2]
                pcol_ = 0
                pS = bS3[:, pcol_:pcol_ + 64]
                with Collect():
                    for h in range(2):
                        sl = HS[h]
                        kb.mm(bS3[sl, pcol_:pcol_ + 64], ident_b[sl, sl], q["H0"][sl, cc, :], start=True, stop=False)
                    for h in range(2):
                        sl = HS[h]
                        kb.mm(bS3[sl, pcol_:pcol_ + 64], q["GT"][sl, cc, :], sb_old[sl, :], start=False, stop=True)
                with Collect():
                    for st_, (A_, B_) in enumerate(((q["RB"], q["U0"]), (q["RK"], q["Vt"]), (q["R2T"], None))):
                        for h in range(2):
                            sl = HS[h]
                            rhs_ = sb_old[sl, :] if B_ is None else B_[sl, cc, :]
                            kb.mm(PY[sl, c6], A_[sl, cc, :], rhs_, start=(st_ == 0), stop=(st_ == 2))
                kb.stt(sb_new[:], S32[:], o["PCc"][:, cc:cc + 1], pS, ALU.mult, ALU.add)
                kb.stt(S32[:], S32[:], o["PCc"][:, cc:cc + 1], pS, ALU.mult, ALU.add)
                yield
            kb.copy("act", ytm[:], PY[:, :])
            for cc in range(8):
                c6 = slice(cc * 64, (cc + 1) * 64)
                kb.stt(sqf[:, c6], ytm[:, c6], 1.0, ytm[:, c6], ALU.mult, ALU.mult, accum_out=gst[:, 8 + cc:9 + cc])
                kb.stt(sqf[:, c6], ytm[:, c6], 1.0, zf[:, :], ALU.mult, ALU.add, accum_out=gst[:, cc:cc + 1])
            yield
            kb.ts("dve", gst[:, 16:24], gst[:, 0:8], 1.0 / 64.0, ALU.mult)
            kb.tt("dve", gst[:, 24:32], gst[:, 16:24], gst[:, 16:24], ALU.mult)
            kb.stt(gst[:, 32:40], gst[:, 8:16], 1.0 / 64.0, gst[:, 24:32], ALU.mult, ALU.subtract)
            kb.act(gst[:, 40:48], gst[:, 32:40], AF.Ln, bias=cG)
            kb.act(gst[:, 40:48], gst[:, 40:48], AF.Exp, scale=-0.5)
            y3 = ytm[:].rearrange("p (c f) -> p c f", f=64)
            kb.tt("dve", y3, y3, gst[:, 16:24].unsqueeze(2).to_broadcast([128, 8, 64]), ALU.subtract)
            kb.tt("dve", ynb[:].rearrange("p (c f) -> p c f", f=64), y3, gst[:, 40:48].unsqueeze(2).to_broadcast([128, 8, 64]), ALU.mult)
            yield
            for cc in cc8():
                for h in range(2):
                    sl = HS[h]
                    kb.tr(PT[sl, 512 + cc * 64:512 + (cc + 1) * 64], ynb[sl, cc * 64:(cc + 1) * 64], ident_b[sl, sl])
            kb.act(ygf[:], PT[:, 512:1024], AF.Identity, scale=PC["ln_x_w"](hp), bias=PC["ln_x_b"](hp))
            kb.tt("pool", ygf[:], ygf[:], o["bonus"][:], ALU.add)
            pg = bk2()
            kb.mm(pg[:, :], g2b1[:, hc], sg1b[:, ts_], start=True, stop=False)
            kb.mm(pg[:, :], g2b2[0:32, hc], sg2b[0:32, ts_], start=False, stop=True)
            kb.tt("dve", mixT[:, 4 + hp, ts_], ygf[:], pg[:, :], ALU.mult)
            yield

        def drain(gen, n=None):
            k = 0
            if gen is None:
                return False
            while n is None or k < n:
                try:
                    next(gen)
                except StopIteration:
                    return False
                k += 1
            return True

        NU = 16
        g1 = {u: S1(u) for u in range(NU)}
        g2 = {u: S2(u) for u in range(NU)}
        drain(g1[0]); drain(g2[0]); drain(g1[1])
        for u in range(NU):
            g3 = S3(u)
            a2 = g2.get(u + 1)
            a1 = g1.get(u + 2)
            alive3 = True
            alive2 = a2 is not None
            alive1 = a1 is not None
            while alive3 or alive2 or alive1:
                if alive3:
                    alive3 = drain(g3, 1)
                if alive2:
                    alive2 = drain(a2, 3)
                if alive1:
                    alive1 = drain(a1, 1)
        dbg("yr", mixT[:, 4, :], [128, T])

    def tail_phase():
        ar = Arena(AR)
        H = ar.t("H", [128, 16, D])
        xnb2 = [ar.t("xnb2_%d" % i, [128, D], BF16) for i in range(2)]
        actT = ar.t("actT", [128, 4, T], BF16)
        rls = [ar.t("rl%d" % i, [128, 512]) for i in range(2)]
        wo_b = ar.t("wo_b", [128, 8, D], BF16)
        wup_l = [kb.sb("wup_b0", [128, 8, 512], BF16, at=kb.off_of["wo_b"]), ar.t("wup_b1", [128, 8, 512], BF16)]
        wdn_l = [kb.sb("wdn_b0", [128, 4, D], BF16, at=kb.off_of["wo_b"] + 8192), ar.t("wdn_b1", [128, 4, D], BF16)]
        kb.pe_warm = True
        load_w(wo_b, I["w_out"], 8, D)
        xst_l = [xst[0], xst[1], kb.sb("xst2", [128, D], F32, at=AR + 80 * 1024), kb.sb("xst3", [128, D], F32, at=AR + 84 * 1024)]
        for tt in range(16):
            xb_ = xst_l[tt % 4]
            kb.dma(xb_[:], I["x"][tt * 128:(tt + 1) * 128, :])
            for dh in range(2):
                ps = bank()
                with Collect():
                    for c in range(8):
                        kb.mm(ps[:, :], mixT[:, c, tt * 128:(tt + 1) * 128], wo_b[:, c, dh * 512:(dh + 1) * 512], start=(c == 0), stop=(c == 7))
                hs = H[:, tt, dh * 512:(dh + 1) * 512]
                kb.tt("dve", hs, ps[:, :], xb_[:, dh * 512:(dh + 1) * 512], ALU.add)
        dbg("h1", H[:, 0, :], [128, D])
        if 'hn' not in skip:
            norm_T(lambda tt: H[:, tt, :], gffn_T, xnT, 16, actT[:, 0, 0:D], xnb2)
        for g in range(0 if 'ffn' in skip else 8):
            wup_b = wup_l[(g + 1) % 2]
            wdn_b = wdn_l[(g + 1) % 2]
            load_w(wup_b, I["w_up"][:, g * 512:(g + 1) * 512], 8, 512)
            load_w(wdn_b, I["w_down"][g * 512:(g + 1) * 512, :], 4, D)
            for fc in range(4):
                for n in range(4):
                    ns = slice(n * 512, (n + 1) * 512)
                    ps = bank()
                    with Collect():
                        for c in range(8):
                            kb.mm(ps[:, :], wup_b[:, c, fc * 128:(fc + 1) * 128], xnT[:, c, ns], start=(c == 0), stop=(c == 7))
                    rl_ = rls[(fc * 4 + n) % 2]
                    kb.act(rl_[:], ps[:, :], AF.Relu)
                    kb.act(actT[:, fc, ns], rl_[:], AF.Square)
            for tt in range(16):
                for dh in range(2):
                    ps = bank()
                    with Collect():
                        for fc in range(4):
                            kb.mm(ps[:, :], actT[:, fc, tt * 128:(tt + 1) * 128], wdn_b[:, fc, dh * 512:(dh + 1) * 512], start=(fc == 0), stop=(fc == 3))
                    hs = H[:, tt, dh * 512:(dh + 1) * 512]
                    kb.tt("dve", hs, hs, ps[:, :], ALU.add)
        for tt in range(16):
            ss = stat[:, 32 + tt:33 + tt]
            kb.stt(actT[:, tt % 4, 0:D], H[:, tt, :], 1.0, H[:, tt, :], ALU.mult, ALU.mult, accum_out=ss)
            kb.act(ss, ss, AF.Sqrt, bias=cE, scale=1.0 / D)
            kb.recip(ss, ss)
            jb_ = xst[tt % 2]
            kb.act(jb_[:], H[:, tt, :], AF.Copy, scale=ss)
            kb.tt("dve", H[:, tt, :], jb_[:], gF[:], ALU.mult)
            kb.dma(out_d[tt * 128:(tt + 1) * 128, :], H[:, tt, :], final=True)

    if "s5" not in skip:
        kb.phase = 'S5'
        s5_phase()
    else:
        kb.memset("pool", mixT[:, 0:4, :], 0.0)
    if "rwkv" not in skip:
        kb.phase = 'RWKV'
        rwkv_phase()
    else:
        kb.memset("pool", mixT[:, 4:8, :], 0.0)
    if 'tail' not in skip:
        kb.phase = 'TAIL'
        tail_phase()
    if 'nosched' not in skip:
        est = kb.schedule()
    kb.emit()
    return nc, dbg_out


_CACHE = {}


def _prep_inputs(inputs):
    sq = {}
    for k, shp in IN_SHAPES.items():
        if k == "x":
            continue
        a = np.asarray(inputs[k], dtype=np.float32)
        sq[k] = np.ascontiguousarray(a.reshape(shp))
    return sq


def kernel(**inputs):
    x = np.asarray(inputs["x"], dtype=np.float32)
    if "nc" not in _CACHE:
        _CACHE["nc"] = build()[0]
    nc = _CACHE["nc"]
    shared = _prep_inputs(inputs)
    in_maps = []
    for b in range(8):
        m = dict(shared)
        m["x"] = np.ascontiguousarray(x[b])
        in_maps.append(m)
    res = run_bass_kernel_spmd(nc, in_maps, core_ids=list(range(8)))
    out = np.stack([np.asarray(r["out"], dtype=np.float32) for r in res.results], axis=0)
    return out.astype(inputs["x"].dtype if hasattr(inputs["x"], "dtype") else np.float32)
```
